# Optimizing a Trainium2 kernel written in Bass

```python
import jax
import jax.numpy as jnp
from jax import lax
import numpy as np

D_MODEL = 1024
BATCH = 8
SEQ = 4096
DEPTH = 2

CTX_LEN = 256
GRID_W = 64
HEAD_DIM = 64
MIX_DIM = D_MODEL
NORM_EPS = 1e-6
RWKV_HEADS = D_MODEL // 4 // HEAD_DIM
RWKV_DIM = RWKV_HEADS * HEAD_DIM
RWKV_W_LORA = 64
RWKV_A_LORA = 64
RWKV_G_LORA = 128
RWKV_COLS = 3 * RWKV_DIM + 2 * RWKV_W_LORA + 2 * RWKV_A_LORA + RWKV_G_LORA
RWKV_GN_EPS = 64e-5
MLA_HEADS = D_MODEL // 2 // HEAD_DIM
MLA_NOPE = 64
MLA_ROPE = 32
MLA_V = HEAD_DIM
MLA_QK = MLA_NOPE + MLA_ROPE
MLA_Q_RANK = D_MODEL // 2
MLA_KV_RANK = D_MODEL // 4
MLA_DIM = MLA_HEADS * MLA_V
MLA_COLS = MLA_Q_RANK + MLA_KV_RANK + MLA_ROPE
Q_BLOCK = 128
ROPE_BASE = 10000.0
MLSTM_HEADS = D_MODEL // 4 // HEAD_DIM
MLSTM_QK = HEAD_DIM // 2
MLSTM_V = HEAD_DIM
MLSTM_DIM = MLSTM_HEADS * MLSTM_V
MLSTM_CHUNK = 64
GATE_SOFTCAP = 15.0
MLSTM_COLS = 2 * MLSTM_HEADS * MLSTM_QK + 2 * MLSTM_DIM + 4 * MLSTM_HEADS
IN_COLS = RWKV_COLS + MLA_COLS + MLSTM_COLS
FFN_HIDDEN = -(-8 * D_MODEL // (3 * 256)) * 256

kernel_name = 'hybrid_rwkv7_mla_mlstm_dit_trunk'


def rmsnorm(x, g, eps=NORM_EPS):
    xf = x.astype(jnp.float32)
    y = xf * lax.rsqrt(jnp.mean(xf * xf, axis=-1, keepdims=True) + eps)
    return (y * g.astype(jnp.float32)).astype(x.dtype)


def modulate(h, shift, scale):
    return h * (1 + scale) + shift


def neighbours(z):
    zp = jnp.pad(z, ((0, 0), (1, 1), (0, 0)))
    return zp[:, :-2], zp[:, 2:]


def soft_cap(x):
    return GATE_SOFTCAP * jnp.tanh(x / GATE_SOFTCAP)


def swiglu(h, w_in, w_out):
    gate, up = jnp.split(h @ w_in, 2, axis=-1)
    return (jax.nn.silu(gate) * up) @ w_out


def axial_rope_tables(seq_len):
    rows = seq_len // GRID_W
    row = jnp.repeat(jnp.arange(rows, dtype=jnp.float32), GRID_W)
    col = jnp.tile(jnp.arange(GRID_W, dtype=jnp.float32), rows)
    n_freq = MLA_ROPE // 4
    inv = jnp.power(ROPE_BASE, -jnp.arange(n_freq, dtype=jnp.float32) / n_freq)
    ang = jnp.concatenate([row[:, None] * inv, col[:, None] * inv], axis=-1)
    return jnp.cos(ang), jnp.sin(ang)


def apply_rope(x, cos, sin):
    half = x.shape[-1] // 2
    xf = x.astype(jnp.float32)
    x1, x2 = xf[..., :half], xf[..., half:]
    c, s = cos[:, None, :], sin[:, None, :]
    return jnp.concatenate([x1 * c - x2 * s, x1 * s + x2 * c], axis=-1).astype(x.dtype)


def rwkv7_scan(r, w, k, v, kk, a, state, reverse):
    def step(S, xs):
        r_t, w_t, k_t, v_t, kk_t, a_t = xs
        s_kk = jnp.einsum('bhvk,bhk->bhv', S, kk_t)
        S = (S * w_t[:, :, None, :] - s_kk[..., None] * (kk_t * a_t)[:, :, None, :]
             + v_t[..., None] * k_t[:, :, None, :])
        return S, jnp.einsum('bhvk,bhk->bhv', S, r_t)
    xs = tuple(jnp.moveaxis(t, 1, 0) for t in (r, w, k, v, kk, a))
    state, y = lax.scan(step, state, xs, reverse=reverse)
    return jnp.moveaxis(y, 0, 1), state


def rwkv7_mixer(zl, zc, mu, w0, w2, a0, a2, g2, k_k, k_a, r_k, ln_g, ln_b, ctx_out):
    H, N = RWKV_HEADS, HEAD_DIM
    split_at = [RWKV_DIM, 2 * RWKV_DIM, 3 * RWKV_DIM,
                3 * RWKV_DIM + 2 * RWKV_W_LORA,
                3 * RWKV_DIM + 2 * RWKV_W_LORA + 2 * RWKV_A_LORA]

    def prep(z):
        B, T, _ = z.shape
        prev, nxt = neighbours(z)
        z = z + mu * (0.5 * (prev + nxt) - z)
        r, k, v, wd, ad, gd = jnp.split(z, split_at, axis=-1)
        heads = lambda t: t.reshape(B, T, H, N).astype(jnp.float32)
        kk = heads(k * k_k)
        kk = kk / jnp.maximum(jnp.sqrt(jnp.sum(kk * kk, axis=-1, keepdims=True)), 1e-12)
        per_dir = []
        for d in range(2):
            w_lo = jnp.tanh(wd[..., d * RWKV_W_LORA:(d + 1) * RWKV_W_LORA]) @ w2[d]
            log_w = -jax.nn.softplus(-(w0[d] + w_lo)) - 0.5
            decay = jnp.exp(-jnp.exp(log_w.astype(jnp.float32)))
            a = jax.nn.sigmoid(a0[d] + ad[..., d * RWKV_A_LORA:(d + 1) * RWKV_A_LORA] @ a2[d])
            k_d = k * (1 + (a - 1) * k_a)
            per_dir.append((heads(decay), heads(k_d), heads(a)))
        return heads(r), heads(k), heads(v), kk, per_dir, gd

    def post(y, r, k, v, gd):
        B, T = y.shape[:2]
        mean = jnp.mean(y, axis=-1, keepdims=True)
        var = jnp.mean(jnp.square(y - mean), axis=-1, keepdims=True)
        yn = ((y - mean) * lax.rsqrt(var + RWKV_GN_EPS)).reshape(B, T, H * N) * ln_g + ln_b
        bonus = jnp.sum(r * k * r_k, axis=-1, keepdims=True) * v
        g = jax.nn.sigmoid(gd) @ g2
        return ((yn + bonus.reshape(B, T, H * N)) * g).astype(zl.dtype)

    rl, kl, vl, kkl, dl, gdl = prep(zl)
    rc, kc, vc, kkc, dc, gdc = prep(zc)
    S0 = jnp.zeros((zl.shape[0], H, N, N), jnp.float32)
    y_lat, y_ctx = [], []
    for d, rev in ((0, False), (1, True)):
        yc_d, S_ctx = rwkv7_scan(rc, dc[d][0], dc[d][1], vc, kkc, dc[d][2], S0, rev)
        yl_d, _ = rwkv7_scan(rl, dl[d][0], dl[d][1], vl, kkl, dl[d][2], S_ctx, rev)
        y_lat.append(yl_d)
        y_ctx.append(yc_d)
    out_l = post(y_lat[0] + y_lat[1], rl, kl, vl, gdl)
    out_c = post(y_ctx[0] + y_ctx[1], rc, kc, vc, gdc) if ctx_out else None
    return out_l, out_c


def attend(q, keys, vals):
    s = jnp.einsum('bhqd,bhkd->bhqk', q, keys).astype(jnp.float32) * (MLA_QK ** -0.5)
    p = jax.nn.softmax(s, axis=-1)
    return jnp.einsum('bhqk,bhkd->bhqd', p.astype(vals.dtype), vals)


def mla_mixer(zl, zc, q_norm_g, w_uq, kv_norm_g, w_ukv, q_qk_g, k_qk_g, cos, sin, ctx_out):
    H = MLA_HEADS

    def project(z, rope):
        B, T, _ = z.shape
        cq, ckv, kr = jnp.split(z, [MLA_Q_RANK, MLA_Q_RANK + MLA_KV_RANK], axis=-1)
        q = (rmsnorm(cq, q_norm_g) @ w_uq).reshape(B, T, H, MLA_QK)
        kv = (rmsnorm(ckv, kv_norm_g) @ w_ukv).reshape(B, T, H, MLA_NOPE + MLA_V)
        k_nope, v = jnp.split(kv, [MLA_NOPE], axis=-1)
        k = jnp.concatenate([k_nope, jnp.broadcast_to(kr[:, :, None, :], (B, T, H, MLA_ROPE))], axis=-1)
        q = rmsnorm(q, q_qk_g)
        k = rmsnorm(k, k_qk_g)
        if rope:
            q = jnp.concatenate([q[..., :MLA_NOPE], apply_rope(q[..., MLA_NOPE:], cos, sin)], axis=-1)
            k = jnp.concatenate([k[..., :MLA_NOPE], apply_rope(k[..., MLA_NOPE:], cos, sin)], axis=-1)
        return q.transpose(0, 2, 1, 3), k.transpose(0, 2, 1, 3), v.transpose(0, 2, 1, 3)

    ql, kl, vl = project(zl, True)
    qc, kc, vc = project(zc, False)
    B, _, S, _ = ql.shape
    keys = jnp.concatenate([kl, kc], axis=2)
    vals = jnp.concatenate([vl, vc], axis=2)
    n_blk = S // Q_BLOCK
    qb = ql.reshape(B, H, n_blk, Q_BLOCK, MLA_QK).transpose(2, 0, 1, 3, 4)
    out = lax.map(lambda q_blk: attend(q_blk, keys, vals), qb)
    out_l = out.transpose(1, 0, 3, 2, 4).reshape(B, S, H * MLA_V)
    out_c = None
    if ctx_out:
        oc = attend(qc, kc, vc)
        out_c = oc.transpose(0, 2, 1, 3).reshape(B, zc.shape[1], H * MLA_V)
    return out_l, out_c


def mlstm_chunkwise(q, k, v, ig, fg, state):
    f32 = jnp.float32
    B, H, T, DK = q.shape
    DV = v.shape[-1]
    L = MLSTM_CHUNK
    NC = T // L
    q = q.astype(f32).reshape(B, H, NC, L, DK) * (DK ** -0.5)
    k = k.astype(f32).reshape(B, H, NC, L, DK)
    v = v.astype(f32).reshape(B, H, NC, L, DV)
    log_i = soft_cap(ig.astype(f32)).reshape(B, H, NC, L)
    log_f = jax.nn.log_sigmoid(soft_cap(fg.astype(f32))).reshape(B, H, NC, L)
    b = jnp.cumsum(log_f, axis=-1)
    g = b[..., -1]
    w_end = g[..., None] - b + log_i
    m_loc = jnp.max(w_end, axis=-1)
    e_end = jnp.exp(w_end - m_loc[..., None])
    C_loc = jnp.einsum('bhcl,bhclv,bhclk->bhcvk', e_end, v, k)
    n_loc = jnp.einsum('bhcl,bhclk->bhck', e_end, k)

    def step(carry, xs):
        C, n, m = carry
        g_c, m_l, C_l, n_l = xs
        m_new = jnp.maximum(g_c + m, m_l)
        s_old = jnp.exp(g_c + m - m_new)
        s_loc = jnp.exp(m_l - m_new)
        new = (s_old[..., None, None] * C + s_loc[..., None, None] * C_l,
               s_old[..., None] * n + s_loc[..., None] * n_l, m_new)
        return new, (C, n, m)

    xs = tuple(jnp.moveaxis(t, 2, 0) for t in (g, m_loc, C_loc, n_loc))
    final, starts = lax.scan(step, state, xs)
    C_s, n_s, m_s = (jnp.moveaxis(t, 0, 2) for t in starts)
    log_inter = b + m_s[..., None]
    causal = jnp.tril(jnp.ones((L, L), dtype=bool))
    log_d = jnp.where(causal, b[..., :, None] - b[..., None, :] + log_i[..., None, :], -jnp.inf)
    m_out = jnp.maximum(log_inter, jnp.max(log_d, axis=-1))
    w_intra = jnp.einsum('bhcsk,bhcjk->bhcsj', q, k) * jnp.exp(log_d - m_out[..., None])
    s_inter = jnp.exp(log_inter - m_out)
    num = (jnp.einsum('bhcsj,bhcjv->bhcsv', w_intra, v)
           + s_inter[..., None] * jnp.einsum('bhcvk,bhcsk->bhcsv', C_s, q))
    den = jnp.sum(w_intra, axis=-1) + s_inter * jnp.einsum('bhck,bhcsk->bhcs', n_s, q)
    h = num / jnp.maximum(jnp.abs(den), jnp.exp(-m_out))[..., None]
    return h.reshape(B, H, T, DV), final


def mlstm_mixer(zl, zc, conv_w, conv_b, i_b, f_b, norm_g, ctx_out):
    H = MLSTM_HEADS
    qk_w = 2 * H * MLSTM_QK

    def prep(z):
        B, T, _ = z.shape
        qk, v, o, gates = jnp.split(z, [qk_w, qk_w + MLSTM_DIM, qk_w + 2 * MLSTM_DIM], axis=-1)
        prev, nxt = neighbours(qk)
        qk = jax.nn.silu(prev * conv_w[0] + qk * conv_w[1] + nxt * conv_w[2] + conv_b)
        q, k = jnp.split(qk, 2, axis=-1)
        to_heads = lambda t, d: t.reshape(B, T, H, d).transpose(0, 2, 1, 3)
        gates = gates.reshape(B, T, 2, 2, H).transpose(2, 3, 0, 4, 1)
        ig = gates[:, 0] + i_b[:, None, :, None]
        fg = gates[:, 1] + f_b[:, None, :, None]
        return to_heads(q, MLSTM_QK), to_heads(k, MLSTM_QK), to_heads(v, MLSTM_V), ig, fg, o

    ql, kl, vl, igl, fgl, ol = prep(zl)
    qc, kc, vc, igc, fgc, oc = prep(zc)
    B = zl.shape[0]
    state0 = (jnp.zeros((B, H, MLSTM_V, MLSTM_QK), jnp.float32),
              jnp.zeros((B, H, MLSTM_QK), jnp.float32),
              jnp.zeros((B, H), jnp.float32))
    h_lat, h_ctx = [], []
    for d in range(2):
        flip = (lambda t: jnp.flip(t, axis=2)) if d == 1 else (lambda t: t)
        hc_d, st = mlstm_chunkwise(flip(qc), flip(kc), flip(vc), flip(igc[d]), flip(fgc[d]), state0)
        hl_d, _ = mlstm_chunkwise(flip(ql), flip(kl), flip(vl), flip(igl[d]), flip(fgl[d]), st)
        h_lat.append(flip(hl_d))
        h_ctx.append(flip(hc_d))

    def post(h, o):
        h = h.transpose(0, 2, 1, 3)
        hn = h * lax.rsqrt(jnp.mean(h * h, axis=-1, keepdims=True) + NORM_EPS)
        Bq, T = h.shape[:2]
        return (hn.reshape(Bq, T, MLSTM_DIM) * norm_g * jax.nn.sigmoid(o)).astype(zl.dtype)

    out_l = post(h_lat[0] + h_lat[1], ol)
    out_c = post(h_ctx[0] + h_ctx[1], oc) if ctx_out else None
    return out_l, out_c


def setup_inputs(seed: int = 0) -> dict:
    key = jax.random.key(seed)
    ks = iter(jax.random.split(key, 48))
    nrm = lambda shape, scale: jax.random.normal(next(ks), shape, jnp.float32) * scale
    L, D = DEPTH, D_MODEL
    inp = {}
    inp['x'] = nrm((BATCH, SEQ, D), 1.0)
    inp['c'] = nrm((BATCH, D), 1.0)
    inp['ctx'] = nrm((BATCH, CTX_LEN, D), 1.0)
    inp['c_ctx'] = nrm((D,), 1.0)
    inp['mod_w'] = nrm((L, D, 6 * D), 0.5 * D ** -0.5)
    inp['mod_b'] = nrm((L, 6 * D), 0.01)
    inp['norm1_g'] = 1.0 + nrm((L, D), 0.02)
    inp['norm2_g'] = 1.0 + nrm((L, D), 0.02)
    inp['w_in'] = nrm((L, D, IN_COLS), D ** -0.5)
    inp['w_out'] = nrm((L, MIX_DIM, D), 0.5 * MIX_DIM ** -0.5)
    inp['ffn_w_in'] = nrm((L, D, 2 * FFN_HIDDEN), D ** -0.5)
    inp['ffn_w_out'] = nrm((L, FFN_HIDDEN, D), 0.5 * FFN_HIDDEN ** -0.5)
    inp['rwkv_mu'] = jax.random.uniform(next(ks), (L, RWKV_COLS), jnp.float32)
    inp['rwkv_w0'] = nrm((L, 2, RWKV_DIM), 0.5)
    inp['rwkv_w2'] = nrm((L, 2, RWKV_W_LORA, RWKV_DIM), 0.1)
    inp['rwkv_a0'] = nrm((L, 2, RWKV_DIM), 0.5)
    inp['rwkv_a2'] = nrm((L, 2, RWKV_A_LORA, RWKV_DIM), 0.1)
    inp['rwkv_g2'] = nrm((L, RWKV_G_LORA, RWKV_DIM), RWKV_G_LORA ** -0.5)
    inp['rwkv_k_k'] = 0.85 + nrm((L, RWKV_DIM), 0.05)
    inp['rwkv_k_a'] = 1.0 + nrm((L, RWKV_DIM), 0.05)
    inp['rwkv_r_k'] = nrm((L, RWKV_HEADS, HEAD_DIM), 0.1)
    inp['rwkv_ln_g'] = 1.0 + nrm((L, RWKV_DIM), 0.02)
    inp['rwkv_ln_b'] = nrm((L, RWKV_DIM), 0.01)
    inp['mla_q_norm_g'] = 1.0 + nrm((L, MLA_Q_RANK), 0.02)
    inp['mla_w_uq'] = nrm((L, MLA_Q_RANK, MLA_HEADS * MLA_QK), MLA_Q_RANK ** -0.5)
    inp['mla_kv_norm_g'] = 1.0 + nrm((L, MLA_KV_RANK), 0.02)
    inp['mla_w_ukv'] = nrm((L, MLA_KV_RANK, MLA_HEADS * (MLA_NOPE + MLA_V)), MLA_KV_RANK ** -0.5)
    inp['mla_q_qknorm_g'] = 1.0 + nrm((L, MLA_QK), 0.02)
    inp['mla_k_qknorm_g'] = 1.0 + nrm((L, MLA_QK), 0.02)
    inp['mlstm_conv_w'] = nrm((L, 3, 2 * MLSTM_HEADS * MLSTM_QK), 3 ** -0.5)
    inp['mlstm_conv_b'] = nrm((L, 2 * MLSTM_HEADS * MLSTM_QK), 0.01)
    inp['mlstm_i_b'] = nrm((L, 2, MLSTM_HEADS), 0.1)
    inp['mlstm_f_b'] = jnp.linspace(3.0, 6.0, MLSTM_HEADS, dtype=jnp.float32) + nrm((L, 2, MLSTM_HEADS), 0.1)
    inp['mlstm_norm_g'] = 1.0 + nrm((L, MLSTM_DIM), 0.02)
    return inp


def reference(x, c, ctx, c_ctx, mod_w, mod_b, norm1_g, norm2_g, w_in, w_out, ffn_w_in, ffn_w_out,
              rwkv_mu, rwkv_w0, rwkv_w2, rwkv_a0, rwkv_a2, rwkv_g2, rwkv_k_k, rwkv_k_a, rwkv_r_k,
              rwkv_ln_g, rwkv_ln_b, mla_q_norm_g, mla_w_uq, mla_kv_norm_g, mla_w_ukv,
              mla_q_qknorm_g, mla_k_qknorm_g, mlstm_conv_w, mlstm_conv_b, mlstm_i_b, mlstm_f_b,
              mlstm_norm_g):
    cos, sin = axial_rope_tables(x.shape[1])
    xl, xc = x, ctx
    silu_c = jax.nn.silu(c)
    silu_cc = jax.nn.silu(c_ctx)
    for i in range(DEPTH):
        ctx_out = i < DEPTH - 1
        mod_l = silu_c @ mod_w[i] + mod_b[i]
        mod_c = silu_cc @ mod_w[i] + mod_b[i]
        sh1_l, sc1_l, gt1_l, sh2_l, sc2_l, gt2_l = jnp.split(mod_l[:, None, :], 6, axis=-1)
        sh1_c, sc1_c, gt1_c, sh2_c, sc2_c, gt2_c = jnp.split(mod_c, 6, axis=-1)
        zl = modulate(rmsnorm(xl, norm1_g[i]), sh1_l, sc1_l) @ w_in[i]
        zc = modulate(rmsnorm(xc, norm1_g[i]), sh1_c, sc1_c) @ w_in[i]
        za_l, zb_l, zm_l = jnp.split(zl, [RWKV_COLS, RWKV_COLS + MLA_COLS], axis=-1)
        za_c, zb_c, zm_c = jnp.split(zc, [RWKV_COLS, RWKV_COLS + MLA_COLS], axis=-1)
        ya_l, ya_c = rwkv7_mixer(za_l, za_c, rwkv_mu[i], rwkv_w0[i], rwkv_w2[i], rwkv_a0[i], rwkv_a2[i],
                                 rwkv_g2[i], rwkv_k_k[i], rwkv_k_a[i], rwkv_r_k[i], rwkv_ln_g[i],
                                 rwkv_ln_b[i], ctx_out)
        yb_l, yb_c = mla_mixer(zb_l, zb_c, mla_q_norm_g[i], mla_w_uq[i], mla_kv_norm_g[i], mla_w_ukv[i],
                               mla_q_qknorm_g[i], mla_k_qknorm_g[i], cos, sin, ctx_out)
        ym_l, ym_c = mlstm_mixer(zm_l, zm_c, mlstm_conv_w[i], mlstm_conv_b[i], mlstm_i_b[i],
                                 mlstm_f_b[i], mlstm_norm_g[i], ctx_out)
        xl = xl + gt1_l * (jnp.concatenate([ya_l, yb_l, ym_l], axis=-1) @ w_out[i])
        xl = xl + gt2_l * swiglu(modulate(rmsnorm(xl, norm2_g[i]), sh2_l, sc2_l), ffn_w_in[i], ffn_w_out[i])
        if ctx_out:
            xc = xc + gt1_c * (jnp.concatenate([ya_c, yb_c, ym_c], axis=-1) @ w_out[i])
            xc = xc + gt2_c * swiglu(modulate(rmsnorm(xc, norm2_g[i]), sh2_c, sc2_c), ffn_w_in[i], ffn_w_out[i])
    return xl
```

```python
import contextlib
import numpy as np
import ml_dtypes
import concourse.bass as bass
import concourse.mybir as mybir
from concourse.bass_utils import run_bass_kernel_spmd

F32 = mybir.dt.float32
BF16 = mybir.dt.bfloat16
ALU = mybir.AluOpType
AF = mybir.ActivationFunctionType

D = 1024
TL = 4096
TC = 256
T = TL + TC
DEPTH = 2
IN_COLS = 2736
FFN_H = 2816
CH = [(i * 512, 512) for i in range(8)] + [(TL, TC)]
NSLOT = 24
EPS = 1e-6
SB_BASE = 16640
SB_TOP = 229376 - 2048


class Sched:
    ENGS = ("pe", "act", "dve", "pool", "sp")

    def __init__(self, nc):
        self.nc = nc
        self.ops = {e: [] for e in self.ENGS}
        self.count = {e: 0 for e in self.ENGS}
        self.known = {e: {} for e in self.ENGS}
        self.sems = {}
        self.wr = {}
        self.rd = {}
        self.slot_n = 0
        self.slot_sp = 0
        self.slot_pl = 0
        self.slot_cnt = [0] * NSLOT
        self.n_ops = 0
        self.limit = None
        self.log = None

    def _deps(self, reads, writes):
        d = {}
        for b in reads:
            w = self.wr.get(b)
            if w and w[1] > d.get(w[0], 0):
                d[w[0]] = w[1]
        for b in writes:
            w = self.wr.get(b)
            if w and w[1] > d.get(w[0], 0):
                d[w[0]] = w[1]
            for s, v in self.rd.get(b, {}).items():
                if v > d.get(s, 0):
                    d[s] = v
        return d

    def _mark(self, reads, writes, src, val):
        for b in reads:
            self.rd.setdefault(b, {})[src] = val
        for b in writes:
            self.wr[b] = (src, val)
            self.rd[b] = {}

    def _waits(self, eng, deps):
        ws = []
        kn = self.known[eng]
        for src, val in deps.items():
            if kn.get(src, 0) >= val:
                continue
            if src == eng == "pe":
                continue
            kn[src] = val
            ws.append((src, val))
        return ws

    def op(self, eng, fn, reads=(), writes=()):
        if self.limit is not None and self.n_ops >= self.limit:
            return
        if self.log is not None:
            import sys as _s
            self.log.append((self.n_ops, eng, _s._getframe(2).f_lineno))
        ws = self._waits(eng, self._deps(reads, writes))
        self.count[eng] += 1
        val = self.count[eng]
        self.ops[eng].append((ws, fn, (eng, 1)))
        self._mark(reads, writes, eng, val)
        self.n_ops += 1

    def dma(self, fn, reads=(), writes=(), q="sp"):
        if self.limit is not None and self.n_ops >= self.limit:
            return
        if self.log is not None:
            import sys as _s
            self.log.append((self.n_ops, "dma-" + q, _s._getframe(2).f_lineno))
        if q == "sp":
            slot = self.slot_sp % 16
            self.slot_sp += 1
        else:
            slot = 16 + self.slot_pl % (NSLOT - 16)
            self.slot_pl += 1
        src = "q%d" % slot
        prev = self.slot_cnt[slot]
        deps = self._deps(reads, writes)
        if prev:
            deps[src] = max(deps.get(src, 0), prev)
        ws = self._waits(q, deps)
        self.slot_cnt[slot] = prev + 1
        self.ops[q].append((ws, fn, (src, 16)))
        self._mark(reads, writes, src, prev + 1)
        self.n_ops += 1

    def barrier(self):
        tot = dict(self.count)
        for s in range(NSLOT):
            if self.slot_cnt[s]:
                tot["q%d" % s] = self.slot_cnt[s]
        for e in self.ENGS:
            ws = self._waits(e, {k: v for k, v in tot.items() if v > 0 and k != e})
            if ws:
                self.ops[e].append((ws, None, None))
        self.wr = {}
        self.rd = {}

    def emit(self):
        nc = self.nc
        EP = 12000
        with contextlib.ExitStack() as st:
            for e in self.ENGS:
                ne = max(1, (self.count[e] + EP - 1) // EP)
                self.sems[e] = [st.enter_context(nc.semaphore("s_%s%d" % (e, i))) for i in range(ne)]
            for s in range(NSLOT):
                self.sems["q%d" % s] = st.enter_context(nc.semaphore("s_q%d" % s))
            block = st.enter_context(nc.Block())
            sems = self.sems

            def run(engname):
                def body(e):
                    n_done = 0
                    for ws, fn, inc in self.ops[engname]:
                        for src, val in ws:
                            if src[0] == "q":
                                e.wait_ge(sems[src], val * 16)
                            else:
                                e.wait_ge(sems[src][(val - 1) // EP], (val - 1) % EP + 1)
                        if fn is not None:
                            if inc[0][0] == "q":
                                fn(e).then_inc(sems[inc[0]], 16)
                            else:
                                fn(e).then_inc(sems[engname][n_done // EP], 1)
                                n_done += 1
                return body
            block.tensor(run("pe"))
            block.scalar(run("act"))
            block.vector(run("dve"))
            block.gpsimd(run("pool"))
            block.sync(run("sp"))


def _nm(aps):
    out = []
    for a in aps:
        if a is None or isinstance(a, (int, float)):
            continue
        out.append(a.name)
    return out


class KB:
    def __init__(self, nc):
        self.nc = nc
        self.S = Sched(nc)
        self.ptr = SB_BASE
        self.uid = 0
        self.ps = [nc.alloc_psum_tensor("psb%d" % i, [128, 512], F32) for i in range(8)]
        self.ps_i = 0
        self.started = {}
        self.acc_i = 0

    def sb(self, shape, dt=F32, name="t"):
        per = 1
        for s in shape[1:]:
            per *= s
        nbytes = per * (4 if dt == F32 else 2)
        nbytes = (nbytes + 63) // 64 * 64
        self.uid += 1
        assert self.ptr + nbytes <= SB_TOP, ("SBUF overflow", name, self.ptr, nbytes)
        t = self.nc.alloc_sbuf_tensor_at("%s_%d" % (name, self.uid), list(shape), dt, offset=self.ptr)
        self.ptr += nbytes
        return t

    def mark(self):
        return self.ptr

    def release(self, mark):
        self.S.barrier()
        self.ptr = mark

    def bank(self):
        b = self.ps[self.ps_i % 6]
        self.ps_i += 1
        self.started[b[:, :].name] = set()
        return b

    def acc(self):
        b = self.ps[6 + self.acc_i % 2]
        self.acc_i += 1
        self.started[b[:, :].name] = set()
        return b

    def dma(self, out, in_, q="sp", cast=False):
        if cast:
            q = "pool"
        self.S.dma(lambda e: e.dma_start(out=out, in_=in_, allow_slow_non_contiguous=True), reads=_nm([in_]), writes=_nm([out]), q=q)

    def mm(self, out, lhsT, rhs, start=True, stop=True):
        p0 = out.base_partition()
        p1 = p0 + out.shape[0]
        quads = set(range(p0 // 32, (p1 - 1) // 32 + 1))
        st = self.started[out.name]
        fresh = quads - st
        assert len(fresh) == 0 or len(fresh) == len(quads), ("mixed quadrant start", out.name, quads, st)
        hw_start = len(fresh) > 0
        st |= quads
        self.S.op("pe", lambda e: e.matmul(out, lhsT=lhsT, rhs=rhs, start=hw_start, stop=stop, skip_group_check=True),
                  reads=_nm([lhsT, rhs]) + ([] if start else _nm([out])), writes=_nm([out]))

    def tr(self, out, in_, ident):
        self.S.op("pe", lambda e: e.transpose(out, in_, ident), reads=_nm([in_, ident]), writes=_nm([out]))

    def act(self, out, in_, func, bias=0.0, scale=1.0, accum=None):
        kw = {}
        if accum is not None:
            kw["accum_out"] = accum
        self.S.op("act", lambda e: e.activation(out=out, in_=in_, func=func, bias=bias, scale=scale, **kw),
                  reads=_nm([in_, bias, scale]), writes=_nm([out, accum]))

    def copy(self, eng, out, in_):
        if eng == "act":
            self.S.op("act", lambda e: e.copy(out=out, in_=in_), reads=_nm([in_]), writes=_nm([out]))
        else:
            self.S.op(eng, lambda e: e.tensor_copy(out=out, in_=in_), reads=_nm([in_]), writes=_nm([out]))

    def tt(self, eng, out, in0, in1, op):
        self.S.op(eng, lambda e: e.tensor_tensor(out=out, in0=in0, in1=in1, op=op),
                  reads=_nm([in0, in1]), writes=_nm([out]))

    def ts(self, eng, out, in0, s1, op0, s2=None, op1=None):
        if op1 is None:
            self.S.op(eng, lambda e: e.tensor_scalar(out=out, in0=in0, scalar1=s1, scalar2=None, op0=op0),
                      reads=_nm([in0, s1]), writes=_nm([out]))
        else:
            self.S.op(eng, lambda e: e.tensor_scalar(out=out, in0=in0, scalar1=s1, scalar2=s2, op0=op0, op1=op1),
                      reads=_nm([in0, s1, s2]), writes=_nm([out]))

    def stt(self, eng, out, in0, scalar, in1, op0, op1):
        self.S.op(eng, lambda e: e.scalar_tensor_tensor(out=out, in0=in0, scalar=scalar, in1=in1, op0=op0, op1=op1),
                  reads=_nm([in0, scalar, in1]), writes=_nm([out]))

    def memset(self, eng, out, val):
        self.S.op(eng, lambda e: e.memset(out, val), writes=_nm([out]))

    def recip(self, out, in_):
        self.S.op("dve", lambda e: e.reciprocal(out=out, in_=in_), reads=_nm([in_]), writes=_nm([out]))

    def scan(self, out, d0, d1, init, op0, op1):
        self.S.op("dve", lambda e: e.tensor_tensor_scan(out=out, data0=d0, data1=d1, initial=init, op0=op0, op1=op1),
                  reads=_nm([d0, d1, init]), writes=_nm([out]))

    def reduce(self, out, in_, op):
        self.S.op("dve", lambda e: e.tensor_reduce(out=out, in_=in_, axis=mybir.AxisListType.X, op=op),
                  reads=_nm([in_]), writes=_nm([out]))


def host_consts():
    c = {}
    c["ident"] = np.eye(128, dtype=np.float32)
    bo = np.zeros((128, 128), np.float32)
    bo[:64, :64] = 1.0
    bo[64:, 64:] = 1.0
    c["blockones"] = bo
    j = np.arange(64)[:, None]
    i = np.arange(64)[None, :]
    m = np.zeros((64, 2, 256), np.float32)
    for d in range(2):
        strict = (j < i) if d == 0 else (j > i)
        incl = (j <= i) if d == 0 else (j >= i)
        m[:, d, 0:64] = strict
        m[:, d, 64:128] = incl
        m[:, d, 128:192] = strict
        m[:, d, 192:256] = incl
    c["rmask"] = m
    m2 = np.zeros((64, 2, 64), np.float32)
    m2[:, 0, :] = (i.T > j.T)
    ii = np.arange(64)[:, None]
    jj = np.arange(64)[None, :]
    m2[:, 0, :] = (jj < ii)
    m2[:, 1, :] = (jj > ii)
    c["nmask"] = m2
    rows = TL // 64
    row = np.repeat(np.arange(rows, dtype=np.float32), 64)
    col = np.tile(np.arange(64, dtype=np.float32), rows)
    nf = 8
    inv = np.power(np.float32(10000.0), -np.arange(nf, dtype=np.float32) / nf).astype(np.float32)
    ang = np.concatenate([row[:, None] * inv, col[:, None] * inv], axis=-1).astype(np.float32)
    cosf = np.ones((96, T), np.float32)
    sinf = np.zeros((96, T), np.float32)
    cosf[64:80, :TL] = np.cos(ang).T
    cosf[80:96, :TL] = np.cos(ang).T
    sinf[64:80, :TL] = np.sin(ang).T
    sinf[80:96, :TL] = np.sin(ang).T
    c["cosf"] = cosf
    c["sinf"] = sinf
    rm = np.zeros((96, 96), np.float32)
    for t in range(16):
        rm[80 + t, 64 + t] = -1.0
        rm[64 + t, 80 + t] = 1.0
    c["rotm"] = rm
    jj = np.arange(128)[:, None]
    ii = np.arange(512)[None, :]
    mm = np.zeros((128, 2, 4, 512), np.float32)
    for r in range(4):
        mm[:, 0, r, :] = (jj + 128 * r) <= ii
        mm[:, 1, r, :] = (jj + 128 * r) >= ii
    c["lmask"] = mm
    sel = np.zeros((8, 8, 128), np.float32)
    for k in range(8):
        sel[k, k, :] = 1.0
    c["sel8"] = sel
    dcol = np.zeros((8, 2), np.float32)
    dcol[0:4, 0] = 1.0
    dcol[4:8, 0] = -1.0
    dcol[4:8, 1] = 1.0
    c["dircols"] = dcol
    return c


PARAMS = ["mod_w", "mod_b", "norm1_g", "norm2_g", "w_in", "w_out", "ffn_w_in", "ffn_w_out",
          "rwkv_mu", "rwkv_w0", "rwkv_w2", "rwkv_a0", "rwkv_a2", "rwkv_g2", "rwkv_k_k", "rwkv_k_a", "rwkv_r_k",
          "rwkv_ln_g", "rwkv_ln_b", "mla_q_norm_g", "mla_w_uq", "mla_kv_norm_g", "mla_w_ukv",
          "mla_q_qknorm_g", "mla_k_qknorm_g", "mlstm_conv_w", "mlstm_conv_b", "mlstm_i_b", "mlstm_f_b",
          "mlstm_norm_g"]


def build(shapes, dbg=(), stop_after=None, layers=DEPTH):
    nc = bass.Bass("TRN2", target_bir_lowering=False)
    I = {}
    for name, shp in shapes.items():
        I[name] = nc.dram_tensor(name, list(shp), F32, kind="ExternalInput").ap()

    _scr = {}

    def scratch(name, shape, dt=F32):
        if name not in _scr:
            kind = "ExternalOutput" if name in dbg else "Internal"
            _scr[name] = nc.dram_tensor(name, list(shape), dt, kind=kind).ap()
        return _scr[name]

    out = nc.dram_tensor("out", [TL, D], F32, kind="ExternalOutput").ap()
    xTa = scratch("xTa", [D, T])
    xTb = scratch("xTb", [D, T])
    zT = scratch("zT", [IN_COLS, T])
    mixT = scratch("mixT", [D, T], BF16)
    ydir = scratch("ydir", [2, 256, T])
    rkvg = scratch("rkvg", [4, 256, T])
    k = KB(nc)
    S = k.S

    ident = k.sb([128, 128], F32, "ident")
    identb = k.sb([128, 128], BF16, "identb")
    bones = k.sb([128, 128], F32, "bones")
    onesb = k.sb([128, 128], BF16, "onesb")
    onesf = k.sb([128, 128], F32, "onesf")
    MOD = k.sb([128, 48, 2], F32, "MOD")
    A1 = k.sb([128, 8, 2], F32, "A1")
    A2 = k.sb([128, 8, 2], F32, "A2")
    epsc = k.sb([128, 1], F32, "epsc")
    k.dma(ident[:, :], I["ident"][:, :])
    k.dma(bones[:, :], I["blockones"][:, :])
    k.copy("dve", identb[:, :], ident[:, :])
    k.memset("dve", onesb[:, :], 1.0)
    k.memset("dve", onesf[:, :], 1.0)
    k.memset("dve", epsc[:, :], EPS)
    base_mark = k.mark()

    def xview(ap):
        return ap.rearrange("(f p) t -> p f t", p=128)

    def phase_A():
        xin = [k.sb([128, D], F32, "xin") for _ in range(2)]
        stage = [k.sb([128, 8, 512], F32, "stg") for _ in range(2)]
        n_t = 0
        for ci, (t0, n) in enumerate(CH):
            st = stage[ci % 2]
            for j in range(n // 128):
                xi = xin[n_t % 2]
                n_t += 1
                if t0 < TL:
                    k.dma(xi[:, :], I["x"][t0 + j * 128:t0 + (j + 1) * 128, :])
                else:
                    k.dma(xi[:, :], I["ctx"][j * 128:(j + 1) * 128, :])
                for half in range(2):
                    pb = k.bank()
                    for f in range(4):
                        k.tr(pb[:, f * 128:(f + 1) * 128], xi[:, (half * 4 + f) * 128:(half * 4 + f + 1) * 128], ident[:, :])
                    k.copy("act" if half == 0 else "dve", st[:, half * 4:half * 4 + 4, j * 128:(j + 1) * 128],
                           pb[:, :].rearrange("p (a b) -> p a b", a=4))
            k.dma(xview(xTa)[:, :, t0:t0 + n], st[:, :, 0:n], q="pool")

    def phase_M(li):
        cc = k.sb([128, 8, 2], F32, "cc")
        modb = k.sb([128, 48], F32, "modb")
        g1 = k.sb([128, 8], F32, "g1")
        g2 = k.sb([128, 8], F32, "g2")
        wst = [k.sb([128, 6144], F32, "wst") for _ in range(2)]
        tmp = k.sb([128, 8, 2], F32, "tmpm")
        with nc.allow_non_contiguous_dma(reason="tiny column-layout parameter loads"):
            k.dma(cc[:, :, 0], I["c"].rearrange("(f p) -> p f", p=128))
            k.dma(cc[:, :, 1], I["c_ctx"].rearrange("(f p) -> p f", p=128))
            k.dma(modb[:, :], I["mod_b"][li].rearrange("(o p) -> p o", p=128))
            k.dma(g1[:, :], I["norm1_g"][li].rearrange("(f p) -> p f", p=128))
            k.dma(g2[:, :], I["norm2_g"][li].rearrange("(f p) -> p f", p=128))
        k.act(cc[:, :, :], cc[:, :, :], AF.Silu)
        pb = k.bank()
        for kt in range(8):
            w = wst[kt % 2]
            k.dma(w[:, :], I["mod_w"][li, kt * 128:(kt + 1) * 128, :])
            for o in range(48):
                k.mm(pb[:, 2 * o:2 * o + 2], w[:, o * 128:(o + 1) * 128], cc[:, kt, :], start=(kt == 0), stop=(kt == 7))
        k.tt("dve", MOD[:, :, :], pb[:, 0:96].rearrange("p (o s) -> p o s", s=2),
             modb[:, :].unsqueeze(2).to_broadcast([128, 48, 2]), ALU.add)
        k.ts("dve", tmp[:, :, :], MOD[:, 8:16, :], 1.0, ALU.add)
        k.tt("dve", A1[:, :, :], tmp[:, :, :], g1[:, :].unsqueeze(2).to_broadcast([128, 8, 2]), ALU.mult)
        k.ts("dve", tmp[:, :, :], MOD[:, 32:40, :], 1.0, ALU.add)
        k.tt("dve", A2[:, :, :], tmp[:, :, :], g2[:, :].unsqueeze(2).to_broadcast([128, 8, 2]), ALU.mult)

    def norm_chunk(xc, n, s, A, sh_off, xm_out, sq, rstd, tmp):
        k.act(sq[:, 0:8, 0:n], xc[:, :, 0:n], AF.Square)
        pb = k.bank()
        for f in range(8):
            k.mm(pb[:, 0:n], onesb[:, :], sq[:, f, 0:n], start=(f == 0), stop=(f == 7))
        k.act(rstd[:, 0:n], pb[:, 0:n], AF.Sqrt, bias=epsc[:, 0:1], scale=1.0 / D)
        k.recip(rstd[:, 0:n], rstd[:, 0:n])
        for f in range(8):
            k.tt("dve" if f % 2 == 0 else "pool", tmp[:, f % 2, 0:n], xc[:, f, 0:n], rstd[:, 0:n], ALU.mult)
            k.act(xm_out[:, f, :], tmp[:, f % 2, 0:n], AF.Identity, bias=MOD[:, sh_off + f, s:s + 1], scale=A[:, f, s:s + 1])

    def phase_NP(li, xsrc):
        xm = k.sb([128, 8, T], BF16, "xm")
        wbf = k.sb([128, 8, IN_COLS], BF16, "wbf")
        k.dma(wbf[:, :, :], I["w_in"][li].rearrange("(f p) c -> p f c", p=128), cast=True)
        m1 = k.mark()
        xc = [k.sb([128, 8, 512], F32, "xc") for _ in range(2)]
        sq = k.sb([128, 8, 512], BF16, "sq")
        rstd = k.sb([128, 512], F32, "rstd")
        tmp = k.sb([128, 2, 512], F32, "tmpn")
        for ci, (t0, n) in enumerate(CH):
            x_ = xc[ci % 2]
            k.dma(x_[:, :, 0:n], xview(xsrc)[:, :, t0:t0 + n])
            norm_chunk(x_, n, 0 if t0 < TL else 1, A1, 0, xm[:, :, t0:t0 + n], sq, rstd, tmp)
        k.release(m1)
        zrow = [k.sb([128, T], F32, "zrow") for _ in range(2)]
        nct = (IN_COLS + 127) // 128
        ev = 0
        for ct in range(nct):
            c0 = ct * 128
            cw = min(128, IN_COLS - c0)
            zr = zrow[ct % 2]
            for (t0, n) in CH:
                pb = k.bank()
                for f in range(8):
                    k.mm(pb[0:cw, 0:n], wbf[:, f, c0:c0 + cw], xm[:, f, t0:t0 + n], start=(f == 0), stop=(f == 7))
                k.copy("act" if ev % 2 == 0 else "dve", zr[0:cw, t0:t0 + n], pb[0:cw, 0:n])
                ev += 1
            k.dma(zT[c0:c0 + cw, :], zr[0:cw, :], q="pool")

    def phase_O(li, xsrc, xdst):
        wo = k.sb([128, 8, D], BF16, "wo")
        k.dma(wo[:, :, :], I["w_out"][li].rearrange("(f p) c -> p f c", p=128), cast=True)
        mx = [k.sb([128, 8, 512], BF16, "mx") for _ in range(2)]
        xc = [k.sb([128, 8, 512], F32, "xco") for _ in range(2)]
        xo = [k.sb([128, 8, 512], F32, "xoo") for _ in range(2)]
        for ci, (t0, n) in enumerate(CH):
            s = 0 if t0 < TL else 1
            m_ = mx[ci % 2]
            x_ = xc[ci % 2]
            o_ = xo[ci % 2]
            k.dma(m_[:, :, 0:n], xview(mixT)[:, :, t0:t0 + n])
            k.dma(x_[:, :, 0:n], xview(xsrc)[:, :, t0:t0 + n])
            for of in range(8):
                pb = k.bank()
                for f in range(8):
                    k.mm(pb[:, 0:n], wo[:, f, of * 128:(of + 1) * 128], m_[:, f, 0:n], start=(f == 0), stop=(f == 7))
                k.stt("dve", o_[:, of, 0:n], pb[:, 0:n], MOD[:, 16 + of, s:s + 1], x_[:, of, 0:n], ALU.mult, ALU.add)
            k.dma(xview(xdst)[:, :, t0:t0 + n], o_[:, :, 0:n], q="pool")

    def phase_F(li, xsrc, xdst, final):
        w1 = k.sb([128, 8, 2 * FFN_H], BF16, "w1")
        w2 = k.sb([128, 22, D], BF16, "w2")
        k.dma(w1[:, :, :], I["ffn_w_in"][li].rearrange("(f p) c -> p f c", p=128), cast=True)
        k.dma(w2[:, :, :], I["ffn_w_out"][li].rearrange("(f p) c -> p f c", p=128), cast=True)
        xc = [k.sb([128, 8, 512], F32, "xcf")] * 2
        rstd = k.sb([128, 512], F32, "rstdf")
        tmp = k.sb([128, 2, 512], F32, "tmpf")
        xm = k.sb([128, 8, 512], BF16, "xmf")
        actT = k.sb([128, 22, 512], BF16, "actT")
        sq = actT
        sg = [tmp[:, 0, :], tmp[:, 1, :]]
        chunks = CH if not final else CH[:8]
        otb = [k.sb([128, D], F32, "otb") for _ in range(2)] if final else None
        for ci, (t0, n) in enumerate(chunks):
            s = 0 if t0 < TL else 1
            x_ = xc[ci % 2]
            k.dma(x_[:, :, 0:n], xview(xsrc)[:, :, t0:t0 + n])
            norm_chunk(x_, n, s, A2, 24, xm[:, :, 0:n], sq, rstd, tmp)
            for h in range(22):
                pg = k.bank()
                pu = k.bank()
                for f in range(8):
                    k.mm(pg[:, 0:n], w1[:, f, h * 128:(h + 1) * 128], xm[:, f, 0:n], start=(f == 0), stop=(f == 7))
                for f in range(8):
                    k.mm(pu[:, 0:n], w1[:, f, FFN_H + h * 128:FFN_H + (h + 1) * 128], xm[:, f, 0:n], start=(f == 0), stop=(f == 7))
                s_ = sg[h % 2]
                k.act(s_[:, 0:n], pg[:, 0:n], AF.Silu)
                k.tt("dve", actT[:, h, 0:n], s_[:, 0:n], pu[:, 0:n], ALU.mult)
            for of in range(8):
                pb = k.bank()
                for h in range(22):
                    k.mm(pb[:, 0:n], w2[:, h, of * 128:(of + 1) * 128], actT[:, h, 0:n], start=(h == 0), stop=(h == 21))
                k.stt("dve", x_[:, of, 0:n], pb[:, 0:n], MOD[:, 40 + of, s:s + 1], x_[:, of, 0:n], ALU.mult, ALU.add)
            if not final:
                k.dma(xview(xdst)[:, :, t0:t0 + n], x_[:, :, 0:n], q="pool")
            else:
                for j in range(n // 128):
                    ot = otb[j % 2]
                    for half in range(2):
                        pb = k.bank()
                        for f in range(4):
                            k.tr(pb[:, f * 128:(f + 1) * 128], x_[:, half * 4 + f, j * 128:(j + 1) * 128], ident[:, :])
                        k.copy("act" if half == 0 else "dve", ot[:, half * 512:(half + 1) * 512], pb[:, :])
                    k.dma(out[t0 + j * 128:t0 + (j + 1) * 128, :], ot[:, :], q="pool")


    def phase_MLA(li, ctx_out):
        wuq = k.sb([128, 4, 768], BF16, "wuq")
        wukv = k.sb([128, 2, 1024], BF16, "wukv")
        k.dma(wuq[:, :, :], I["mla_w_uq"][li].rearrange("(f p) c -> p f c", p=128), cast=True)
        k.dma(wukv[:, :, :], I["mla_w_ukv"][li].rearrange("(f p) c -> p f c", p=128), cast=True)
        gq = k.sb([128, 4], F32, "gq")
        gkv = k.sb([128, 2], F32, "gkv")
        gqk = k.sb([96, 2], F32, "gqk")
        rotm = k.sb([96, 96], F32, "rotm")
        with nc.allow_non_contiguous_dma(reason="tiny column-layout parameter loads"):
            k.dma(gq[:, :], I["mla_q_norm_g"][li].rearrange("(f p) -> p f", p=128))
            k.dma(gkv[:, :], I["mla_kv_norm_g"][li].rearrange("(f p) -> p f", p=128))
            k.dma(gqk[:, 0:1], I["mla_q_qknorm_g"][li].rearrange("(p o) -> p o", o=1))
            k.dma(gqk[:, 1:2], I["mla_k_qknorm_g"][li].rearrange("(p o) -> p o", o=1))
        k.dma(rotm[:, :], I["rotm"][:, :])
        k.ts("dve", gqk[:, 0:1], gqk[:, 0:1], float(96 ** -0.5), ALU.mult)
        QTd = scratch("QTd", [96, 8, T], BF16)
        qst = [k.sb([96, 512], BF16, "qst") for _ in range(2)]
        KT = k.sb([96, 8, T], BF16, "KT")
        V = k.sb([128, 34, 8, 65], BF16, "V")
        k.memset("pool", V[:, :, :, 64:65], 1.0)
        m1 = k.mark()
        cq = k.sb([128, 4, 512], F32, "cq")
        ckv = k.sb([128, 2, 512], F32, "ckv")
        krb = k.sb([96, 512], F32, "krb")
        cs = k.sb([96, 512], F32, "cs")
        sn = k.sb([96, 512], F32, "sn")
        sqb = k.sb([128, 4, 512], BF16, "sqb")
        rs = k.sb([128, 512], F32, "rs")
        cqn = k.sb([128, 4, 512], BF16, "cqn")
        ckvn = k.sb([128, 2, 512], BF16, "ckvn")
        kfull = k.sb([96, 512], F32, "kfull")
        qn = k.sb([96, 512], F32, "qn")
        sq96 = k.sb([96, 512], BF16, "sq96")
        rs96 = k.sb([96, 512], F32, "rs96")
        t1 = k.sb([96, 512], F32, "t1")
        t2 = k.sb([96, 512], F32, "t2")

        def norm_feat(src, nt, nfeat, gcol, dst, n):
            k.act(sqb[:, 0:nt, 0:n], src[:, 0:nt, 0:n], AF.Square)
            pb = k.bank()
            for f in range(nt):
                k.mm(pb[:, 0:n], onesb[:, :], sqb[:, f, 0:n], start=(f == 0), stop=(f == nt - 1))
            k.act(rs[:, 0:n], pb[:, 0:n], AF.Sqrt, bias=epsc[:, 0:1], scale=1.0 / nfeat)
            k.recip(rs[:, 0:n], rs[:, 0:n])
            for f in range(nt):
                k.stt("dve", dst[:, f, 0:n], src[:, f, 0:n], gcol[:, f:f + 1], rs[:, 0:n], ALU.mult, ALU.mult)

        def qk_norm_rope(src, gcol, dst, n):
            k.act(sq96[:, 0:n], src, AF.Square)
            pb = k.bank()
            k.mm(pb[0:96, 0:n], onesb[0:96, 0:96], sq96[0:96, 0:n])
            k.act(rs96[:, 0:n], pb[0:96, 0:n], AF.Sqrt, bias=epsc[0:96, 0:1], scale=1.0 / 96)
            k.recip(rs96[:, 0:n], rs96[:, 0:n])
            k.stt("dve", qn[:, 0:n], src, gcol, rs96[:, 0:n], ALU.mult, ALU.mult)
            pr = k.bank()
            k.mm(pr[0:96, 0:n], rotm[0:96, 0:96], qn[0:96, 0:n])
            k.tt("pool", t1[:, 0:n], qn[:, 0:n], cs[:, 0:n], ALU.mult)
            k.tt("dve", t2[:, 0:n], pr[0:96, 0:n], sn[:, 0:n], ALU.mult)
            k.tt("dve", dst, t1[:, 0:n], t2[:, 0:n], ALU.add)

        zcq = zT[1152:1664, :].rearrange("(f p) t -> p f t", p=128)
        zckv = zT[1664:1920, :].rearrange("(f p) t -> p f t", p=128)
        for (t0, n) in CH:
            k.dma(cq[:, :, 0:n], zcq[:, :, t0:t0 + n])
            k.dma(ckv[:, :, 0:n], zckv[:, :, t0:t0 + n])
            k.dma(krb[64:96, 0:n], zT[1920:1952, t0:t0 + n])
            k.dma(cs[:, 0:n], I["cosf"][:, t0:t0 + n])
            k.dma(sn[:, 0:n], I["sinf"][:, t0:t0 + n])
            norm_feat(cq, 4, 512, gq, cqn, n)
            norm_feat(ckv, 2, 256, gkv, ckvn, n)
            for h in range(8):
                pq = k.bank()
                for f in range(4):
                    k.mm(pq[0:96, 0:n], wuq[:, f, 96 * h:96 * h + 96], cqn[:, f, 0:n], start=(f == 0), stop=(f == 3))
                qk_norm_rope(pq[0:96, 0:n], gqk[:, 0:1], qst[h % 2][:, 0:n], n)
                k.dma(QTd[:, h, t0:t0 + n], qst[h % 2][:, 0:n], q="pool")
                pk = k.bank()
                for f in range(2):
                    k.mm(pk[0:64, 0:n], wukv[:, f, 128 * h:128 * h + 64], ckvn[:, f, 0:n], start=(f == 0), stop=(f == 1))
                k.copy("act", kfull[0:64, 0:n], pk[0:64, 0:n])
                k.copy("pool", kfull[64:96, 0:n], krb[64:96, 0:n])
                qk_norm_rope(kfull[:, 0:n], gqk[:, 1:2], KT[:, h, t0:t0 + n], n)
            for j in range(n // 128):
                pv = k.bank()
                for f in range(2):
                    k.mm(pv[:, 0:512], ckvn[:, f, j * 128:(j + 1) * 128],
                         wukv[:, f, :].rearrange("p (h c) -> p h c", c=128)[:, :, 64:128], start=(f == 0), stop=(f == 1))
                k.copy("act", V[:, t0 // 128 + j, :, 0:64], pv[:, 0:512].rearrange("p (h c) -> p h c", c=64))
        k.release(m1)
        pT = [k.sb([128, 512], BF16, "pT") for _ in range(3)]
        Otok = [k.sb([128, 4, 512], F32, "Otok") for _ in range(2)]
        OT = [k.sb([128, 4, 512], BF16, "OT") for _ in range(2)]
        rden = k.sb([128, 8, 4], F32, "rden")
        QTc = [k.sb([96, 8, 512], BF16, "QTc") for _ in range(2)]
        mv = mixT.rearrange("(f p) t -> p f t", p=128)
        chunks = CH if ctx_out else CH[:8]
        for ci, (t0, n) in enumerate(chunks):
            nq = n // 128
            keys = list(range(34)) if t0 < TL else [32, 33]
            ot = Otok[ci % 2]
            QT = QTc[ci % 2]
            k.dma(QT[:, :, 0:n], QTd[:, :, t0:t0 + n])
            for h in range(8):
                po = k.acc()
                for idx, kt in enumerate(keys):
                    ps_ = k.bank()
                    k.mm(ps_[:, 0:n], KT[0:96, h, kt * 128:(kt + 1) * 128], QT[0:96, h, 0:n])
                    p_ = pT[idx % 3]
                    k.act(p_[:, 0:n], ps_[:, 0:n], AF.Exp)
                    for qi in range(nq):
                        k.mm(po[:, qi * 65:(qi + 1) * 65], p_[:, qi * 128:(qi + 1) * 128], V[:, kt, h, :],
                             start=(idx == 0), stop=(idx == len(keys) - 1))
                pov = po[:, 0:nq * 65].rearrange("p (q c) -> p q c", c=65)
                k.recip(rden[:, h, 0:nq], pov[:, :, 64])
                for qi in range(nq):
                    k.ts("dve", ot[:, qi, h * 64:(h + 1) * 64], pov[:, qi, 0:64], rden[:, h, qi:qi + 1], ALU.mult)
            o2 = OT[ci % 2]
            for qi in range(nq):
                pb = k.bank()
                for f in range(4):
                    k.tr(pb[:, f * 128:(f + 1) * 128], ot[:, qi, f * 128:(f + 1) * 128], ident[:, :])
                k.copy("act", o2[:, :, qi * 128:(qi + 1) * 128], pb[:, :].rearrange("p (a b) -> p a b", a=4))
            k.dma(mv[:, 2:6, t0:t0 + n], o2[:, :, 0:n], q="pool")


    def phase_RW(li):
        zsT = scratch("zsT", [1152, T])
        Tp = T + 3
        C = 64
        mu = k.sb([128, 9], F32, "mu")
        muh = k.sb([128, 9], F32, "muh")
        omu = k.sb([128, 9], F32, "omu")
        k.dma(mu[:, :], I["rwkv_mu"][li].rearrange("(f p) -> p f", p=128))
        k.ts("dve", muh[:, :], mu[:, :], 0.5, ALU.mult)
        k.ts("dve", omu[:, :], mu[:, :], -1.0, ALU.mult, 1.0, ALU.add)
        m0 = k.mark()
        zpb = [k.sb([128, Tp], F32, "zp") for _ in range(2)]
        tb = [k.sb([128, Tp], F32, "tsh") for _ in range(2)]
        for rt in range(9):
            zp = zpb[rt % 2]
            tm = tb[rt % 2]
            k.memset("pool", zp[:, 0:1], 0.0)
            k.memset("pool", zp[:, TL + 1:TL + 2], 0.0)
            k.memset("pool", zp[:, Tp - 1:Tp], 0.0)
            k.dma(zp[:, 1:TL + 1], zT[rt * 128:(rt + 1) * 128, 0:TL])
            k.dma(zp[:, TL + 2:TL + 2 + TC], zT[rt * 128:(rt + 1) * 128, TL:T])
            k.tt("dve", tm[:, 1:Tp - 1], zp[:, 0:Tp - 2], zp[:, 2:Tp], ALU.add)
            k.ts("dve", tm[:, 1:Tp - 1], tm[:, 1:Tp - 1], muh[:, rt:rt + 1], ALU.mult)
            k.stt("dve", tm[:, 1:Tp - 1], zp[:, 1:Tp - 1], omu[:, rt:rt + 1], tm[:, 1:Tp - 1], ALU.mult, ALU.add)
            k.dma(zsT[rt * 128:(rt + 1) * 128, 0:TL], tm[:, 1:TL + 1], q="pool")
            k.dma(zsT[rt * 128:(rt + 1) * 128, TL:T], tm[:, TL + 2:TL + 2 + TC], q="pool")
        k.release(m0)
        kkc = k.sb([64, 4], F32, "kkc")
        kac = k.sb([64, 4], F32, "kac")
        omka = k.sb([64, 4], F32, "omka")
        w0c = k.sb([64, 2, 4], F32, "w0c")
        a0c = k.sb([64, 2, 4], F32, "a0c")
        w2d = [k.sb([64, 256], F32, "w2d") for _ in range(2)]
        a2d = [k.sb([64, 256], F32, "a2d") for _ in range(2)]
        g2sb = k.sb([128, 256], F32, "g2sb")
        cmask = k.sb([64, 512], F32, "cmask")
        rmask = k.sb([64, 2, 256], F32, "rmask")
        nmask = k.sb([64, 2, 64], F32, "nmask")
        ones64 = onesf[0:64, 0:64]
        id64 = ident[0:64, 0:64]
        k.dma(kkc[:, :], I["rwkv_k_k"][li].rearrange("(h p) -> p h", p=64))
        k.dma(kac[:, :], I["rwkv_k_a"][li].rearrange("(h p) -> p h", p=64))
        for d_ in range(2):
            k.dma(w0c[:, d_, :], I["rwkv_w0"][li, d_].rearrange("(h p) -> p h", p=64))
            k.dma(a0c[:, d_, :], I["rwkv_a0"][li, d_].rearrange("(h p) -> p h", p=64))
            k.dma(w2d[d_][:, :], I["rwkv_w2"][li, d_])
            k.dma(a2d[d_][:, :], I["rwkv_a2"][li, d_])
        k.dma(g2sb[:, :], I["rwkv_g2"][li])
        k.dma(rmask[:, :, :], I["rmask"])
        k.dma(nmask[:, :, :], I["nmask"])
        k.ts("dve", omka[:, :], kac[:, :], -1.0, ALU.mult, 1.0, ALU.add)
        k.memset("dve", cmask[:, :], 1.0)
        k.memset("dve", cmask[:, :].rearrange("p (c i) -> p c i", i=C)[:, :, 0:1], 0.0)
        m1 = k.mark()

        def v3(t, n):
            return t[:, 0:n].rearrange("p (c i) -> p c i", i=C)

        names = ["r_", "k_", "v_", "kq", "sqk", "nrm", "kk", "sig", "logw", "a_", "tq", "kt", "b_", "Gf", "G", "Gex", "E1", "E2", "E3", "E4"]
        Wt = [{nm: k.sb([64, 512], F32, nm) for nm in names if nm != "v_"} for _ in range(2)]
        Vh = [k.sb([64, 512], F32, "v_") for _ in range(4)]
        W = {h: dict(Wt[h % 2], v_=Vh[h]) for h in range(4)}
        AR = [k.sb([64, 8, 2, C], F32, "AR") for _ in range(4)]
        BK = [k.sb([64, 8, 2, C], F32, "BK") for _ in range(4)]
        BbT = [k.sb([64, 512], F32, "BbT") for _ in range(4)]
        KbT = [k.sb([64, 512], F32, "KbT") for _ in range(4)]
        gam = [k.sb([64, 8], F32, "gam") for _ in range(4)]
        yst = [k.sb([64, 2, 512], F32, "yst") for _ in range(2)]
        Sst = [k.sb([64, 2, 64], F32, "Sst") for _ in range(2)]
        wdd = k.sb([64, 512], F32, "wdd")
        add_ = k.sb([64, 512], F32, "add")
        gd = k.sb([128, 512], F32, "gd")
        gbuf = [k.sb([64, 512], F32, "gbuf") for _ in range(2)]
        BKV = [k.sb([64, 2, 3, C], F32, "BKV") for _ in range(2)]
        NPMQ = [k.sb([64, 2, 256], F32, "NPMQ") for _ in range(2)]
        N0 = [k.sb([64, 2, 64], F32, "N0") for _ in range(2)]
        NN = [[k.sb([64, 2, 2, 64], F32, "NN") for _ in range(2)] for _ in range(2)]
        TT = [[k.sb([64, 2, 64], F32, "TT") for _ in range(2)] for _ in range(2)]
        Xs = [k.sb([64, 2, 64], F32, "Xs") for _ in range(2)]
        Us = [k.sb([64, 2, 64], F32, "Us") for _ in range(2)]
        Stmp = [k.sb([64, 2, 64], F32, "Stmp") for _ in range(2)]
        NEG = -float(np.exp(-0.5))
        for d in range(2):
            for hp in range(2):
                k.memset("dve", Sst[hp][:, :, :], 0.0)
            groups = [CH[8]] + (CH[:8] if d == 0 else CH[7::-1])
            for (t0, n) in groups:
                ncnk = n // C
                k.dma(wdd[:, 0:n], zsT[768 + 64 * d:768 + 64 * d + 64, t0:t0 + n])
                k.dma(add_[:, 0:n], zsT[896 + 64 * d:896 + 64 * d + 64, t0:t0 + n])
                k.act(wdd[:, 0:n], wdd[:, 0:n], AF.Tanh)
                if d == 0:
                    k.dma(gd[:, 0:n], zsT[1024:1152, t0:t0 + n])
                    k.act(gd[:, 0:n], gd[:, 0:n], AF.Sigmoid)
                for h in range(4):
                    w = W[h]
                    hs = slice(h * 64, (h + 1) * 64)
                    k.dma(w["r_"][:, 0:n], zsT[h * 64:(h + 1) * 64, t0:t0 + n])
                    k.dma(w["k_"][:, 0:n], zsT[256 + h * 64:256 + (h + 1) * 64, t0:t0 + n])
                    k.dma(w["v_"][:, 0:n], zsT[512 + h * 64:512 + (h + 1) * 64, t0:t0 + n])
                    if d == 0:
                        pg = k.bank()
                        k.mm(pg[0:64, 0:n], g2sb[:, hs], gd[:, 0:n])
                        k.copy("act", gbuf[h % 2][:, 0:n], pg[0:64, 0:n])
                        k.dma(rkvg[3, hs, t0:t0 + n], gbuf[h % 2][:, 0:n], q="pool")
                    k.ts("dve", w["kq"][:, 0:n], w["k_"][:, 0:n], kkc[:, h:h + 1], ALU.mult)
                    k.act(w["sqk"][:, 0:n], w["kq"][:, 0:n], AF.Square)
                    pss = k.bank()
                    k.mm(pss[0:64, 0:n], ones64, w["sqk"][:, 0:n])
                    k.act(w["nrm"][:, 0:n], pss[0:64, 0:n], AF.Sqrt)
                    k.ts("dve", w["nrm"][:, 0:n], w["nrm"][:, 0:n], 1e-12, ALU.max)
                    k.recip(w["nrm"][:, 0:n], w["nrm"][:, 0:n])
                    k.tt("dve", w["kk"][:, 0:n], w["kq"][:, 0:n], w["nrm"][:, 0:n], ALU.mult)
                    pw = k.bank()
                    k.mm(pw[0:64, 0:n], w2d[d][:, hs], wdd[:, 0:n])
                    k.act(w["sig"][:, 0:n], pw[0:64, 0:n], AF.Sigmoid, bias=w0c[:, d, h:h + 1])
                    k.ts("dve", w["logw"][:, 0:n], w["sig"][:, 0:n], NEG, ALU.mult)
                    pa = k.bank()
                    k.mm(pa[0:64, 0:n], a2d[d][:, hs], add_[:, 0:n])
                    k.act(w["a_"][:, 0:n], pa[0:64, 0:n], AF.Sigmoid, bias=a0c[:, d, h:h + 1])
                    k.act(w["tq"][:, 0:n], w["a_"][:, 0:n], AF.Identity, bias=omka[:, h:h + 1], scale=kac[:, h:h + 1])
                    k.tt("pool", w["kt"][:, 0:n], w["k_"][:, 0:n], w["tq"][:, 0:n], ALU.mult)
                    k.tt("pool", w["b_"][:, 0:n], w["kk"][:, 0:n], w["a_"][:, 0:n], ALU.mult)
                    k.scan(w["Gf"][:, 0:n], cmask[:, 0:n], w["logw"][:, 0:n], 0.0, ALU.mult, ALU.add)
                    totb = v3(w["Gf"], n)[:, :, C - 1:C].to_broadcast([64, ncnk, C])
                    if d == 0:
                        G = w["Gf"]
                        k.tt("dve", w["Gex"][:, 0:n], w["Gf"][:, 0:n], w["logw"][:, 0:n], ALU.subtract)
                    else:
                        G = w["G"]
                        k.tt("dve", v3(w["Gex"], n), totb, v3(w["Gf"], n), ALU.subtract)
                        k.tt("dve", w["G"][:, 0:n], w["Gex"][:, 0:n], w["logw"][:, 0:n], ALU.add)
                    k.act(w["E1"][:, 0:n], G[:, 0:n], AF.Exp)
                    k.act(w["E2"][:, 0:n], G[:, 0:n], AF.Exp, scale=-1.0)
                    k.act(w["E3"][:, 0:n], w["Gex"][:, 0:n], AF.Exp)
                    k.tt("dve", v3(w["E4"], n), totb, v3(G, n), ALU.subtract)
                    k.act(w["E4"][:, 0:n], w["E4"][:, 0:n], AF.Exp)
                    k.act(gam[h][:, 0:ncnk], v3(w["Gf"], n)[:, :, C - 1], AF.Exp)
                    k.stt("dve", AR[h][:, 0:ncnk, 0, :], v3(w["kk"], n), -1.0, v3(w["E3"], n), ALU.mult, ALU.mult)
                    k.tt("pool", AR[h][:, 0:ncnk, 1, :], v3(w["r_"], n), v3(w["E1"], n), ALU.mult)
                    k.tt("dve", BK[h][:, 0:ncnk, 0, :], v3(w["b_"], n), v3(w["E2"], n), ALU.mult)
                    k.tt("pool", BK[h][:, 0:ncnk, 1, :], v3(w["kt"], n), v3(w["E2"], n), ALU.mult)
                    k.tt("dve", BbT[h][:, 0:n], w["b_"][:, 0:n], w["E4"][:, 0:n], ALU.mult)
                    k.tt("pool", KbT[h][:, 0:n], w["kt"][:, 0:n], w["E4"][:, 0:n], ALU.mult)
                order = list(range(ncnk)) if d == 0 else list(range(ncnk - 1, -1, -1))
                for c in order:
                    cs_ = slice(c * C, (c + 1) * C)
                    for hp in range(2):
                        hh2 = [2 * hp, 2 * hp + 1]
                        pT_ = k.bank()
                        for hh in range(2):
                            h = hh2[hh]
                            k.tr(pT_[0:64, hh * 192:hh * 192 + 64], BbT[h][:, cs_], id64)
                            k.tr(pT_[0:64, hh * 192 + 64:hh * 192 + 128], KbT[h][:, cs_], id64)
                            k.tr(pT_[0:64, hh * 192 + 128:hh * 192 + 192], W[h]["v_"][:, cs_], id64)
                        bkv = BKV[hp]
                        k.copy("act", bkv[:, :, :, :], pT_[0:64, 0:384].rearrange("p (h a b) -> p h a b", h=2, a=3))
                        pA = k.bank()
                        pN = k.bank()
                        for hh in range(2):
                            h = hh2[hh]
                            arf = AR[h][:, c, :, :].rearrange("p a b -> p (a b)")
                            k.mm(pA[0:64, hh * 256:hh * 256 + 128], BK[h][:, c, 0, :], arf)
                            k.mm(pA[0:64, hh * 256 + 128:hh * 256 + 256], BK[h][:, c, 1, :], arf)
                            k.mm(pN[0:64, hh * 64:(hh + 1) * 64], AR[h][:, c, 0, :], BK[h][:, c, 0, :])
                        npmq = NPMQ[hp]
                        k.tt("dve", npmq[:, :, :], pA[0:64, 0:512].rearrange("p (h c) -> p h c", h=2),
                             rmask[:, d:d + 1, :].to_broadcast([64, 2, 256]), ALU.mult)
                        n0 = N0[hp]
                        k.tt("dve", n0[:, :, :], pN[0:64, 0:128].rearrange("p (h c) -> p h c", h=2),
                             nmask[:, d:d + 1, :].to_broadcast([64, 2, 64]), ALU.mult)
                        tt_ = TT[hp][0]
                        k.tt("pool", tt_[:, :, :], npmq[:, :, 0:64], id64.unsqueeze(1).to_broadcast([64, 2, 64]), ALU.add)
                        ncur = [n0[:, 0, :], n0[:, 1, :]]
                        ntcur = [npmq[:, 0, 0:64], npmq[:, 1, 0:64]]
                        for p in range(5):
                            pC = k.bank()
                            for hh in range(2):
                                k.mm(pC[0:64, hh * 128:hh * 128 + 64], ntcur[hh], ncur[hh])
                                k.mm(pC[0:64, hh * 128 + 64:hh * 128 + 128], ncur[hh], ntcur[hh])
                            nn = NN[hp][p % 2]
                            k.copy("act", nn[:, :, :, :], pC[0:64, 0:256].rearrange("p (h a c) -> p h a c", h=2, a=2))
                            ncur = [nn[:, 0, 0, :], nn[:, 1, 0, :]]
                            ntcur = [nn[:, 0, 1, :], nn[:, 1, 1, :]]
                            pD = k.bank()
                            for hh in range(2):
                                k.mm(pD[0:64, hh * 64:(hh + 1) * 64], ncur[hh], tt_[:, hh, :])
                            tn = TT[hp][(p + 1) % 2]
                            k.tt("dve", tn[:, :, :], tt_[:, :, :], pD[0:64, 0:128].rearrange("p (h c) -> p h c", h=2), ALU.add)
                            tt_ = tn
                        S_ = Sst[hp]
                        pX = k.bank()
                        for hh in range(2):
                            h = hh2[hh]
                            k.mm(pX[0:64, hh * 64:(hh + 1) * 64], AR[h][:, c, 0, :], S_[:, hh, :], start=True, stop=False)
                            k.mm(pX[0:64, hh * 64:(hh + 1) * 64], npmq[:, hh, 128:192], bkv[:, hh, 2, :], start=False, stop=True)
                        k.copy("act", Xs[hp][:, :, :], pX[0:64, 0:128].rearrange("p (h c) -> p h c", h=2))
                        pU = k.bank()
                        for hh in range(2):
                            k.mm(pU[0:64, hh * 64:(hh + 1) * 64], tt_[:, hh, :], Xs[hp][:, hh, :])
                        k.copy("act", Us[hp][:, :, :], pU[0:64, 0:128].rearrange("p (h c) -> p h c", h=2))
                        pY = k.bank()
                        pS = k.bank()
                        for hh in range(2):
                            h = hh2[hh]
                            ysl = pY[0:64, hh * 64:(hh + 1) * 64]
                            k.mm(ysl, S_[:, hh, :], AR[h][:, c, 1, :], start=True, stop=False)
                            k.mm(ysl, Us[hp][:, hh, :], npmq[:, hh, 64:128], start=False, stop=False)
                            k.mm(ysl, bkv[:, hh, 2, :], npmq[:, hh, 192:256], start=False, stop=True)
                            ssl = pS[0:64, hh * 64:(hh + 1) * 64]
                            k.mm(ssl, bkv[:, hh, 0, :], Us[hp][:, hh, :], start=True, stop=False)
                            k.mm(ssl, bkv[:, hh, 1, :], bkv[:, hh, 2, :], start=False, stop=True)
                        k.copy("act", yst[hp][:, :, cs_], pY[0:64, 0:128].rearrange("p (h c) -> p h c", h=2))
                        for hh in range(2):
                            h = hh2[hh]
                            k.stt("dve", S_[:, hh, :], S_[:, hh, :], gam[h][:, c:c + 1], pS[0:64, hh * 64:(hh + 1) * 64], ALU.mult, ALU.add)
                for hp in range(2):
                    k.dma(ydir[d, hp * 128:(hp + 1) * 128, t0:t0 + n].rearrange("(h v) t -> v h t", v=64), yst[hp][:, :, 0:n], q="pool")
        k.release(m1)
        rkc = k.sb([128, 2], F32, "rkc")
        lng = k.sb([128, 2], F32, "lng")
        lnb = k.sb([128, 2], F32, "lnb")
        gne = k.sb([128, 1], F32, "gne")
        k.dma(rkc[:, :], I["rwkv_r_k"][li].rearrange("(a b) c -> (b c) a", b=2))
        k.dma(lng[:, :], I["rwkv_ln_g"][li].rearrange("(h p) -> p h", p=128))
        k.dma(lnb[:, :], I["rwkv_ln_b"][li].rearrange("(h p) -> p h", p=128))
        k.memset("dve", gne[:, :], 64e-5)
        nm2 = ["yf", "yr", "r_", "k_", "v_", "g_", "cen", "sq", "rs", "rk"]
        P2 = [{nm: k.sb([128, 512], F32, nm) for nm in nm2} for _ in range(2)]
        ob = [k.sb([128, 512], BF16, "ob") for _ in range(2)]
        it = 0
        for (t0, n) in CH:
            for hp in range(2):
                w = P2[it % 2]
                o_ = ob[it % 2]
                it += 1
                hs = slice(hp * 128, (hp + 1) * 128)
                k.dma(w["yf"][:, 0:n], ydir[0, hs, t0:t0 + n])
                k.dma(w["yr"][:, 0:n], ydir[1, hs, t0:t0 + n])
                k.dma(w["r_"][:, 0:n], zsT[hp * 128:(hp + 1) * 128, t0:t0 + n])
                k.dma(w["k_"][:, 0:n], zsT[256 + hp * 128:256 + (hp + 1) * 128, t0:t0 + n])
                k.dma(w["v_"][:, 0:n], zsT[512 + hp * 128:512 + (hp + 1) * 128, t0:t0 + n])
                k.dma(w["g_"][:, 0:n], rkvg[3, hs, t0:t0 + n])
                k.tt("dve", w["yf"][:, 0:n], w["yf"][:, 0:n], w["yr"][:, 0:n], ALU.add)
                pm = k.bank()
                k.mm(pm[:, 0:n], bones[:, :], w["yf"][:, 0:n])
                k.stt("dve", w["cen"][:, 0:n], pm[:, 0:n], -1.0 / 64, w["yf"][:, 0:n], ALU.mult, ALU.add)
                k.act(w["sq"][:, 0:n], w["cen"][:, 0:n], AF.Square)
                pv = k.bank()
                k.mm(pv[:, 0:n], bones[:, :], w["sq"][:, 0:n])
                k.act(w["rs"][:, 0:n], pv[:, 0:n], AF.Sqrt, bias=gne[:, 0:1], scale=1.0 / 64)
                k.recip(w["rs"][:, 0:n], w["rs"][:, 0:n])
                k.tt("dve", w["cen"][:, 0:n], w["cen"][:, 0:n], w["rs"][:, 0:n], ALU.mult)
                k.act(w["cen"][:, 0:n], w["cen"][:, 0:n], AF.Identity, bias=lnb[:, hp:hp + 1], scale=lng[:, hp:hp + 1])
                k.stt("dve", w["rk"][:, 0:n], w["r_"][:, 0:n], rkc[:, hp:hp + 1], w["k_"][:, 0:n], ALU.mult, ALU.mult)
                pb2 = k.bank()
                k.mm(pb2[:, 0:n], bones[:, :], w["rk"][:, 0:n])
                k.tt("dve", w["rk"][:, 0:n], pb2[:, 0:n], w["v_"][:, 0:n], ALU.mult)
                k.tt("dve", w["cen"][:, 0:n], w["cen"][:, 0:n], w["rk"][:, 0:n], ALU.add)
                k.tt("dve", o_[:, 0:n], w["cen"][:, 0:n], w["g_"][:, 0:n], ALU.mult)
                k.dma(mixT[hs, t0:t0 + n], o_[:, 0:n], q="pool")


    def phase_ML(li):
        B0 = 1952
        Tp = T + 3
        cw = k.sb([32, 8, 3], F32, "cw")
        cb = k.sb([32, 8], F32, "cb")
        for j in range(3):
            k.dma(cw[:, :, j], I["mlstm_conv_w"][li, j].rearrange("(g p) -> p g", p=32))
        k.dma(cb[:, :], I["mlstm_conv_b"][li].rearrange("(g p) -> p g", p=32))
        qkb = k.sb([32, 8, T], BF16, "qkb")
        VL = k.sb([128, 34, 4, 65], BF16, "VL")
        k.memset("pool", VL[:, :, :, 64:65], 1.0)
        Hs = k.sb([128, 34, 256], F32, "Hs")
        acol = k.sb([128, 34, 8], F32, "acol")
        Fcol = k.sb([128, 34, 8], F32, "Fcol")
        Rb = k.sb([128, 8, 9], F32, "Rb")
        nRb = k.sb([128, 8, 9], F32, "nRb")
        lmask = k.sb([128, 2, 4, 512], BF16, "lmask")
        k.dma(lmask[:, :, :, :], I["lmask"], cast=True)
        m1 = k.mark()
        zpb = [k.sb([32, Tp], F32, "zpm") for _ in range(2)]
        accb = [k.sb([32, Tp], F32, "accm") for _ in range(2)]
        for g in range(8):
            zp = zpb[g % 2]
            acc = accb[g % 2]
            k.memset("pool", zp[:, 0:1], 0.0)
            k.memset("pool", zp[:, TL + 1:TL + 2], 0.0)
            k.memset("pool", zp[:, Tp - 1:Tp], 0.0)
            k.dma(zp[:, 1:TL + 1], zT[B0 + 32 * g:B0 + 32 * g + 32, 0:TL])
            k.dma(zp[:, TL + 2:TL + 2 + TC], zT[B0 + 32 * g:B0 + 32 * g + 32, TL:T])
            k.ts("dve", acc[:, 1:Tp - 1], zp[:, 1:Tp - 1], cw[:, g, 1:2], ALU.mult)
            k.stt("dve", acc[:, 1:Tp - 1], zp[:, 0:Tp - 2], cw[:, g, 0:1], acc[:, 1:Tp - 1], ALU.mult, ALU.add)
            k.stt("dve", acc[:, 1:Tp - 1], zp[:, 2:Tp], cw[:, g, 2:3], acc[:, 1:Tp - 1], ALU.mult, ALU.add)
            if g < 4:
                k.act(acc[:, 1:Tp - 1], acc[:, 1:Tp - 1], AF.Silu, bias=cb[:, g:g + 1])
                k.ts("dve", qkb[:, g, 0:TL], acc[:, 1:TL + 1], float(32 ** -0.5), ALU.mult)
                k.ts("dve", qkb[:, g, TL:T], acc[:, TL + 2:TL + 2 + TC], float(32 ** -0.5), ALU.mult)
            else:
                k.act(qkb[:, g, 0:TL], acc[:, 1:TL + 1], AF.Silu, bias=cb[:, g:g + 1])
                k.act(qkb[:, g, TL:T], acc[:, TL + 2:TL + 2 + TC], AF.Silu, bias=cb[:, g:g + 1])
        k.release(m1)
        vT = [k.sb([128, 2, 512], F32, "vT") for _ in range(2)]
        for ci, (t0, n) in enumerate(CH):
            v_ = vT[ci % 2]
            k.dma(v_[:, :, 0:n], zT[B0 + 256:B0 + 512, t0:t0 + n].rearrange("(f p) t -> p f t", p=128))
            for j in range(n // 128):
                pb = k.bank()
                for f in range(2):
                    k.tr(pb[:, f * 128:(f + 1) * 128], v_[:, f, j * 128:(j + 1) * 128], ident[:, :])
                k.copy("act", VL[:, t0 // 128 + j, :, 0:64], pb[:, 0:256].rearrange("p (h c) -> p h c", c=64))
        k.release(m1)
        GI = k.sb([8, T], F32, "GI")
        GF = k.sb([8, T], F32, "GF")
        Fp = k.sb([8, T], F32, "Fp")
        ones8 = k.sb([8, TL], F32, "ones8")
        bi = k.sb([8, 2], F32, "bi")
        dc = k.sb([8, 2], F32, "dc")
        sm = k.sb([8, 16], F32, "sm")
        Rt = k.sb([8, 9], F32, "Rt")
        cm = k.sb([8, 9], F32, "cm")
        pmx = k.sb([8, 8], F32, "pmx")
        smx = k.sb([8, 8], F32, "smx")
        sel8 = k.sb([8, 8, 128], F32, "sel8")
        GB = B0 + 768
        for d in range(2):
            k.dma(GI[4 * d:4 * d + 4, :], zT[GB + 8 * d:GB + 8 * d + 4, :])
            k.dma(GF[4 * d:4 * d + 4, :], zT[GB + 8 * d + 4:GB + 8 * d + 8, :])
        k.dma(bi[:, 0:1], I["mlstm_i_b"][li].rearrange("d (h o) -> (d h) o", o=1))
        k.dma(bi[:, 1:2], I["mlstm_f_b"][li].rearrange("d (h o) -> (d h) o", o=1))
        k.dma(dc[:, :], I["dircols"])
        k.dma(sel8[:, :, :], I["sel8"])
        k.memset("dve", ones8[:, :], 1.0)
        k.ts("dve", bi[:, :], bi[:, :], 1.0 / 15.0, ALU.mult)
        k.act(GI[:, :], GI[:, :], AF.Tanh, bias=bi[:, 0:1], scale=1.0 / 15.0)
        k.ts("dve", GI[:, :], GI[:, :], 15.0, ALU.mult)
        k.act(GF[:, :], GF[:, :], AF.Tanh, bias=bi[:, 1:2], scale=1.0 / 15.0)
        k.act(GF[:, :], GF[:, :], AF.Exp, scale=-15.0)
        k.ts("dve", GF[:, :], GF[:, :], 1.0, ALU.add)
        k.act(GF[:, :], GF[:, :], AF.Ln)
        k.ts("dve", GF[:, :], GF[:, :], -1.0, ALU.mult)
        k.scan(Fp[:, 0:TL], ones8[:, 0:TL], GF[:, 0:TL], 0.0, ALU.mult, ALU.add)
        k.scan(Fp[:, TL:T], ones8[:, 0:TC], GF[:, TL:T], 0.0, ALU.mult, ALU.add)
        k.copy("dve", sm[:, 0:1], Fp[:, TL - 1:TL])
        k.copy("dve", sm[:, 1:2], Fp[:, T - 1:T])
        k.stt("dve", sm[:, 2:3], sm[:, 0:1], dc[:, 1:2], sm[:, 1:2], ALU.mult, ALU.add)
        k.tt("dve", sm[:, 3:4], sm[:, 1:2], dc[:, 1:2], ALU.mult)
        k.ts("dve", Fp[:, :], Fp[:, :], dc[:, 0:1], ALU.mult)
        k.stt("dve", Fp[:, :], GF[:, :], dc[:, 1:2], Fp[:, :], ALU.mult, ALU.add)
        k.ts("dve", Fp[:, 0:TL], Fp[:, 0:TL], sm[:, 2:3], ALU.add)
        k.ts("dve", Fp[:, TL:T], Fp[:, TL:T], sm[:, 3:4], ALU.add)
        k.tt("dve", GI[:, :], GI[:, :], Fp[:, :], ALU.subtract)
        k.reduce(cm[:, 0:8], GI[:, 0:TL].rearrange("p (c i) -> p c i", i=512), ALU.max)
        k.reduce(cm[:, 8:9], GI[:, TL:T].rearrange("p (c i) -> p c i", i=TC), ALU.max)
        k.tt("dve", pmx[:, 0:1], cm[:, 0:1], cm[:, 8:9], ALU.max)
        for c in range(1, 8):
            k.tt("dve", pmx[:, c:c + 1], pmx[:, c - 1:c], cm[:, c:c + 1], ALU.max)
        k.tt("dve", smx[:, 7:8], cm[:, 7:8], cm[:, 8:9], ALU.max)
        for c in range(6, -1, -1):
            k.tt("dve", smx[:, c:c + 1], smx[:, c + 1:c + 2], cm[:, c:c + 1], ALU.max)
        k.tt("dve", smx[:, :], smx[:, :], pmx[:, :], ALU.subtract)
        k.stt("dve", Rt[:, 0:8], smx[:, :], dc[:, 1:2], pmx[:, :], ALU.mult, ALU.add)
        k.copy("dve", Rt[:, 8:9], cm[:, 8:9])
        pa_ = k.bank()
        pf_ = k.bank()
        for j in range(34):
            k.tr(pa_[:, 8 * j:8 * j + 8], GI[0:8, j * 128:(j + 1) * 128], ident[0:8, 0:8])
            k.tr(pf_[:, 8 * j:8 * j + 8], Fp[0:8, j * 128:(j + 1) * 128], ident[0:8, 0:8])
        k.copy("act", acol[:, :, :], pa_[:, 0:272].rearrange("p (j c) -> p j c", c=8))
        k.copy("act", Fcol[:, :, :], pf_[:, 0:272].rearrange("p (j c) -> p j c", c=8))
        pr_ = k.bank()
        for c in range(8):
            k.mm(pr_[:, 9 * c:9 * c + 9], sel8[:, c, :], Rt[:, :])
        k.copy("act", Rb[:, :, :], pr_[:, 0:72].rearrange("p (c i) -> p c i", i=9))
        k.ts("dve", nRb[:, :, :], Rb[:, :, :], -1.0, ALU.mult)
        k.release(m1)
        Et = k.sb([128, 34, 9], F32, "Et")
        Wb = [k.sb([128, 512], BF16, "Wb") for _ in range(3)]
        thr = k.sb([128, 4], F32, "thr")
        den = k.sb([128, 4], F32, "den")
        hd = k.sb([128, 64], F32, "hd")
        for d in range(2):
            for h in range(4):
                c = d * 4 + h
                for J in range(34):
                    k.ts("dve", Et[:, J, :], nRb[:, c, :], acol[:, J, c:c + 1], ALU.add)
                k.ts("dve", Et[:, :, :], Et[:, :, :], 0.0, ALU.min)
                k.act(Et[:, :, :], Et[:, :, :], AF.Exp)
                for I_, (t0, n) in enumerate(CH):
                    nq = n // 128
                    tb0 = t0 // 128
                    if t0 >= TL:
                        keys = [(32, 0), (33, 1)]
                    elif d == 0:
                        keys = [(32, None), (33, None)] + [(J, None) for J in range(0, 4 * I_)] + [(4 * I_ + r, r) for r in range(4)]
                    else:
                        keys = [(32, None), (33, None)] + [(J, None) for J in range(4 * I_ + 4, 32)] + [(4 * I_ + r, r) for r in range(4)]
                    def contributes(r, qi):
                        if r is None:
                            return True
                        return (r <= qi) if d == 0 else (r >= qi)
                    lists = {qi: [ix for ix, (J, r) in enumerate(keys) if contributes(r, qi)] for qi in range(nq)}
                    po = k.acc()
                    for ix, (J, r) in enumerate(keys):
                        ps_ = k.bank()
                        k.mm(ps_[:, 0:n], qkb[0:32, 4 + h, J * 128:(J + 1) * 128], qkb[0:32, h, t0:t0 + n])
                        w_ = Wb[ix % 3]
                        if r is None:
                            k.ts("dve", w_[:, 0:n], ps_[:, 0:n], Et[:, J, I_:I_ + 1], ALU.mult)
                        else:
                            k.stt("dve", w_[:, 0:n], ps_[:, 0:n], Et[:, J, I_:I_ + 1], lmask[:, d, r, 0:n], ALU.mult, ALU.mult)
                        for qi in range(nq):
                            if ix in lists[qi]:
                                k.mm(po[:, qi * 65:(qi + 1) * 65], w_[:, qi * 128:(qi + 1) * 128], VL[:, J, h, :],
                                     start=(ix == lists[qi][0]), stop=(ix == lists[qi][-1]))
                    pov = po[:, 0:nq * 65].rearrange("p (q c) -> p q c", c=65)
                    k.act(thr[:, 0:nq], Fcol[:, tb0:tb0 + nq, c], AF.Exp, bias=nRb[:, c, I_:I_ + 1], scale=-1.0)
                    k.act(den[:, 0:nq], pov[:, :, 64], AF.Abs)
                    k.tt("dve", den[:, 0:nq], den[:, 0:nq], thr[:, 0:nq], ALU.max)
                    k.recip(den[:, 0:nq], den[:, 0:nq])
                    for qi in range(nq):
                        if d == 0:
                            k.ts("dve", Hs[:, tb0 + qi, h * 64:(h + 1) * 64], pov[:, qi, 0:64], den[:, qi:qi + 1], ALU.mult)
                        else:
                            k.stt("dve", Hs[:, tb0 + qi, h * 64:(h + 1) * 64], pov[:, qi, 0:64], den[:, qi:qi + 1],
                                  Hs[:, tb0 + qi, h * 64:(h + 1) * 64], ALU.mult, ALU.add)
        ng = k.sb([128, 2], F32, "ng")
        k.dma(ng[:, :], I["mlstm_norm_g"][li].rearrange("(f p) -> p f", p=128))
        ssq = k.sb([128, 4], F32, "ssq")
        junk = k.sb([128, 64], F32, "junk")
        hn = [k.sb([128, 256], F32, "hn") for _ in range(2)]
        oT = [k.sb([128, 2, 512], F32, "oT") for _ in range(2)]
        hT = [k.sb([128, 2, 512], F32, "hT") for _ in range(2)]
        hb = [k.sb([128, 2, 512], BF16, "hb") for _ in range(2)]
        for ci, (t0, n) in enumerate(CH):
            o_ = oT[ci % 2]
            k.dma(o_[:, :, 0:n], zT[B0 + 512:B0 + 768, t0:t0 + n].rearrange("(f p) t -> p f t", p=128))
            k.act(o_[:, :, 0:n], o_[:, :, 0:n], AF.Sigmoid)
            ht = hT[ci % 2]
            for j in range(n // 128):
                tix = t0 // 128 + j
                for h in range(4):
                    k.act(junk[:, :], Hs[:, tix, h * 64:(h + 1) * 64], AF.Square, accum=ssq[:, h:h + 1])
                k.act(ssq[:, :], ssq[:, :], AF.Sqrt, bias=epsc[:, 0:1], scale=1.0 / 64)
                k.recip(ssq[:, :], ssq[:, :])
                hn_ = hn[j % 2]
                k.tt("dve", hn_[:, :].rearrange("p (h c) -> p h c", c=64), Hs[:, tix, :].rearrange("p (h c) -> p h c", c=64),
                     ssq[:, :].unsqueeze(2).to_broadcast([128, 4, 64]), ALU.mult)
                pb = k.bank()
                for f in range(2):
                    k.tr(pb[:, f * 128:(f + 1) * 128], hn_[:, f * 128:(f + 1) * 128], ident[:, :])
                for f in range(2):
                    k.ts("dve", ht[:, f, j * 128:(j + 1) * 128], pb[:, f * 128:(f + 1) * 128], ng[:, f:f + 1], ALU.mult)
            k.tt("pool", hb[ci % 2][:, :, 0:n], ht[:, :, 0:n], o_[:, :, 0:n], ALU.mult)
            k.dma(mixT[768:1024, t0:t0 + n].rearrange("(f p) t -> p f t", p=128), hb[ci % 2][:, :, 0:n], q="pool")

    PH = dict(A=phase_A, M=phase_M, NP=phase_NP, O=phase_O, F=phase_F, MLA=phase_MLA, RW=phase_RW, ML=phase_ML)
    return nc, k, PH, base_mark, (xTa, xTb)


def finish_build(nc, k, PH, base_mark, xs, n_layers=DEPTH, enabled=("RW", "MLA", "ML"), tail=True):
    xTa, xTb = xs
    PH["A"]()
    k.release(base_mark)
    for li in range(n_layers):
        ctx_out = li < DEPTH - 1
        PH["M"](li)
        k.release(base_mark)
        PH["NP"](li, xTa)
        k.release(base_mark)
        if "RW" in enabled:
            PH["RW"](li)
            k.release(base_mark)
        if "MLA" in enabled:
            PH["MLA"](li, ctx_out)
            k.release(base_mark)
        if "ML" in enabled:
            PH["ML"](li)
            k.release(base_mark)
        if tail:
            PH["O"](li, xTa, xTb)
            k.release(base_mark)
            PH["F"](li, xTb, xTa, li == DEPTH - 1)
            k.release(base_mark)
    k.S.barrier()
    k.S.emit()
    return nc


def make_in_maps(inputs, consts, cores):
    maps = []
    for b in cores:
        m = {"x": np.ascontiguousarray(inputs["x"][b]), "ctx": np.ascontiguousarray(inputs["ctx"][b]),
             "c": np.ascontiguousarray(inputs["c"][b]), "c_ctx": np.ascontiguousarray(inputs["c_ctx"])}
        for p in PARAMS:
            m[p] = np.ascontiguousarray(inputs[p])
        m.update(consts)
        maps.append(m)
    return maps


def all_shapes(inputs, consts):
    shp = {"x": (TL, D), "ctx": (TC, D), "c": (D,), "c_ctx": (D,)}
    for p in PARAMS:
        shp[p] = tuple(inputs[p].shape)
    for kk, v in consts.items():
        shp[kk] = tuple(v.shape)
    return shp


def kernel(**inputs):
    inputs = {kk: np.asarray(v, dtype=np.float32) for kk, v in inputs.items()}
    consts = host_consts()
    nc, k, PH, bm, xs = build(all_shapes(inputs, consts))
    finish_build(nc, k, PH, bm, xs)
    cores = list(range(8))
    res = run_bass_kernel_spmd(nc, make_in_maps(inputs, consts, cores), core_ids=cores)
    return np.stack([np.asarray(res.results[b]["out"], dtype=np.float32) for b in cores], axis=0)
```

```python
import contextlib
import numpy as np
import ml_dtypes
import concourse.bass as bass
import concourse.mybir as mybir
from concourse.bass_utils import run_bass_kernel_spmd

F32 = mybir.dt.float32
BF16 = mybir.dt.bfloat16
ALU = mybir.AluOpType
AF = mybir.ActivationFunctionType

D = 1024
TL = 4096
TC = 256
T = TL + TC
DEPTH = 2
IN_COLS = 2736
FFN_H = 2816
CH = [(i * 512, 512) for i in range(8)] + [(TL, TC)]
NSLOT = 24
EPS = 1e-6
SB_BASE = 16640
SB_TOP = 229376 - 2048


class Sched:
    ENGS = ("pe", "act", "dve", "pool", "sp")

    def __init__(self, nc):
        self.nc = nc
        self.ops = {e: [] for e in self.ENGS}
        self.count = {e: 0 for e in self.ENGS}
        self.known = {e: {} for e in self.ENGS}
        self.sems = {}
        self.wr = {}
        self.rd = {}
        self.slot_n = 0
        self.slot_sp = 0
        self.slot_pl = 0
        self.slot_cnt = [0] * NSLOT
        self.n_ops = 0
        self.limit = None
        self.log = None
        self.capture = None

    def _deps(self, reads, writes):
        d = {}
        for b in reads:
            w = self.wr.get(b)
            if w and w[1] > d.get(w[0], 0):
                d[w[0]] = w[1]
        for b in writes:
            w = self.wr.get(b)
            if w and w[1] > d.get(w[0], 0):
                d[w[0]] = w[1]
            for s, v in self.rd.get(b, {}).items():
                if v > d.get(s, 0):
                    d[s] = v
        return d

    def _mark(self, reads, writes, src, val):
        for b in reads:
            self.rd.setdefault(b, {})[src] = val
        for b in writes:
            self.wr[b] = (src, val)
            self.rd[b] = {}

    def _waits(self, eng, deps):
        ws = []
        kn = self.known[eng]
        for src, val in deps.items():
            if kn.get(src, 0) >= val:
                continue
            if src == eng == "pe":
                continue
            kn[src] = val
            ws.append((src, val))
        return ws

    def op(self, eng, fn, reads=(), writes=()):
        if self.capture is not None:
            self.capture.append(("op", eng, fn, tuple(reads), tuple(writes)))
            return
        if self.limit is not None and self.n_ops >= self.limit:
            return
        if self.log is not None:
            import sys as _s
            self.log.append((self.n_ops, eng, _s._getframe(2).f_lineno))
        ws = self._waits(eng, self._deps(reads, writes))
        self.count[eng] += 1
        val = self.count[eng]
        self.ops[eng].append((ws, fn, (eng, 1)))
        self._mark(reads, writes, eng, val)
        self.n_ops += 1

    def dma(self, fn, reads=(), writes=(), q="sp"):
        if self.capture is not None:
            self.capture.append(("dma", q, fn, tuple(reads), tuple(writes)))
            return
        if self.limit is not None and self.n_ops >= self.limit:
            return
        if self.log is not None:
            import sys as _s
            self.log.append((self.n_ops, "dma-" + q, _s._getframe(2).f_lineno))
        if q == "sp":
            slot = self.slot_sp % 16
            self.slot_sp += 1
        else:
            slot = 16 + self.slot_pl % (NSLOT - 16)
            self.slot_pl += 1
        src = "q%d" % slot
        prev = self.slot_cnt[slot]
        deps = self._deps(reads, writes)
        if prev:
            deps[src] = max(deps.get(src, 0), prev)
        ws = self._waits(q, deps)
        self.slot_cnt[slot] = prev + 1
        self.ops[q].append((ws, fn, (src, 16)))
        self._mark(reads, writes, src, prev + 1)
        self.n_ops += 1

    def replay(self, items, n=None):
        cnt = 0
        while items and (n is None or cnt < n):
            kind, e, fn, rd, wr = items.pop(0)
            if kind == "op":
                self.op(e, fn, rd, wr)
            else:
                self.dma(fn, rd, wr, q=e)
            cnt += 1

    def barrier(self):
        tot = dict(self.count)
        for s in range(NSLOT):
            if self.slot_cnt[s]:
                tot["q%d" % s] = self.slot_cnt[s]
        for e in self.ENGS:
            ws = self._waits(e, {k: v for k, v in tot.items() if v > 0 and k != e})
            if ws:
                self.ops[e].append((ws, None, None))
        self.wr = {}
        self.rd = {}

    def emit(self):
        nc = self.nc
        EP = 12000
        with contextlib.ExitStack() as st:
            for e in self.ENGS:
                ne = max(1, (self.count[e] + EP - 1) // EP)
                self.sems[e] = [st.enter_context(nc.semaphore("s_%s%d" % (e, i))) for i in range(ne)]
            for s in range(NSLOT):
                self.sems["q%d" % s] = st.enter_context(nc.semaphore("s_q%d" % s))
            block = st.enter_context(nc.Block())
            sems = self.sems

            def run(engname):
                def body(e):
                    n_done = 0
                    for ws, fn, inc in self.ops[engname]:
                        for src, val in ws:
                            if src[0] == "q":
                                e.wait_ge(sems[src], val * 16)
                            else:
                                e.wait_ge(sems[src][(val - 1) // EP], (val - 1) % EP + 1)
                        if fn is not None:
                            if inc[0][0] == "q":
                                fn(e).then_inc(sems[inc[0]], 16)
                            else:
                                fn(e).then_inc(sems[engname][n_done // EP], 1)
                                n_done += 1
                return body
            block.tensor(run("pe"))
            block.scalar(run("act"))
            block.vector(run("dve"))
            block.gpsimd(run("pool"))
            block.sync(run("sp"))


def _nm(aps):
    out = []
    for a in aps:
        if a is None or isinstance(a, (int, float)):
            continue
        out.append(a.name)
    return out


class KB:
    def __init__(self, nc):
        self.nc = nc
        self.S = Sched(nc)
        self.ptr = SB_BASE
        self.uid = 0
        self.ps = [nc.alloc_psum_tensor("psb%d" % i, [128, 512], F32) for i in range(8)]
        self.ps_i = 0
        self.pool = [0, 1, 2, 3, 4, 5]
        self.started = {}
        self.acc_i = 0

    def sb(self, shape, dt=F32, name="t"):
        per = 1
        for s in shape[1:]:
            per *= s
        nbytes = per * (4 if dt == F32 else 2)
        nbytes = (nbytes + 63) // 64 * 64
        self.uid += 1
        assert self.ptr + nbytes <= SB_TOP, ("SBUF overflow", name, self.ptr, nbytes)
        t = self.nc.alloc_sbuf_tensor_at("%s_%d" % (name, self.uid), list(shape), dt, offset=self.ptr)
        self.ptr += nbytes
        return t

    def mark(self):
        return self.ptr

    def release(self, mark):
        self.S.barrier()
        self.ptr = mark

    def bank(self):
        b = self.ps[self.pool[self.ps_i % len(self.pool)]]
        self.ps_i += 1
        self.started[b[:, :].name] = set()
        return b

    def acc(self):
        b = self.ps[6 + self.acc_i % 2]
        self.acc_i += 1
        self.started[b[:, :].name] = set()
        return b

    def dma(self, out, in_, q="sp", cast=False):
        if cast:
            q = "pool"
        self.S.dma(lambda e: e.dma_start(out=out, in_=in_, allow_slow_non_contiguous=True), reads=_nm([in_]), writes=_nm([out]), q=q)

    def mm(self, out, lhsT, rhs, start=True, stop=True):
        p0 = out.base_partition()
        p1 = p0 + out.shape[0]
        quads = set(range(p0 // 32, (p1 - 1) // 32 + 1))
        st = self.started[out.name]
        fresh = quads - st
        assert len(fresh) == 0 or len(fresh) == len(quads), ("mixed quadrant start", out.name, quads, st)
        hw_start = len(fresh) > 0
        st |= quads
        self.S.op("pe", lambda e: e.matmul(out, lhsT=lhsT, rhs=rhs, start=hw_start, stop=stop, skip_group_check=True),
                  reads=_nm([lhsT, rhs]) + ([] if start else _nm([out])), writes=_nm([out]))

    def tr(self, out, in_, ident):
        self.S.op("pe", lambda e: e.transpose(out, in_, ident), reads=_nm([in_, ident]), writes=_nm([out]))

    def act(self, out, in_, func, bias=0.0, scale=1.0, accum=None):
        kw = {}
        if accum is not None:
            kw["accum_out"] = accum
        self.S.op("act", lambda e: e.activation(out=out, in_=in_, func=func, bias=bias, scale=scale, **kw),
                  reads=_nm([in_, bias, scale]), writes=_nm([out, accum]))

    def copy(self, eng, out, in_):
        if eng == "act":
            self.S.op("act", lambda e: e.copy(out=out, in_=in_), reads=_nm([in_]), writes=_nm([out]))
        else:
            self.S.op(eng, lambda e: e.tensor_copy(out=out, in_=in_), reads=_nm([in_]), writes=_nm([out]))

    def tt(self, eng, out, in0, in1, op):
        self.S.op(eng, lambda e: e.tensor_tensor(out=out, in0=in0, in1=in1, op=op),
                  reads=_nm([in0, in1]), writes=_nm([out]))

    def ts(self, eng, out, in0, s1, op0, s2=None, op1=None):
        if op1 is None:
            self.S.op(eng, lambda e: e.tensor_scalar(out=out, in0=in0, scalar1=s1, scalar2=None, op0=op0),
                      reads=_nm([in0, s1]), writes=_nm([out]))
        else:
            self.S.op(eng, lambda e: e.tensor_scalar(out=out, in0=in0, scalar1=s1, scalar2=s2, op0=op0, op1=op1),
                      reads=_nm([in0, s1, s2]), writes=_nm([out]))

    def stt(self, eng, out, in0, scalar, in1, op0, op1):
        self.S.op(eng, lambda e: e.scalar_tensor_tensor(out=out, in0=in0, scalar=scalar, in1=in1, op0=op0, op1=op1),
                  reads=_nm([in0, scalar, in1]), writes=_nm([out]))

    def memset(self, eng, out, val):
        self.S.op(eng, lambda e: e.memset(out, val), writes=_nm([out]))

    def recip(self, out, in_):
        self.S.op("dve", lambda e: e.reciprocal(out=out, in_=in_), reads=_nm([in_]), writes=_nm([out]))

    def scan(self, out, d0, d1, init, op0, op1):
        self.S.op("dve", lambda e: e.tensor_tensor_scan(out=out, data0=d0, data1=d1, initial=init, op0=op0, op1=op1),
                  reads=_nm([d0, d1, init]), writes=_nm([out]))

    def reduce(self, out, in_, op):
        self.S.op("dve", lambda e: e.tensor_reduce(out=out, in_=in_, axis=mybir.AxisListType.X, op=op),
                  reads=_nm([in_]), writes=_nm([out]))


def host_consts():
    c = {}
    c["ident"] = np.eye(128, dtype=np.float32)
    bo = np.zeros((128, 128), np.float32)
    bo[:64, :64] = 1.0
    bo[64:, 64:] = 1.0
    c["blockones"] = bo
    j = np.arange(64)[:, None]
    i = np.arange(64)[None, :]
    m = np.zeros((64, 2, 256), np.float32)
    for d in range(2):
        strict = (j < i) if d == 0 else (j > i)
        incl = (j <= i) if d == 0 else (j >= i)
        m[:, d, 0:64] = strict
        m[:, d, 64:128] = incl
        m[:, d, 128:192] = strict
        m[:, d, 192:256] = incl
    c["rmask"] = m
    m2 = np.zeros((64, 2, 64), np.float32)
    m2[:, 0, :] = (i.T > j.T)
    ii = np.arange(64)[:, None]
    jj = np.arange(64)[None, :]
    m2[:, 0, :] = (jj < ii)
    m2[:, 1, :] = (jj > ii)
    c["nmask"] = m2
    rows = TL // 64
    row = np.repeat(np.arange(rows, dtype=np.float32), 64)
    col = np.tile(np.arange(64, dtype=np.float32), rows)
    nf = 8
    inv = np.power(np.float32(10000.0), -np.arange(nf, dtype=np.float32) / nf).astype(np.float32)
    ang = np.concatenate([row[:, None] * inv, col[:, None] * inv], axis=-1).astype(np.float32)
    cosf = np.ones((96, T), np.float32)
    sinf = np.zeros((96, T), np.float32)
    cosf[64:80, :TL] = np.cos(ang).T
    cosf[80:96, :TL] = np.cos(ang).T
    sinf[64:80, :TL] = np.sin(ang).T
    sinf[80:96, :TL] = np.sin(ang).T
    c["cosf"] = cosf
    c["sinf"] = sinf
    rm = np.zeros((96, 96), np.float32)
    for t in range(16):
        rm[80 + t, 64 + t] = -1.0
        rm[64 + t, 80 + t] = 1.0
    c["rotm"] = rm
    jj = np.arange(128)[:, None]
    ii = np.arange(512)[None, :]
    mm = np.zeros((128, 2, 4, 512), np.float32)
    for r in range(4):
        mm[:, 0, r, :] = (jj + 128 * r) <= ii
        mm[:, 1, r, :] = (jj + 128 * r) >= ii
    c["lmask"] = mm
    sel = np.zeros((8, 8, 128), np.float32)
    for k in range(8):
        sel[k, k, :] = 1.0
    c["sel8"] = sel
    dcol = np.zeros((8, 2), np.float32)
    dcol[0:4, 0] = 1.0
    dcol[4:8, 0] = -1.0
    dcol[4:8, 1] = 1.0
    c["dircols"] = dcol
    return c


PARAMS = ["mod_w", "mod_b", "norm1_g", "norm2_g", "w_in", "w_out", "ffn_w_in", "ffn_w_out",
          "rwkv_mu", "rwkv_w0", "rwkv_w2", "rwkv_a0", "rwkv_a2", "rwkv_g2", "rwkv_k_k", "rwkv_k_a", "rwkv_r_k",
          "rwkv_ln_g", "rwkv_ln_b", "mla_q_norm_g", "mla_w_uq", "mla_kv_norm_g", "mla_w_ukv",
          "mla_q_qknorm_g", "mla_k_qknorm_g", "mlstm_conv_w", "mlstm_conv_b", "mlstm_i_b", "mlstm_f_b",
          "mlstm_norm_g"]


def build(shapes, dbg=(), stop_after=None, layers=DEPTH):
    nc = bass.Bass("TRN2", target_bir_lowering=False)
    I = {}
    for name, shp in shapes.items():
        I[name] = nc.dram_tensor(name, list(shp), F32, kind="ExternalInput").ap()

    _scr = {}

    def scratch(name, shape, dt=F32):
        if name not in _scr:
            kind = "ExternalOutput" if name in dbg else "Internal"
            _scr[name] = nc.dram_tensor(name, list(shape), dt, kind=kind).ap()
        return _scr[name]

    out = nc.dram_tensor("out", [TL, D], F32, kind="ExternalOutput").ap()
    xTa = scratch("xTa", [D, T])
    xTb = scratch("xTb", [D, T])
    zT = scratch("zT", [IN_COLS, T])
    mixT = scratch("mixT", [D, T], BF16)
    ydir = scratch("ydir", [2, 256, T])
    rkvg = scratch("rkvg", [4, 256, T])
    k = KB(nc)
    S = k.S

    ident = k.sb([128, 128], F32, "ident")
    identb = k.sb([128, 128], BF16, "identb")
    bones = k.sb([128, 128], F32, "bones")
    onesb = k.sb([128, 128], BF16, "onesb")
    onesf = k.sb([128, 128], F32, "onesf")
    MOD = k.sb([128, 48, 2], F32, "MOD")
    A1 = k.sb([128, 8, 2], F32, "A1")
    A2 = k.sb([128, 8, 2], F32, "A2")
    epsc = k.sb([128, 1], F32, "epsc")
    k.dma(ident[:, :], I["ident"][:, :])
    k.dma(bones[:, :], I["blockones"][:, :])
    k.copy("dve", identb[:, :], ident[:, :])
    k.memset("dve", onesb[:, :], 1.0)
    k.memset("dve", onesf[:, :], 1.0)
    k.memset("dve", epsc[:, :], EPS)
    base_mark = k.mark()

    def xview(ap):
        return ap.rearrange("(f p) t -> p f t", p=128)

    def phase_A():
        xin = [k.sb([128, D], F32, "xin") for _ in range(2)]
        stage = [k.sb([128, 8, 512], F32, "stg") for _ in range(2)]
        n_t = 0
        for ci, (t0, n) in enumerate(CH):
            st = stage[ci % 2]
            for j in range(n // 128):
                xi = xin[n_t % 2]
                n_t += 1
                if t0 < TL:
                    k.dma(xi[:, :], I["x"][t0 + j * 128:t0 + (j + 1) * 128, :])
                else:
                    k.dma(xi[:, :], I["ctx"][j * 128:(j + 1) * 128, :])
                for half in range(2):
                    pb = k.bank()
                    for f in range(4):
                        k.tr(pb[:, f * 128:(f + 1) * 128], xi[:, (half * 4 + f) * 128:(half * 4 + f + 1) * 128], ident[:, :])
                    k.copy("act" if half == 0 else "dve", st[:, half * 4:half * 4 + 4, j * 128:(j + 1) * 128],
                           pb[:, :].rearrange("p (a b) -> p a b", a=4))
            k.dma(xview(xTa)[:, :, t0:t0 + n], st[:, :, 0:n], q="pool")

    def phase_M(li):
        cc = k.sb([128, 8, 2], F32, "cc")
        modb = k.sb([128, 48], F32, "modb")
        g1 = k.sb([128, 8], F32, "g1")
        g2 = k.sb([128, 8], F32, "g2")
        wst = [k.sb([128, 6144], F32, "wst") for _ in range(2)]
        tmp = k.sb([128, 8, 2], F32, "tmpm")
        with nc.allow_non_contiguous_dma(reason="tiny column-layout parameter loads"):
            k.dma(cc[:, :, 0], I["c"].rearrange("(f p) -> p f", p=128))
            k.dma(cc[:, :, 1], I["c_ctx"].rearrange("(f p) -> p f", p=128))
            k.dma(modb[:, :], I["mod_b"][li].rearrange("(o p) -> p o", p=128))
            k.dma(g1[:, :], I["norm1_g"][li].rearrange("(f p) -> p f", p=128))
            k.dma(g2[:, :], I["norm2_g"][li].rearrange("(f p) -> p f", p=128))
        k.act(cc[:, :, :], cc[:, :, :], AF.Silu)
        pb = k.bank()
        for kt in range(8):
            w = wst[kt % 2]
            k.dma(w[:, :], I["mod_w"][li, kt * 128:(kt + 1) * 128, :])
            for o in range(48):
                k.mm(pb[:, 2 * o:2 * o + 2], w[:, o * 128:(o + 1) * 128], cc[:, kt, :], start=(kt == 0), stop=(kt == 7))
        k.tt("dve", MOD[:, :, :], pb[:, 0:96].rearrange("p (o s) -> p o s", s=2),
             modb[:, :].unsqueeze(2).to_broadcast([128, 48, 2]), ALU.add)
        k.ts("dve", tmp[:, :, :], MOD[:, 8:16, :], 1.0, ALU.add)
        k.tt("dve", A1[:, :, :], tmp[:, :, :], g1[:, :].unsqueeze(2).to_broadcast([128, 8, 2]), ALU.mult)
        k.ts("dve", tmp[:, :, :], MOD[:, 32:40, :], 1.0, ALU.add)
        k.tt("dve", A2[:, :, :], tmp[:, :, :], g2[:, :].unsqueeze(2).to_broadcast([128, 8, 2]), ALU.mult)

    def norm_chunk(xc, n, s, A, sh_off, xm_out, sq, rstd, tmp):
        k.act(sq[:, 0:8, 0:n], xc[:, :, 0:n], AF.Square)
        pb = k.bank()
        for f in range(8):
            k.mm(pb[:, 0:n], onesb[:, :], sq[:, f, 0:n], start=(f == 0), stop=(f == 7))
        k.act(rstd[:, 0:n], pb[:, 0:n], AF.Sqrt, bias=epsc[:, 0:1], scale=1.0 / D)
        k.recip(rstd[:, 0:n], rstd[:, 0:n])
        for f in range(8):
            k.tt("dve" if f % 2 == 0 else "pool", tmp[:, f % 2, 0:n], xc[:, f, 0:n], rstd[:, 0:n], ALU.mult)
            k.act(xm_out[:, f, :], tmp[:, f % 2, 0:n], AF.Identity, bias=MOD[:, sh_off + f, s:s + 1], scale=A[:, f, s:s + 1])

    def phase_NP(li, xsrc):
        xm = k.sb([128, 8, T], BF16, "xm")
        wbf = k.sb([128, 8, IN_COLS], BF16, "wbf")
        k.dma(wbf[:, :, :], I["w_in"][li].rearrange("(f p) c -> p f c", p=128), cast=True)
        m1 = k.mark()
        xc = [k.sb([128, 8, 512], F32, "xc") for _ in range(2)]
        sq = k.sb([128, 8, 512], BF16, "sq")
        rstd = k.sb([128, 512], F32, "rstd")
        tmp = k.sb([128, 2, 512], F32, "tmpn")
        for ci, (t0, n) in enumerate(CH):
            x_ = xc[ci % 2]
            k.dma(x_[:, :, 0:n], xview(xsrc)[:, :, t0:t0 + n])
            norm_chunk(x_, n, 0 if t0 < TL else 1, A1, 0, xm[:, :, t0:t0 + n], sq, rstd, tmp)
        k.release(m1)
        zrow = [k.sb([128, T], F32, "zrow") for _ in range(2)]
        nct = (IN_COLS + 127) // 128
        ev = 0
        for ct in range(nct):
            c0 = ct * 128
            cw = min(128, IN_COLS - c0)
            zr = zrow[ct % 2]
            for (t0, n) in CH:
                pb = k.bank()
                for f in range(8):
                    k.mm(pb[0:cw, 0:n], wbf[:, f, c0:c0 + cw], xm[:, f, t0:t0 + n], start=(f == 0), stop=(f == 7))
                k.copy("act" if ev % 2 == 0 else "dve", zr[0:cw, t0:t0 + n], pb[0:cw, 0:n])
                ev += 1
            k.dma(zT[c0:c0 + cw, :], zr[0:cw, :], q="pool")

    def phase_O(li, xsrc, xdst):
        wo = k.sb([128, 8, D], BF16, "wo")
        k.dma(wo[:, :, :], I["w_out"][li].rearrange("(f p) c -> p f c", p=128), cast=True)
        mx = [k.sb([128, 8, 512], BF16, "mx") for _ in range(2)]
        xc = [k.sb([128, 8, 512], F32, "xco") for _ in range(2)]
        xo = [k.sb([128, 8, 512], F32, "xoo") for _ in range(2)]
        for ci, (t0, n) in enumerate(CH):
            s = 0 if t0 < TL else 1
            m_ = mx[ci % 2]
            x_ = xc[ci % 2]
            o_ = xo[ci % 2]
            k.dma(m_[:, :, 0:n], xview(mixT)[:, :, t0:t0 + n])
            k.dma(x_[:, :, 0:n], xview(xsrc)[:, :, t0:t0 + n])
            for of in range(8):
                pb = k.bank()
                for f in range(8):
                    k.mm(pb[:, 0:n], wo[:, f, of * 128:(of + 1) * 128], m_[:, f, 0:n], start=(f == 0), stop=(f == 7))
                k.stt("dve", o_[:, of, 0:n], pb[:, 0:n], MOD[:, 16 + of, s:s + 1], x_[:, of, 0:n], ALU.mult, ALU.add)
            k.dma(xview(xdst)[:, :, t0:t0 + n], o_[:, :, 0:n], q="pool")

    def phase_F(li, xsrc, xdst, final):
        w1 = k.sb([128, 8, 2 * FFN_H], BF16, "w1")
        w2 = k.sb([128, 22, D], BF16, "w2")
        k.dma(w1[:, :, :], I["ffn_w_in"][li].rearrange("(f p) c -> p f c", p=128), cast=True)
        k.dma(w2[:, :, :], I["ffn_w_out"][li].rearrange("(f p) c -> p f c", p=128), cast=True)
        xc = [k.sb([128, 8, 512], F32, "xcf")] * 2
        rstd = k.sb([128, 512], F32, "rstdf")
        tmp = k.sb([128, 2, 512], F32, "tmpf")
        xm = k.sb([128, 8, 512], BF16, "xmf")
        actT = k.sb([128, 22, 512], BF16, "actT")
        sq = actT
        sg = [tmp[:, 0, :], tmp[:, 1, :]]
        chunks = CH if not final else CH[:8]
        otb = [k.sb([128, D], F32, "otb") for _ in range(2)] if final else None
        for ci, (t0, n) in enumerate(chunks):
            s = 0 if t0 < TL else 1
            x_ = xc[ci % 2]
            k.dma(x_[:, :, 0:n], xview(xsrc)[:, :, t0:t0 + n])
            norm_chunk(x_, n, s, A2, 24, xm[:, :, 0:n], sq, rstd, tmp)
            for h in range(22):
                pg = k.bank()
                pu = k.bank()
                for f in range(8):
                    k.mm(pg[:, 0:n], w1[:, f, h * 128:(h + 1) * 128], xm[:, f, 0:n], start=(f == 0), stop=(f == 7))
                for f in range(8):
                    k.mm(pu[:, 0:n], w1[:, f, FFN_H + h * 128:FFN_H + (h + 1) * 128], xm[:, f, 0:n], start=(f == 0), stop=(f == 7))
                s_ = sg[h % 2]
                k.act(s_[:, 0:n], pg[:, 0:n], AF.Silu)
                k.tt("dve", actT[:, h, 0:n], s_[:, 0:n], pu[:, 0:n], ALU.mult)
            for of in range(8):
                pb = k.bank()
                for h in range(22):
                    k.mm(pb[:, 0:n], w2[:, h, of * 128:(of + 1) * 128], actT[:, h, 0:n], start=(h == 0), stop=(h == 21))
                k.stt("dve", x_[:, of, 0:n], pb[:, 0:n], MOD[:, 40 + of, s:s + 1], x_[:, of, 0:n], ALU.mult, ALU.add)
            if not final:
                k.dma(xview(xdst)[:, :, t0:t0 + n], x_[:, :, 0:n], q="pool")
            else:
                for j in range(n // 128):
                    ot = otb[j % 2]
                    for half in range(2):
                        pb = k.bank()
                        for f in range(4):
                            k.tr(pb[:, f * 128:(f + 1) * 128], x_[:, half * 4 + f, j * 128:(j + 1) * 128], ident[:, :])
                        k.copy("act" if half == 0 else "dve", ot[:, half * 512:(half + 1) * 512], pb[:, :])
                    k.dma(out[t0 + j * 128:t0 + (j + 1) * 128, :], ot[:, :], q="pool")


    def phase_MLA(li, ctx_out):
        wuq = k.sb([128, 4, 768], BF16, "wuq")
        wukv = k.sb([128, 2, 1024], BF16, "wukv")
        k.dma(wuq[:, :, :], I["mla_w_uq"][li].rearrange("(f p) c -> p f c", p=128), cast=True)
        k.dma(wukv[:, :, :], I["mla_w_ukv"][li].rearrange("(f p) c -> p f c", p=128), cast=True)
        gq = k.sb([128, 4], F32, "gq")
        gkv = k.sb([128, 2], F32, "gkv")
        gqk = k.sb([96, 2], F32, "gqk")
        rotm = k.sb([96, 96], F32, "rotm")
        with nc.allow_non_contiguous_dma(reason="tiny column-layout parameter loads"):
            k.dma(gq[:, :], I["mla_q_norm_g"][li].rearrange("(f p) -> p f", p=128))
            k.dma(gkv[:, :], I["mla_kv_norm_g"][li].rearrange("(f p) -> p f", p=128))
            k.dma(gqk[:, 0:1], I["mla_q_qknorm_g"][li].rearrange("(p o) -> p o", o=1))
            k.dma(gqk[:, 1:2], I["mla_k_qknorm_g"][li].rearrange("(p o) -> p o", o=1))
        k.dma(rotm[:, :], I["rotm"][:, :])
        k.ts("dve", gqk[:, 0:1], gqk[:, 0:1], float(96 ** -0.5), ALU.mult)
        QTd = scratch("QTd", [96, 8, T], BF16)
        qst = [k.sb([96, 512], BF16, "qst") for _ in range(2)]
        KT = k.sb([96, 8, T], BF16, "KT")
        V = k.sb([128, 34, 8, 65], BF16, "V")
        k.memset("pool", V[:, :, :, 64:65], 1.0)
        m1 = k.mark()
        cq = k.sb([128, 4, 512], F32, "cq")
        ckv = k.sb([128, 2, 512], F32, "ckv")
        krb = k.sb([96, 512], F32, "krb")
        cs = k.sb([96, 512], F32, "cs")
        sn = k.sb([96, 512], F32, "sn")
        sqb = k.sb([128, 4, 512], BF16, "sqb")
        rs = k.sb([128, 512], F32, "rs")
        cqn = k.sb([128, 4, 512], BF16, "cqn")
        ckvn = k.sb([128, 2, 512], BF16, "ckvn")
        kfull2 = [k.sb([96, 512], F32, "kfull") for _ in range(2)]
        QN = [dict(qn=k.sb([96, 512], F32, "qn"), sq96=k.sb([96, 512], BF16, "sq96"), rs96=k.sb([96, 512], F32, "rs96"),
                   t1=k.sb([96, 512], F32, "t1"), t2=k.sb([96, 512], F32, "t2")) for _ in range(3)]
        qn_i = [0]

        def norm_feat(src, nt, nfeat, gcol, dst, n):
            k.act(sqb[:, 0:nt, 0:n], src[:, 0:nt, 0:n], AF.Square)
            pb = k.bank()
            for f in range(nt):
                k.mm(pb[:, 0:n], onesb[:, :], sqb[:, f, 0:n], start=(f == 0), stop=(f == nt - 1))
            k.act(rs[:, 0:n], pb[:, 0:n], AF.Sqrt, bias=epsc[:, 0:1], scale=1.0 / nfeat)
            k.recip(rs[:, 0:n], rs[:, 0:n])
            for f in range(nt):
                k.stt("dve", dst[:, f, 0:n], src[:, f, 0:n], gcol[:, f:f + 1], rs[:, 0:n], ALU.mult, ALU.mult)

        def qk_norm_rope(src, gcol, dst, n):
            bs = QN[qn_i[0] % 3]
            qn_i[0] += 1
            qn, sq96, rs96, t1, t2 = bs["qn"], bs["sq96"], bs["rs96"], bs["t1"], bs["t2"]
            k.act(sq96[:, 0:n], src, AF.Square)
            pb = k.bank()
            k.mm(pb[0:96, 0:n], onesb[0:96, 0:96], sq96[0:96, 0:n])
            k.act(rs96[:, 0:n], pb[0:96, 0:n], AF.Sqrt, bias=epsc[0:96, 0:1], scale=1.0 / 96)
            k.recip(rs96[:, 0:n], rs96[:, 0:n])
            k.stt("dve", qn[:, 0:n], src, gcol, rs96[:, 0:n], ALU.mult, ALU.mult)
            pr = k.bank()
            k.mm(pr[0:96, 0:n], rotm[0:96, 0:96], qn[0:96, 0:n])
            k.tt("pool", t1[:, 0:n], qn[:, 0:n], cs[:, 0:n], ALU.mult)
            k.tt("dve", t2[:, 0:n], pr[0:96, 0:n], sn[:, 0:n], ALU.mult)
            k.tt("dve", dst, t1[:, 0:n], t2[:, 0:n], ALU.add)

        zcq = zT[1152:1664, :].rearrange("(f p) t -> p f t", p=128)
        zckv = zT[1664:1920, :].rearrange("(f p) t -> p f t", p=128)
        for (t0, n) in CH:
            k.dma(cq[:, :, 0:n], zcq[:, :, t0:t0 + n])
            k.dma(ckv[:, :, 0:n], zckv[:, :, t0:t0 + n])
            k.dma(krb[64:96, 0:n], zT[1920:1952, t0:t0 + n])
            k.dma(cs[:, 0:n], I["cosf"][:, t0:t0 + n])
            k.dma(sn[:, 0:n], I["sinf"][:, t0:t0 + n])
            norm_feat(cq, 4, 512, gq, cqn, n)
            norm_feat(ckv, 2, 256, gkv, ckvn, n)
            for h in range(8):
                pq = k.bank()
                for f in range(4):
                    k.mm(pq[0:96, 0:n], wuq[:, f, 96 * h:96 * h + 96], cqn[:, f, 0:n], start=(f == 0), stop=(f == 3))
                qk_norm_rope(pq[0:96, 0:n], gqk[:, 0:1], qst[h % 2][:, 0:n], n)
                k.dma(QTd[:, h, t0:t0 + n], qst[h % 2][:, 0:n], q="pool")
                pk = k.bank()
                for f in range(2):
                    k.mm(pk[0:64, 0:n], wukv[:, f, 128 * h:128 * h + 64], ckvn[:, f, 0:n], start=(f == 0), stop=(f == 1))
                kfull = kfull2[h % 2]
                k.copy("act", kfull[0:64, 0:n], pk[0:64, 0:n])
                k.copy("pool", kfull[64:96, 0:n], krb[64:96, 0:n])
                qk_norm_rope(kfull[:, 0:n], gqk[:, 1:2], KT[:, h, t0:t0 + n], n)
            for j in range(n // 128):
                pv = k.bank()
                for f in range(2):
                    k.mm(pv[:, 0:512], ckvn[:, f, j * 128:(j + 1) * 128],
                         wukv[:, f, :].rearrange("p (h c) -> p h c", c=128)[:, :, 64:128], start=(f == 0), stop=(f == 1))
                k.copy("act", V[:, t0 // 128 + j, :, 0:64], pv[:, 0:512].rearrange("p (h c) -> p h c", c=64))
        k.release(m1)
        pT = [k.sb([128, 512], BF16, "pT") for _ in range(4)]
        Otok = [k.sb([128, 4, 512], F32, "Otok") for _ in range(2)]
        OT = [k.sb([128, 4, 512], BF16, "OT") for _ in range(2)]
        rden = k.sb([128, 8, 4], F32, "rden")
        QTc = [k.sb([96, 8, 512], BF16, "QTc") for _ in range(2)]
        mv = mixT.rearrange("(f p) t -> p f t", p=128)
        chunks = CH if ctx_out else CH[:8]
        for ci, (t0, n) in enumerate(chunks):
            nq = n // 128
            keys = list(range(34)) if t0 < TL else [32, 33]
            ot = Otok[ci % 2]
            QT = QTc[ci % 2]
            k.dma(QT[:, :, 0:n], QTd[:, :, t0:t0 + n])
            for h in range(8):
                po = k.acc()
                AHEAD = 4
                psl = {}

                def issue_qk(ix, h=h, n=n, QT=QT, keys=keys, psl=psl):
                    b_ = k.bank()
                    kt_ = keys[ix]
                    k.mm(b_[:, 0:n], KT[0:96, h, kt_ * 128:(kt_ + 1) * 128], QT[0:96, h, 0:n])
                    psl[ix] = b_
                for ix in range(min(AHEAD, len(keys))):
                    issue_qk(ix)
                for idx, kt in enumerate(keys):
                    ps_ = psl.pop(idx)
                    p_ = pT[idx % 4]
                    k.act(p_[:, 0:n], ps_[:, 0:n], AF.Exp)
                    if idx + AHEAD < len(keys):
                        issue_qk(idx + AHEAD)
                    for qi in range(nq):
                        k.mm(po[:, qi * 65:(qi + 1) * 65], p_[:, qi * 128:(qi + 1) * 128], V[:, kt, h, :],
                             start=(idx == 0), stop=(idx == len(keys) - 1))
                pov = po[:, 0:nq * 65].rearrange("p (q c) -> p q c", c=65)
                k.recip(rden[:, h, 0:nq], pov[:, :, 64])
                for qi in range(nq):
                    k.ts("dve", ot[:, qi, h * 64:(h + 1) * 64], pov[:, qi, 0:64], rden[:, h, qi:qi + 1], ALU.mult)
            o2 = OT[ci % 2]
            for qi in range(nq):
                pb = k.bank()
                for f in range(4):
                    k.tr(pb[:, f * 128:(f + 1) * 128], ot[:, qi, f * 128:(f + 1) * 128], ident[:, :])
                k.copy("act", o2[:, :, qi * 128:(qi + 1) * 128], pb[:, :].rearrange("p (a b) -> p a b", a=4))
            k.dma(mv[:, 2:6, t0:t0 + n], o2[:, :, 0:n], q="pool")


    def phase_RW(li):
        zsT = scratch("zsT", [1152, T])
        Tp = T + 3
        C = 64
        mu = k.sb([128, 9], F32, "mu")
        muh = k.sb([128, 9], F32, "muh")
        omu = k.sb([128, 9], F32, "omu")
        k.dma(mu[:, :], I["rwkv_mu"][li].rearrange("(f p) -> p f", p=128))
        k.ts("dve", muh[:, :], mu[:, :], 0.5, ALU.mult)
        k.ts("dve", omu[:, :], mu[:, :], -1.0, ALU.mult, 1.0, ALU.add)
        m0 = k.mark()
        zpb = [k.sb([128, Tp], F32, "zp") for _ in range(2)]
        tb = [k.sb([128, Tp], F32, "tsh") for _ in range(2)]
        for rt in range(9):
            zp = zpb[rt % 2]
            tm = tb[rt % 2]
            k.memset("pool", zp[:, 0:1], 0.0)
            k.memset("pool", zp[:, TL + 1:TL + 2], 0.0)
            k.memset("pool", zp[:, Tp - 1:Tp], 0.0)
            k.dma(zp[:, 1:TL + 1], zT[rt * 128:(rt + 1) * 128, 0:TL])
            k.dma(zp[:, TL + 2:TL + 2 + TC], zT[rt * 128:(rt + 1) * 128, TL:T])
            k.tt("dve", tm[:, 1:Tp - 1], zp[:, 0:Tp - 2], zp[:, 2:Tp], ALU.add)
            k.ts("dve", tm[:, 1:Tp - 1], tm[:, 1:Tp - 1], muh[:, rt:rt + 1], ALU.mult)
            k.stt("dve", tm[:, 1:Tp - 1], zp[:, 1:Tp - 1], omu[:, rt:rt + 1], tm[:, 1:Tp - 1], ALU.mult, ALU.add)
            k.dma(zsT[rt * 128:(rt + 1) * 128, 0:TL], tm[:, 1:TL + 1], q="pool")
            k.dma(zsT[rt * 128:(rt + 1) * 128, TL:T], tm[:, TL + 2:TL + 2 + TC], q="pool")
        k.release(m0)
        kkc = k.sb([64, 4], F32, "kkc")
        kac = k.sb([64, 4], F32, "kac")
        omka = k.sb([64, 4], F32, "omka")
        w0c = k.sb([64, 2, 4], F32, "w0c")
        a0c = k.sb([64, 2, 4], F32, "a0c")
        w2d = [k.sb([64, 256], F32, "w2d") for _ in range(2)]
        a2d = [k.sb([64, 256], F32, "a2d") for _ in range(2)]
        g2sb = k.sb([128, 256], F32, "g2sb")
        cmask = k.sb([64, 512], F32, "cmask")
        rmask = k.sb([64, 2, 256], F32, "rmask")
        nmask = k.sb([64, 2, 64], F32, "nmask")
        ones64 = onesf[0:64, 0:64]
        id64 = ident[0:64, 0:64]
        k.dma(kkc[:, :], I["rwkv_k_k"][li].rearrange("(h p) -> p h", p=64))
        k.dma(kac[:, :], I["rwkv_k_a"][li].rearrange("(h p) -> p h", p=64))
        for d_ in range(2):
            k.dma(w0c[:, d_, :], I["rwkv_w0"][li, d_].rearrange("(h p) -> p h", p=64))
            k.dma(a0c[:, d_, :], I["rwkv_a0"][li, d_].rearrange("(h p) -> p h", p=64))
            k.dma(w2d[d_][:, :], I["rwkv_w2"][li, d_])
            k.dma(a2d[d_][:, :], I["rwkv_a2"][li, d_])
        k.dma(g2sb[:, :], I["rwkv_g2"][li])
        k.dma(rmask[:, :, :], I["rmask"])
        k.dma(nmask[:, :, :], I["nmask"])
        k.ts("dve", omka[:, :], kac[:, :], -1.0, ALU.mult, 1.0, ALU.add)
        k.memset("dve", cmask[:, :], 1.0)
        k.memset("dve", cmask[:, :].rearrange("p (c i) -> p c i", i=C)[:, :, 0:1], 0.0)
        m1 = k.mark()

        def v3(t, n):
            return t[:, 0:n].rearrange("p (c i) -> p c i", i=C)

        GN = 256
        names = ["r_", "k_", "v_", "nrm", "kk", "logw", "a_", "kt", "b_", "Gf", "G", "Gex", "E1", "E2", "E4"]
        Wt = [{nm: k.sb([64, GN], F32, nm) for nm in names if nm != "v_"} for _ in range(2)]
        Vh = [[k.sb([64, GN], F32, "v_") for _ in range(4)] for _ in range(2)]
        AR = [[k.sb([64, GN // C, 2, C], F32, "AR") for _ in range(4)] for _ in range(2)]
        BK = [[k.sb([64, GN // C, 2, C], F32, "BK") for _ in range(4)] for _ in range(2)]
        BbT = [[k.sb([64, GN], F32, "BbT") for _ in range(4)] for _ in range(2)]
        KbT = [[k.sb([64, GN], F32, "KbT") for _ in range(4)] for _ in range(2)]
        gam = [[k.sb([64, GN // C], F32, "gam") for _ in range(4)] for _ in range(2)]
        yst = [[k.sb([64, 2, GN], F32, "yst") for _ in range(2)] for _ in range(2)]
        Sst = [k.sb([64, 2, 64], F32, "Sst") for _ in range(2)]
        wdd = k.sb([64, GN], F32, "wdd")
        add_ = k.sb([64, GN], F32, "add")
        gd = k.sb([128, GN], F32, "gd")
        gbuf = [k.sb([64, GN], F32, "gbuf") for _ in range(2)]
        NSL = 4
        SL = [dict(bkv=k.sb([64, 2, 3, C], F32, "BKV"), npmq=k.sb([64, 2, 256], F32, "NPMQ"), n0=k.sb([64, 2, 64], F32, "N0"),
                   nn=[k.sb([64, 2, 2, 64], F32, "NN") for _ in range(2)], tt=[k.sb([64, 2, 64], F32, "TT") for _ in range(2)],
                   xs=k.sb([64, 2, 64], F32, "Xs"), us=k.sb([64, 2, 64], F32, "Us")) for _ in range(NSL)]
        NEG = -float(np.exp(-0.5))

        def emit_prep(d, t0, n, par):
            ARg, BKg, BbTg, KbTg, gamg = AR[par], BK[par], BbT[par], KbT[par], gam[par]
            Wg = {h: dict(Wt[h % 2], v_=Vh[par][h]) for h in range(4)}
            ncnk = n // C
            k.dma(wdd[:, 0:n], zsT[768 + 64 * d:768 + 64 * d + 64, t0:t0 + n])
            k.dma(add_[:, 0:n], zsT[896 + 64 * d:896 + 64 * d + 64, t0:t0 + n])
            k.act(wdd[:, 0:n], wdd[:, 0:n], AF.Tanh)
            if d == 0:
                k.dma(gd[:, 0:n], zsT[1024:1152, t0:t0 + n])
                k.act(gd[:, 0:n], gd[:, 0:n], AF.Sigmoid)
            for h in range(4):
                w = Wg[h]
                hs = slice(h * 64, (h + 1) * 64)
                k.dma(w["r_"][:, 0:n], zsT[h * 64:(h + 1) * 64, t0:t0 + n])
                k.dma(w["k_"][:, 0:n], zsT[256 + h * 64:256 + (h + 1) * 64, t0:t0 + n])
                k.dma(w["v_"][:, 0:n], zsT[512 + h * 64:512 + (h + 1) * 64, t0:t0 + n])
                if d == 0:
                    pg = k.bank()
                    k.mm(pg[0:64, 0:n], g2sb[:, hs], gd[:, 0:n])
                    k.copy("act", gbuf[h % 2][:, 0:n], pg[0:64, 0:n])
                    k.dma(rkvg[3, hs, t0:t0 + n], gbuf[h % 2][:, 0:n], q="pool")
                k.ts("dve", w["kk"][:, 0:n], w["k_"][:, 0:n], kkc[:, h:h + 1], ALU.mult)
                k.act(w["nrm"][:, 0:n], w["kk"][:, 0:n], AF.Square)
                pss = k.bank()
                k.mm(pss[0:64, 0:n], ones64, w["nrm"][:, 0:n])
                k.act(w["nrm"][:, 0:n], pss[0:64, 0:n], AF.Sqrt)
                k.ts("dve", w["nrm"][:, 0:n], w["nrm"][:, 0:n], 1e-12, ALU.max)
                k.recip(w["nrm"][:, 0:n], w["nrm"][:, 0:n])
                k.tt("dve", w["kk"][:, 0:n], w["kk"][:, 0:n], w["nrm"][:, 0:n], ALU.mult)
                pw = k.bank()
                k.mm(pw[0:64, 0:n], w2d[d][:, hs], wdd[:, 0:n])
                k.act(w["logw"][:, 0:n], pw[0:64, 0:n], AF.Sigmoid, bias=w0c[:, d, h:h + 1])
                k.ts("dve", w["logw"][:, 0:n], w["logw"][:, 0:n], NEG, ALU.mult)
                pa = k.bank()
                k.mm(pa[0:64, 0:n], a2d[d][:, hs], add_[:, 0:n])
                k.act(w["a_"][:, 0:n], pa[0:64, 0:n], AF.Sigmoid, bias=a0c[:, d, h:h + 1])
                k.act(w["kt"][:, 0:n], w["a_"][:, 0:n], AF.Identity, bias=omka[:, h:h + 1], scale=kac[:, h:h + 1])
                k.tt("pool", w["kt"][:, 0:n], w["k_"][:, 0:n], w["kt"][:, 0:n], ALU.mult)
                k.tt("pool", w["b_"][:, 0:n], w["kk"][:, 0:n], w["a_"][:, 0:n], ALU.mult)
                k.scan(w["Gf"][:, 0:n], cmask[:, 0:n], w["logw"][:, 0:n], 0.0, ALU.mult, ALU.add)
                totb = v3(w["Gf"], n)[:, :, C - 1:C].to_broadcast([64, ncnk, C])
                if d == 0:
                    G = w["Gf"]
                    k.tt("dve", w["Gex"][:, 0:n], w["Gf"][:, 0:n], w["logw"][:, 0:n], ALU.subtract)
                else:
                    G = w["G"]
                    k.tt("dve", v3(w["Gex"], n), totb, v3(w["Gf"], n), ALU.subtract)
                    k.tt("dve", w["G"][:, 0:n], w["Gex"][:, 0:n], w["logw"][:, 0:n], ALU.add)
                k.act(w["E1"][:, 0:n], G[:, 0:n], AF.Exp)
                k.act(w["E2"][:, 0:n], G[:, 0:n], AF.Exp, scale=-1.0)
                k.act(w["Gex"][:, 0:n], w["Gex"][:, 0:n], AF.Exp)
                k.tt("dve", v3(w["E4"], n), totb, v3(G, n), ALU.subtract)
                k.act(w["E4"][:, 0:n], w["E4"][:, 0:n], AF.Exp)
                k.act(gamg[h][:, 0:ncnk], v3(w["Gf"], n)[:, :, C - 1], AF.Exp)
                k.stt("dve", ARg[h][:, 0:ncnk, 0, :], v3(w["kk"], n), -1.0, v3(w["Gex"], n), ALU.mult, ALU.mult)
                k.tt("pool", ARg[h][:, 0:ncnk, 1, :], v3(w["r_"], n), v3(w["E1"], n), ALU.mult)
                k.tt("dve", BKg[h][:, 0:ncnk, 0, :], v3(w["b_"], n), v3(w["E2"], n), ALU.mult)
                k.tt("pool", BKg[h][:, 0:ncnk, 1, :], v3(w["kt"], n), v3(w["E2"], n), ALU.mult)
                k.tt("dve", BbTg[h][:, 0:n], w["b_"][:, 0:n], w["E4"][:, 0:n], ALU.mult)
                k.tt("pool", KbTg[h][:, 0:n], w["kt"][:, 0:n], w["E4"][:, 0:n], ALU.mult)

        def emit_chain(d, t0, n, par, drip):
            ARg, BKg, BbTg, KbTg, gamg, ystg = AR[par], BK[par], BbT[par], KbT[par], gam[par], yst[par]
            Wg = {h: dict(Wt[h % 2], v_=Vh[par][h]) for h in range(4)}
            ncnk = n // C
            order = list(range(ncnk)) if d == 0 else list(range(ncnk - 1, -1, -1))
            chains = [(c, hp) for c in order for hp in range(2)]
            for w0 in range(0, len(chains), NSL):
                wave = chains[w0:w0 + NSL]
                for si, (c, hp) in enumerate(wave):
                    sl = SL[si]
                    cs_ = slice(c * C, (c + 1) * C)
                    pT_ = k.bank()
                    for hh in range(2):
                        h = 2 * hp + hh
                        k.tr(pT_[0:64, hh * 192:hh * 192 + 64], BbTg[h][:, cs_], id64)
                        k.tr(pT_[0:64, hh * 192 + 64:hh * 192 + 128], KbTg[h][:, cs_], id64)
                        k.tr(pT_[0:64, hh * 192 + 128:hh * 192 + 192], Wg[h]["v_"][:, cs_], id64)
                    k.copy("act", sl["bkv"][:, :, :, :], pT_[0:64, 0:384].rearrange("p (h a b) -> p h a b", h=2, a=3))
                    drip()
                for si, (c, hp) in enumerate(wave):
                    sl = SL[si]
                    pA = k.bank()
                    pN = k.bank()
                    for hh in range(2):
                        h = 2 * hp + hh
                        arf = ARg[h][:, c, :, :].rearrange("p a b -> p (a b)")
                        k.mm(pA[0:64, hh * 256:hh * 256 + 128], BKg[h][:, c, 0, :], arf)
                        k.mm(pA[0:64, hh * 256 + 128:hh * 256 + 256], BKg[h][:, c, 1, :], arf)
                        k.mm(pN[0:64, hh * 64:(hh + 1) * 64], ARg[h][:, c, 0, :], BKg[h][:, c, 0, :])
                    npmq = sl["npmq"]
                    n0 = sl["n0"]
                    k.tt("dve", npmq[:, :, :], pA[0:64, 0:512].rearrange("p (h c) -> p h c", h=2),
                         rmask[:, d:d + 1, :].to_broadcast([64, 2, 256]), ALU.mult)
                    k.tt("dve", n0[:, :, :], pN[0:64, 0:128].rearrange("p (h c) -> p h c", h=2),
                         nmask[:, d:d + 1, :].to_broadcast([64, 2, 64]), ALU.mult)
                    k.tt("pool", sl["tt"][0][:, :, :], npmq[:, :, 0:64], id64.unsqueeze(1).to_broadcast([64, 2, 64]), ALU.add)
                    sl["ttc"] = sl["tt"][0]
                    sl["ncur"] = [n0[:, 0, :], n0[:, 1, :]]
                    sl["ntcur"] = [npmq[:, 0, 0:64], npmq[:, 1, 0:64]]
                    drip()
                for p in range(5):
                    for si, (c, hp) in enumerate(wave):
                        sl = SL[si]
                        pC = k.bank()
                        for hh in range(2):
                            k.mm(pC[0:64, hh * 128:hh * 128 + 64], sl["ntcur"][hh], sl["ncur"][hh])
                            k.mm(pC[0:64, hh * 128 + 64:hh * 128 + 128], sl["ncur"][hh], sl["ntcur"][hh])
                        nn = sl["nn"][p % 2]
                        k.copy("act", nn[:, :, :, :], pC[0:64, 0:256].rearrange("p (h a c) -> p h a c", h=2, a=2))
                        sl["ncur"] = [nn[:, 0, 0, :], nn[:, 1, 0, :]]
                        sl["ntcur"] = [nn[:, 0, 1, :], nn[:, 1, 1, :]]
                        drip()
                    for si, (c, hp) in enumerate(wave):
                        sl = SL[si]
                        pD = k.bank()
                        for hh in range(2):
                            k.mm(pD[0:64, hh * 64:(hh + 1) * 64], sl["ncur"][hh], sl["ttc"][:, hh, :])
                        tn = sl["tt"][(p + 1) % 2]
                        k.tt("dve", tn[:, :, :], sl["ttc"][:, :, :], pD[0:64, 0:128].rearrange("p (h c) -> p h c", h=2), ALU.add)
                        sl["ttc"] = tn
                        drip()
                for si, (c, hp) in enumerate(wave):
                    sl = SL[si]
                    cs_ = slice(c * C, (c + 1) * C)
                    npmq = sl["npmq"]
                    bkv = sl["bkv"]
                    tt_ = sl["ttc"]
                    S_ = Sst[hp]
                    pX = k.bank()
                    for hh in range(2):
                        h = 2 * hp + hh
                        k.mm(pX[0:64, hh * 64:(hh + 1) * 64], ARg[h][:, c, 0, :], S_[:, hh, :], start=True, stop=False)
                        k.mm(pX[0:64, hh * 64:(hh + 1) * 64], npmq[:, hh, 128:192], bkv[:, hh, 2, :], start=False, stop=True)
                    k.copy("act", sl["xs"][:, :, :], pX[0:64, 0:128].rearrange("p (h c) -> p h c", h=2))
                    pU = k.bank()
                    for hh in range(2):
                        k.mm(pU[0:64, hh * 64:(hh + 1) * 64], tt_[:, hh, :], sl["xs"][:, hh, :])
                    k.copy("act", sl["us"][:, :, :], pU[0:64, 0:128].rearrange("p (h c) -> p h c", h=2))
                    drip()
                    pY = k.bank()
                    pS = k.bank()
                    for hh in range(2):
                        h = 2 * hp + hh
                        ysl = pY[0:64, hh * 64:(hh + 1) * 64]
                        k.mm(ysl, S_[:, hh, :], ARg[h][:, c, 1, :], start=True, stop=False)
                        k.mm(ysl, sl["us"][:, hh, :], npmq[:, hh, 64:128], start=False, stop=False)
                        k.mm(ysl, bkv[:, hh, 2, :], npmq[:, hh, 192:256], start=False, stop=True)
                        ssl = pS[0:64, hh * 64:(hh + 1) * 64]
                        k.mm(ssl, bkv[:, hh, 0, :], sl["us"][:, hh, :], start=True, stop=False)
                        k.mm(ssl, bkv[:, hh, 1, :], bkv[:, hh, 2, :], start=False, stop=True)
                    k.copy("act", ystg[hp][:, :, cs_], pY[0:64, 0:128].rearrange("p (h c) -> p h c", h=2))
                    for hh in range(2):
                        h = 2 * hp + hh
                        k.stt("dve", S_[:, hh, :], S_[:, hh, :], gamg[h][:, c:c + 1], pS[0:64, hh * 64:(hh + 1) * 64], ALU.mult, ALU.add)
            for hp in range(2):
                k.dma(ydir[d, hp * 128:(hp + 1) * 128, t0:t0 + n].rearrange("(h v) t -> v h t", v=64), ystg[hp][:, :, 0:n], q="pool")

        for d in range(2):
            for hp in range(2):
                k.memset("dve", Sst[hp][:, :, :], 0.0)
            lat = [(i * GN, GN) for i in range(TL // GN)]
            groups = [(TL, TC)] + (lat if d == 0 else lat[::-1])
            k.pool = [4, 5]
            emit_prep(d, groups[0][0], groups[0][1], 0)
            for gi, (t0, n) in enumerate(groups):
                pend = []
                if gi + 1 < len(groups):
                    S.capture = pend
                    k.pool = [4, 5]
                    emit_prep(d, groups[gi + 1][0], groups[gi + 1][1], (gi + 1) % 2)
                    S.capture = None
                k.pool = [0, 1, 2, 3, 6, 7]
                emit_chain(d, t0, n, gi % 2, lambda: S.replay(pend, 3))
                S.replay(pend)
        k.pool = [0, 1, 2, 3, 4, 5]
        k.release(m1)
        rkc = k.sb([128, 2], F32, "rkc")
        lng = k.sb([128, 2], F32, "lng")
        lnb = k.sb([128, 2], F32, "lnb")
        gne = k.sb([128, 1], F32, "gne")
        k.dma(rkc[:, :], I["rwkv_r_k"][li].rearrange("(a b) c -> (b c) a", b=2))
        k.dma(lng[:, :], I["rwkv_ln_g"][li].rearrange("(h p) -> p h", p=128))
        k.dma(lnb[:, :], I["rwkv_ln_b"][li].rearrange("(h p) -> p h", p=128))
        k.memset("dve", gne[:, :], 64e-5)
        nm2 = ["yf", "yr", "r_", "k_", "v_", "g_", "cen", "sq", "rs", "rk"]
        P2 = [{nm: k.sb([128, 512], F32, nm) for nm in nm2} for _ in range(2)]
        ob = [k.sb([128, 512], BF16, "ob") for _ in range(2)]
        it = 0
        for (t0, n) in CH:
            for hp in range(2):
                w = P2[it % 2]
                o_ = ob[it % 2]
                it += 1
                hs = slice(hp * 128, (hp + 1) * 128)
                k.dma(w["yf"][:, 0:n], ydir[0, hs, t0:t0 + n])
                k.dma(w["yr"][:, 0:n], ydir[1, hs, t0:t0 + n])
                k.dma(w["r_"][:, 0:n], zsT[hp * 128:(hp + 1) * 128, t0:t0 + n])
                k.dma(w["k_"][:, 0:n], zsT[256 + hp * 128:256 + (hp + 1) * 128, t0:t0 + n])
                k.dma(w["v_"][:, 0:n], zsT[512 + hp * 128:512 + (hp + 1) * 128, t0:t0 + n])
                k.dma(w["g_"][:, 0:n], rkvg[3, hs, t0:t0 + n])
                k.tt("dve", w["yf"][:, 0:n], w["yf"][:, 0:n], w["yr"][:, 0:n], ALU.add)
                pm = k.bank()
                k.mm(pm[:, 0:n], bones[:, :], w["yf"][:, 0:n])
                k.stt("dve", w["cen"][:, 0:n], pm[:, 0:n], -1.0 / 64, w["yf"][:, 0:n], ALU.mult, ALU.add)
                k.act(w["sq"][:, 0:n], w["cen"][:, 0:n], AF.Square)
                pv = k.bank()
                k.mm(pv[:, 0:n], bones[:, :], w["sq"][:, 0:n])
                k.act(w["rs"][:, 0:n], pv[:, 0:n], AF.Sqrt, bias=gne[:, 0:1], scale=1.0 / 64)
                k.recip(w["rs"][:, 0:n], w["rs"][:, 0:n])
                k.tt("dve", w["cen"][:, 0:n], w["cen"][:, 0:n], w["rs"][:, 0:n], ALU.mult)
                k.act(w["cen"][:, 0:n], w["cen"][:, 0:n], AF.Identity, bias=lnb[:, hp:hp + 1], scale=lng[:, hp:hp + 1])
                k.stt("dve", w["rk"][:, 0:n], w["r_"][:, 0:n], rkc[:, hp:hp + 1], w["k_"][:, 0:n], ALU.mult, ALU.mult)
                pb2 = k.bank()
                k.mm(pb2[:, 0:n], bones[:, :], w["rk"][:, 0:n])
                k.tt("dve", w["rk"][:, 0:n], pb2[:, 0:n], w["v_"][:, 0:n], ALU.mult)
                k.tt("dve", w["cen"][:, 0:n], w["cen"][:, 0:n], w["rk"][:, 0:n], ALU.add)
                k.tt("dve", o_[:, 0:n], w["cen"][:, 0:n], w["g_"][:, 0:n], ALU.mult)
                k.dma(mixT[hs, t0:t0 + n], o_[:, 0:n], q="pool")


    def phase_ML(li):
        B0 = 1952
        Tp = T + 3
        cw = k.sb([32, 8, 3], F32, "cw")
        cb = k.sb([32, 8], F32, "cb")
        for j in range(3):
            k.dma(cw[:, :, j], I["mlstm_conv_w"][li, j].rearrange("(g p) -> p g", p=32))
        k.dma(cb[:, :], I["mlstm_conv_b"][li].rearrange("(g p) -> p g", p=32))
        qkb = k.sb([32, 8, T], BF16, "qkb")
        VL = k.sb([128, 34, 4, 65], BF16, "VL")
        k.memset("pool", VL[:, :, :, 64:65], 1.0)
        Hs = k.sb([128, 34, 256], F32, "Hs")
        acol = k.sb([128, 34, 8], F32, "acol")
        Fcol = k.sb([128, 34, 8], F32, "Fcol")
        Rb = k.sb([128, 8, 9], F32, "Rb")
        nRb = k.sb([128, 8, 9], F32, "nRb")
        lmask = k.sb([128, 2, 4, 512], BF16, "lmask")
        k.dma(lmask[:, :, :, :], I["lmask"], cast=True)
        m1 = k.mark()
        zpb = [k.sb([32, Tp], F32, "zpm") for _ in range(2)]
        accb = [k.sb([32, Tp], F32, "accm") for _ in range(2)]
        for g in range(8):
            zp = zpb[g % 2]
            acc = accb[g % 2]
            k.memset("pool", zp[:, 0:1], 0.0)
            k.memset("pool", zp[:, TL + 1:TL + 2], 0.0)
            k.memset("pool", zp[:, Tp - 1:Tp], 0.0)
            k.dma(zp[:, 1:TL + 1], zT[B0 + 32 * g:B0 + 32 * g + 32, 0:TL])
            k.dma(zp[:, TL + 2:TL + 2 + TC], zT[B0 + 32 * g:B0 + 32 * g + 32, TL:T])
            k.ts("dve", acc[:, 1:Tp - 1], zp[:, 1:Tp - 1], cw[:, g, 1:2], ALU.mult)
            k.stt("dve", acc[:, 1:Tp - 1], zp[:, 0:Tp - 2], cw[:, g, 0:1], acc[:, 1:Tp - 1], ALU.mult, ALU.add)
            k.stt("dve", acc[:, 1:Tp - 1], zp[:, 2:Tp], cw[:, g, 2:3], acc[:, 1:Tp - 1], ALU.mult, ALU.add)
            if g < 4:
                k.act(acc[:, 1:Tp - 1], acc[:, 1:Tp - 1], AF.Silu, bias=cb[:, g:g + 1])
                k.ts("dve", qkb[:, g, 0:TL], acc[:, 1:TL + 1], float(32 ** -0.5), ALU.mult)
                k.ts("dve", qkb[:, g, TL:T], acc[:, TL + 2:TL + 2 + TC], float(32 ** -0.5), ALU.mult)
            else:
                k.act(qkb[:, g, 0:TL], acc[:, 1:TL + 1], AF.Silu, bias=cb[:, g:g + 1])
                k.act(qkb[:, g, TL:T], acc[:, TL + 2:TL + 2 + TC], AF.Silu, bias=cb[:, g:g + 1])
        k.release(m1)
        vT = [k.sb([128, 2, 512], F32, "vT") for _ in range(2)]
        for ci, (t0, n) in enumerate(CH):
            v_ = vT[ci % 2]
            k.dma(v_[:, :, 0:n], zT[B0 + 256:B0 + 512, t0:t0 + n].rearrange("(f p) t -> p f t", p=128))
            for j in range(n // 128):
                pb = k.bank()
                for f in range(2):
                    k.tr(pb[:, f * 128:(f + 1) * 128], v_[:, f, j * 128:(j + 1) * 128], ident[:, :])
                k.copy("act", VL[:, t0 // 128 + j, :, 0:64], pb[:, 0:256].rearrange("p (h c) -> p h c", c=64))
        k.release(m1)
        GI = k.sb([8, T], F32, "GI")
        GF = k.sb([8, T], F32, "GF")
        Fp = k.sb([8, T], F32, "Fp")
        ones8 = k.sb([8, TL], F32, "ones8")
        bi = k.sb([8, 2], F32, "bi")
        dc = k.sb([8, 2], F32, "dc")
        sm = k.sb([8, 16], F32, "sm")
        Rt = k.sb([8, 9], F32, "Rt")
        cm = k.sb([8, 9], F32, "cm")
        pmx = k.sb([8, 8], F32, "pmx")
        smx = k.sb([8, 8], F32, "smx")
        sel8 = k.sb([8, 8, 128], F32, "sel8")
        GB = B0 + 768
        for d in range(2):
            k.dma(GI[4 * d:4 * d + 4, :], zT[GB + 8 * d:GB + 8 * d + 4, :])
            k.dma(GF[4 * d:4 * d + 4, :], zT[GB + 8 * d + 4:GB + 8 * d + 8, :])
        k.dma(bi[:, 0:1], I["mlstm_i_b"][li].rearrange("d (h o) -> (d h) o", o=1))
        k.dma(bi[:, 1:2], I["mlstm_f_b"][li].rearrange("d (h o) -> (d h) o", o=1))
        k.dma(dc[:, :], I["dircols"])
        k.dma(sel8[:, :, :], I["sel8"])
        k.memset("dve", ones8[:, :], 1.0)
        k.ts("dve", bi[:, :], bi[:, :], 1.0 / 15.0, ALU.mult)
        k.act(GI[:, :], GI[:, :], AF.Tanh, bias=bi[:, 0:1], scale=1.0 / 15.0)
        k.ts("dve", GI[:, :], GI[:, :], 15.0, ALU.mult)
        k.act(GF[:, :], GF[:, :], AF.Tanh, bias=bi[:, 1:2], scale=1.0 / 15.0)
        k.act(GF[:, :], GF[:, :], AF.Exp, scale=-15.0)
        k.ts("dve", GF[:, :], GF[:, :], 1.0, ALU.add)
        k.act(GF[:, :], GF[:, :], AF.Ln)
        k.ts("dve", GF[:, :], GF[:, :], -1.0, ALU.mult)
        k.scan(Fp[:, 0:TL], ones8[:, 0:TL], GF[:, 0:TL], 0.0, ALU.mult, ALU.add)
        k.scan(Fp[:, TL:T], ones8[:, 0:TC], GF[:, TL:T], 0.0, ALU.mult, ALU.add)
        k.copy("dve", sm[:, 0:1], Fp[:, TL - 1:TL])
        k.copy("dve", sm[:, 1:2], Fp[:, T - 1:T])
        k.stt("dve", sm[:, 2:3], sm[:, 0:1], dc[:, 1:2], sm[:, 1:2], ALU.mult, ALU.add)
        k.tt("dve", sm[:, 3:4], sm[:, 1:2], dc[:, 1:2], ALU.mult)
        k.ts("dve", Fp[:, :], Fp[:, :], dc[:, 0:1], ALU.mult)
        k.stt("dve", Fp[:, :], GF[:, :], dc[:, 1:2], Fp[:, :], ALU.mult, ALU.add)
        k.ts("dve", Fp[:, 0:TL], Fp[:, 0:TL], sm[:, 2:3], ALU.add)
        k.ts("dve", Fp[:, TL:T], Fp[:, TL:T], sm[:, 3:4], ALU.add)
        k.tt("dve", GI[:, :], GI[:, :], Fp[:, :], ALU.subtract)
        k.reduce(cm[:, 0:8], GI[:, 0:TL].rearrange("p (c i) -> p c i", i=512), ALU.max)
        k.reduce(cm[:, 8:9], GI[:, TL:T].rearrange("p (c i) -> p c i", i=TC), ALU.max)
        k.tt("dve", pmx[:, 0:1], cm[:, 0:1], cm[:, 8:9], ALU.max)
        for c in range(1, 8):
            k.tt("dve", pmx[:, c:c + 1], pmx[:, c - 1:c], cm[:, c:c + 1], ALU.max)
        k.tt("dve", smx[:, 7:8], cm[:, 7:8], cm[:, 8:9], ALU.max)
        for c in range(6, -1, -1):
            k.tt("dve", smx[:, c:c + 1], smx[:, c + 1:c + 2], cm[:, c:c + 1], ALU.max)
        k.tt("dve", smx[:, :], smx[:, :], pmx[:, :], ALU.subtract)
        k.stt("dve", Rt[:, 0:8], smx[:, :], dc[:, 1:2], pmx[:, :], ALU.mult, ALU.add)
        k.copy("dve", Rt[:, 8:9], cm[:, 8:9])
        pa_ = k.bank()
        pf_ = k.bank()
        for j in range(34):
            k.tr(pa_[:, 8 * j:8 * j + 8], GI[0:8, j * 128:(j + 1) * 128], ident[0:8, 0:8])
            k.tr(pf_[:, 8 * j:8 * j + 8], Fp[0:8, j * 128:(j + 1) * 128], ident[0:8, 0:8])
        k.copy("act", acol[:, :, :], pa_[:, 0:272].rearrange("p (j c) -> p j c", c=8))
        k.copy("act", Fcol[:, :, :], pf_[:, 0:272].rearrange("p (j c) -> p j c", c=8))
        pr_ = k.bank()
        for c in range(8):
            k.mm(pr_[:, 9 * c:9 * c + 9], sel8[:, c, :], Rt[:, :])
        k.copy("act", Rb[:, :, :], pr_[:, 0:72].rearrange("p (c i) -> p c i", i=9))
        k.ts("dve", nRb[:, :, :], Rb[:, :, :], -1.0, ALU.mult)
        k.release(m1)
        Et = k.sb([128, 34, 9], F32, "Et")
        Wb = [k.sb([128, 512], BF16, "Wb") for _ in range(4)]
        thr = k.sb([128, 4], F32, "thr")
        den = k.sb([128, 4], F32, "den")
        hd = k.sb([128, 64], F32, "hd")
        for d in range(2):
            for h in range(4):
                c = d * 4 + h
                for J in range(34):
                    k.ts("dve", Et[:, J, :], nRb[:, c, :], acol[:, J, c:c + 1], ALU.add)
                k.ts("dve", Et[:, :, :], Et[:, :, :], 0.0, ALU.min)
                k.act(Et[:, :, :], Et[:, :, :], AF.Exp)
                for I_, (t0, n) in enumerate(CH):
                    nq = n // 128
                    tb0 = t0 // 128
                    if t0 >= TL:
                        keys = [(32, 0), (33, 1)]
                    elif d == 0:
                        keys = [(32, None), (33, None)] + [(J, None) for J in range(0, 4 * I_)] + [(4 * I_ + r, r) for r in range(4)]
                    else:
                        keys = [(32, None), (33, None)] + [(J, None) for J in range(4 * I_ + 4, 32)] + [(4 * I_ + r, r) for r in range(4)]
                    def contributes(r, qi):
                        if r is None:
                            return True
                        return (r <= qi) if d == 0 else (r >= qi)
                    lists = {qi: [ix for ix, (J, r) in enumerate(keys) if contributes(r, qi)] for qi in range(nq)}
                    po = k.acc()
                    AHEAD = 4
                    psl = {}

                    def issue_qk(ixx, h=h, n=n, t0=t0, keys=keys, psl=psl):
                        b_ = k.bank()
                        J_ = keys[ixx][0]
                        k.mm(b_[:, 0:n], qkb[0:32, 4 + h, J_ * 128:(J_ + 1) * 128], qkb[0:32, h, t0:t0 + n])
                        psl[ixx] = b_
                    for ixx in range(min(AHEAD, len(keys))):
                        issue_qk(ixx)
                    for ix, (J, r) in enumerate(keys):
                        ps_ = psl.pop(ix)
                        if ix + AHEAD < len(keys):
                            issue_qk(ix + AHEAD)
                        w_ = Wb[ix % 4]
                        if r is None:
                            k.ts("dve", w_[:, 0:n], ps_[:, 0:n], Et[:, J, I_:I_ + 1], ALU.mult)
                        else:
                            k.stt("dve", w_[:, 0:n], ps_[:, 0:n], Et[:, J, I_:I_ + 1], lmask[:, d, r, 0:n], ALU.mult, ALU.mult)
                        for qi in range(nq):
                            if ix in lists[qi]:
                                k.mm(po[:, qi * 65:(qi + 1) * 65], w_[:, qi * 128:(qi + 1) * 128], VL[:, J, h, :],
                                     start=(ix == lists[qi][0]), stop=(ix == lists[qi][-1]))
                    pov = po[:, 0:nq * 65].rearrange("p (q c) -> p q c", c=65)
                    k.act(thr[:, 0:nq], Fcol[:, tb0:tb0 + nq, c], AF.Exp, bias=nRb[:, c, I_:I_ + 1], scale=-1.0)
                    k.act(den[:, 0:nq], pov[:, :, 64], AF.Abs)
                    k.tt("dve", den[:, 0:nq], den[:, 0:nq], thr[:, 0:nq], ALU.max)
                    k.recip(den[:, 0:nq], den[:, 0:nq])
                    for qi in range(nq):
                        if d == 0:
                            k.ts("dve", Hs[:, tb0 + qi, h * 64:(h + 1) * 64], pov[:, qi, 0:64], den[:, qi:qi + 1], ALU.mult)
                        else:
                            k.stt("dve", Hs[:, tb0 + qi, h * 64:(h + 1) * 64], pov[:, qi, 0:64], den[:, qi:qi + 1],
                                  Hs[:, tb0 + qi, h * 64:(h + 1) * 64], ALU.mult, ALU.add)
        ng = k.sb([128, 2], F32, "ng")
        k.dma(ng[:, :], I["mlstm_norm_g"][li].rearrange("(f p) -> p f", p=128))
        ssq = k.sb([128, 4], F32, "ssq")
        junk = k.sb([128, 64], F32, "junk")
        hn = [k.sb([128, 256], F32, "hn") for _ in range(2)]
        oT = [k.sb([128, 2, 512], F32, "oT") for _ in range(2)]
        hT = [k.sb([128, 2, 512], F32, "hT") for _ in range(2)]
        hb = [k.sb([128, 2, 512], BF16, "hb") for _ in range(2)]
        for ci, (t0, n) in enumerate(CH):
            o_ = oT[ci % 2]
            k.dma(o_[:, :, 0:n], zT[B0 + 512:B0 + 768, t0:t0 + n].rearrange("(f p) t -> p f t", p=128))
            k.act(o_[:, :, 0:n], o_[:, :, 0:n], AF.Sigmoid)
            ht = hT[ci % 2]
            for j in range(n // 128):
                tix = t0 // 128 + j
                for h in range(4):
                    k.act(junk[:, :], Hs[:, tix, h * 64:(h + 1) * 64], AF.Square, accum=ssq[:, h:h + 1])
                k.act(ssq[:, :], ssq[:, :], AF.Sqrt, bias=epsc[:, 0:1], scale=1.0 / 64)
                k.recip(ssq[:, :], ssq[:, :])
                hn_ = hn[j % 2]
                k.tt("dve", hn_[:, :].rearrange("p (h c) -> p h c", c=64), Hs[:, tix, :].rearrange("p (h c) -> p h c", c=64),
                     ssq[:, :].unsqueeze(2).to_broadcast([128, 4, 64]), ALU.mult)
                pb = k.bank()
                for f in range(2):
                    k.tr(pb[:, f * 128:(f + 1) * 128], hn_[:, f * 128:(f + 1) * 128], ident[:, :])
                for f in range(2):
                    k.ts("dve", ht[:, f, j * 128:(j + 1) * 128], pb[:, f * 128:(f + 1) * 128], ng[:, f:f + 1], ALU.mult)
            k.tt("pool", hb[ci % 2][:, :, 0:n], ht[:, :, 0:n], o_[:, :, 0:n], ALU.mult)
            k.dma(mixT[768:1024, t0:t0 + n].rearrange("(f p) t -> p f t", p=128), hb[ci % 2][:, :, 0:n], q="pool")

    PH = dict(A=phase_A, M=phase_M, NP=phase_NP, O=phase_O, F=phase_F, MLA=phase_MLA, RW=phase_RW, ML=phase_ML)
    return nc, k, PH, base_mark, (xTa, xTb)


def finish_build(nc, k, PH, base_mark, xs, n_layers=DEPTH, enabled=("RW", "MLA", "ML"), tail=True):
    xTa, xTb = xs
    PH["A"]()
    k.release(base_mark)
    for li in range(n_layers):
        ctx_out = li < DEPTH - 1
        PH["M"](li)
        k.release(base_mark)
        PH["NP"](li, xTa)
        k.release(base_mark)
        if "RW" in enabled:
            PH["RW"](li)
            k.release(base_mark)
        if "MLA" in enabled:
            PH["MLA"](li, ctx_out)
            k.release(base_mark)
        if "ML" in enabled:
            PH["ML"](li)
            k.release(base_mark)
        if tail:
            PH["O"](li, xTa, xTb)
            k.release(base_mark)
            PH["F"](li, xTb, xTa, li == DEPTH - 1)
            k.release(base_mark)
    k.S.barrier()
    k.S.emit()
    return nc


def make_in_maps(inputs, consts, cores):
    maps = []
    for b in cores:
        m = {"x": np.ascontiguousarray(inputs["x"][b]), "ctx": np.ascontiguousarray(inputs["ctx"][b]),
             "c": np.ascontiguousarray(inputs["c"][b]), "c_ctx": np.ascontiguousarray(inputs["c_ctx"])}
        for p in PARAMS:
            m[p] = np.ascontiguousarray(inputs[p])
        m.update(consts)
        maps.append(m)
    return maps


def all_shapes(inputs, consts):
    shp = {"x": (TL, D), "ctx": (TC, D), "c": (D,), "c_ctx": (D,)}
    for p in PARAMS:
        shp[p] = tuple(inputs[p].shape)
    for kk, v in consts.items():
        shp[kk] = tuple(v.shape)
    return shp


def kernel(**inputs):
    inputs = {kk: np.asarray(v, dtype=np.float32) for kk, v in inputs.items()}
    consts = host_consts()
    nc, k, PH, bm, xs = build(all_shapes(inputs, consts))
    finish_build(nc, k, PH, bm, xs)
    cores = list(range(8))
    res = run_bass_kernel_spmd(nc, make_in_maps(inputs, consts, cores), core_ids=cores)
    return np.stack([np.asarray(res.results[b]["out"], dtype=np.float32) for b in cores], axis=0)
```

```python
import contextlib
import numpy as np
import ml_dtypes
import concourse.bass as bass
import concourse.mybir as mybir
from concourse.bass_utils import run_bass_kernel_spmd

F32 = mybir.dt.float32
BF16 = mybir.dt.bfloat16
ALU = mybir.AluOpType
AF = mybir.ActivationFunctionType

D = 1024
TL = 4096
TC = 256
T = TL + TC
DEPTH = 2
IN_COLS = 2736
FFN_H = 2816
CH = [(i * 512, 512) for i in range(8)] + [(TL, TC)]
NSLOT = 24
EPS = 1e-6
SB_BASE = 16640
SB_TOP = 229376 - 2048


class Sched:
    ENGS = ("pe", "act", "dve", "pool", "sp")

    def __init__(self, nc):
        self.nc = nc
        self.ops = {e: [] for e in self.ENGS}
        self.count = {e: 0 for e in self.ENGS}
        self.known = {e: {} for e in self.ENGS}
        self.sems = {}
        self.wr = {}
        self.rd = {}
        self.slot_n = 0
        self.slot_sp = 0
        self.slot_pl = 0
        self.slot_cnt = [0] * NSLOT
        self.n_ops = 0
        self.limit = None
        self.log = None
        self.capture = None

    def _deps(self, reads, writes):
        d = {}
        for b in reads:
            w = self.wr.get(b)
            if w and w[1] > d.get(w[0], 0):
                d[w[0]] = w[1]
        for b in writes:
            w = self.wr.get(b)
            if w and w[1] > d.get(w[0], 0):
                d[w[0]] = w[1]
            for s, v in self.rd.get(b, {}).items():
                if v > d.get(s, 0):
                    d[s] = v
        return d

    def _mark(self, reads, writes, src, val):
        for b in reads:
            self.rd.setdefault(b, {})[src] = val
        for b in writes:
            self.wr[b] = (src, val)
            self.rd[b] = {}

    def _waits(self, eng, deps):
        ws = []
        kn = self.known[eng]
        for src, val in deps.items():
            if kn.get(src, 0) >= val:
                continue
            if src == eng == "pe":
                continue
            kn[src] = val
            ws.append((src, val))
        return ws

    def op(self, eng, fn, reads=(), writes=()):
        if self.capture is not None:
            self.capture.append(("op", eng, fn, tuple(reads), tuple(writes)))
            return
        if self.limit is not None and self.n_ops >= self.limit:
            return
        if self.log is not None:
            import sys as _s
            self.log.append((self.n_ops, eng, _s._getframe(2).f_lineno))
        ws = self._waits(eng, self._deps(reads, writes))
        self.count[eng] += 1
        val = self.count[eng]
        self.ops[eng].append((ws, fn, (eng, 1)))
        self._mark(reads, writes, eng, val)
        self.n_ops += 1

    def dma(self, fn, reads=(), writes=(), q="sp"):
        if self.capture is not None:
            self.capture.append(("dma", q, fn, tuple(reads), tuple(writes)))
            return
        if self.limit is not None and self.n_ops >= self.limit:
            return
        if self.log is not None:
            import sys as _s
            self.log.append((self.n_ops, "dma-" + q, _s._getframe(2).f_lineno))
        if q == "sp":
            slot = self.slot_sp % 16
            self.slot_sp += 1
        else:
            slot = 16 + self.slot_pl % (NSLOT - 16)
            self.slot_pl += 1
        src = "q%d" % slot
        prev = self.slot_cnt[slot]
        deps = self._deps(reads, writes)
        if prev:
            deps[src] = max(deps.get(src, 0), prev)
        ws = self._waits(q, deps)
        self.slot_cnt[slot] = prev + 1
        self.ops[q].append((ws, fn, (src, 16)))
        self._mark(reads, writes, src, prev + 1)
        self.n_ops += 1

    def replay(self, items, n=None):
        cnt = 0
        while items and (n is None or cnt < n):
            kind, e, fn, rd, wr = items.pop(0)
            if kind == "op":
                self.op(e, fn, rd, wr)
            else:
                self.dma(fn, rd, wr, q=e)
            cnt += 1

    def barrier(self):
        tot = dict(self.count)
        for s in range(NSLOT):
            if self.slot_cnt[s]:
                tot["q%d" % s] = self.slot_cnt[s]
        for e in self.ENGS:
            ws = self._waits(e, {k: v for k, v in tot.items() if v > 0 and k != e})
            if ws:
                self.ops[e].append((ws, None, None))
        self.wr = {}
        self.rd = {}

    def emit(self):
        nc = self.nc
        EP = 12000
        with contextlib.ExitStack() as st:
            for e in self.ENGS:
                ne = max(1, (self.count[e] + EP - 1) // EP)
                self.sems[e] = [st.enter_context(nc.semaphore("s_%s%d" % (e, i))) for i in range(ne)]
            for s in range(NSLOT):
                self.sems["q%d" % s] = st.enter_context(nc.semaphore("s_q%d" % s))
            block = st.enter_context(nc.Block())
            sems = self.sems

            def run(engname):
                def body(e):
                    n_done = 0
                    for ws, fn, inc in self.ops[engname]:
                        for src, val in ws:
                            if src[0] == "q":
                                e.wait_ge(sems[src], val * 16)
                            else:
                                e.wait_ge(sems[src][(val - 1) // EP], (val - 1) % EP + 1)
                        if fn is not None:
                            if inc[0][0] == "q":
                                fn(e).then_inc(sems[inc[0]], 16)
                            else:
                                fn(e).then_inc(sems[engname][n_done // EP], 1)
                                n_done += 1
                return body
            block.tensor(run("pe"))
            block.scalar(run("act"))
            block.vector(run("dve"))
            block.gpsimd(run("pool"))
            block.sync(run("sp"))


def _nm(aps):
    out = []
    for a in aps:
        if a is None or isinstance(a, (int, float)):
            continue
        out.append(a.name)
    return out


class KB:
    def __init__(self, nc):
        self.nc = nc
        self.S = Sched(nc)
        self.ptr = SB_BASE
        self.uid = 0
        self.ps = [nc.alloc_psum_tensor("psb%d" % i, [128, 512], F32) for i in range(8)]
        self.ps_i = 0
        self.pool = [0, 1, 2, 3, 4, 5]
        self.started = {}
        self.acc_i = 0

    def sb(self, shape, dt=F32, name="t"):
        per = 1
        for s in shape[1:]:
            per *= s
        nbytes = per * (4 if dt == F32 else 2)
        nbytes = (nbytes + 63) // 64 * 64
        self.uid += 1
        assert self.ptr + nbytes <= SB_TOP, ("SBUF overflow", name, self.ptr, nbytes)
        t = self.nc.alloc_sbuf_tensor_at("%s_%d" % (name, self.uid), list(shape), dt, offset=self.ptr)
        self.ptr += nbytes
        return t

    def mark(self):
        return self.ptr

    def release(self, mark):
        self.S.barrier()
        self.ptr = mark

    def bank(self):
        b = self.ps[self.pool[self.ps_i % len(self.pool)]]
        self.ps_i += 1
        self.started[b[:, :].name] = set()
        return b

    def acc(self):
        b = self.ps[6 + self.acc_i % 2]
        self.acc_i += 1
        self.started[b[:, :].name] = set()
        return b

    def dma(self, out, in_, q="sp", cast=False):
        if cast:
            q = "pool"
        self.S.dma(lambda e: e.dma_start(out=out, in_=in_, allow_slow_non_contiguous=True), reads=_nm([in_]), writes=_nm([out]), q=q)

    def mm(self, out, lhsT, rhs, start=True, stop=True):
        p0 = out.base_partition()
        p1 = p0 + out.shape[0]
        quads = set(range(p0 // 32, (p1 - 1) // 32 + 1))
        st = self.started[out.name]
        fresh = quads - st
        assert len(fresh) == 0 or len(fresh) == len(quads), ("mixed quadrant start", out.name, quads, st)
        hw_start = len(fresh) > 0
        st |= quads
        self.S.op("pe", lambda e: e.matmul(out, lhsT=lhsT, rhs=rhs, start=hw_start, stop=stop, skip_group_check=True),
                  reads=_nm([lhsT, rhs]) + ([] if start else _nm([out])), writes=_nm([out]))

    def tr(self, out, in_, ident):
        self.S.op("pe", lambda e: e.transpose(out, in_, ident), reads=_nm([in_, ident]), writes=_nm([out]))

    def act(self, out, in_, func, bias=0.0, scale=1.0, accum=None):
        kw = {}
        if accum is not None:
            kw["accum_out"] = accum
        self.S.op("act", lambda e: e.activation(out=out, in_=in_, func=func, bias=bias, scale=scale, **kw),
                  reads=_nm([in_, bias, scale]), writes=_nm([out, accum]))

    def copy(self, eng, out, in_):
        if eng == "act":
            self.S.op("act", lambda e: e.copy(out=out, in_=in_), reads=_nm([in_]), writes=_nm([out]))
        else:
            self.S.op(eng, lambda e: e.tensor_copy(out=out, in_=in_), reads=_nm([in_]), writes=_nm([out]))

    def tt(self, eng, out, in0, in1, op):
        self.S.op(eng, lambda e: e.tensor_tensor(out=out, in0=in0, in1=in1, op=op),
                  reads=_nm([in0, in1]), writes=_nm([out]))

    def ts(self, eng, out, in0, s1, op0, s2=None, op1=None):
        if op1 is None:
            self.S.op(eng, lambda e: e.tensor_scalar(out=out, in0=in0, scalar1=s1, scalar2=None, op0=op0),
                      reads=_nm([in0, s1]), writes=_nm([out]))
        else:
            self.S.op(eng, lambda e: e.tensor_scalar(out=out, in0=in0, scalar1=s1, scalar2=s2, op0=op0, op1=op1),
                      reads=_nm([in0, s1, s2]), writes=_nm([out]))

    def stt(self, eng, out, in0, scalar, in1, op0, op1):
        self.S.op(eng, lambda e: e.scalar_tensor_tensor(out=out, in0=in0, scalar=scalar, in1=in1, op0=op0, op1=op1),
                  reads=_nm([in0, scalar, in1]), writes=_nm([out]))

    def memset(self, eng, out, val):
        self.S.op(eng, lambda e: e.memset(out, val), writes=_nm([out]))

    def recip(self, out, in_):
        self.S.op("dve", lambda e: e.reciprocal(out=out, in_=in_), reads=_nm([in_]), writes=_nm([out]))

    def scan(self, out, d0, d1, init, op0, op1):
        self.S.op("dve", lambda e: e.tensor_tensor_scan(out=out, data0=d0, data1=d1, initial=init, op0=op0, op1=op1),
                  reads=_nm([d0, d1, init]), writes=_nm([out]))

    def reduce(self, out, in_, op):
        self.S.op("dve", lambda e: e.tensor_reduce(out=out, in_=in_, axis=mybir.AxisListType.X, op=op),
                  reads=_nm([in_]), writes=_nm([out]))


def host_consts():
    c = {}
    c["ident"] = np.eye(128, dtype=np.float32)
    bo = np.zeros((128, 128), np.float32)
    bo[:64, :64] = 1.0
    bo[64:, 64:] = 1.0
    c["blockones"] = bo
    j = np.arange(64)[:, None]
    i = np.arange(64)[None, :]
    m = np.zeros((64, 2, 256), np.float32)
    for d in range(2):
        strict = (j < i) if d == 0 else (j > i)
        incl = (j <= i) if d == 0 else (j >= i)
        m[:, d, 0:64] = strict
        m[:, d, 64:128] = incl
        m[:, d, 128:192] = strict
        m[:, d, 192:256] = incl
    c["rmask"] = m
    m2 = np.zeros((64, 2, 64), np.float32)
    m2[:, 0, :] = (i.T > j.T)
    ii = np.arange(64)[:, None]
    jj = np.arange(64)[None, :]
    m2[:, 0, :] = (jj < ii)
    m2[:, 1, :] = (jj > ii)
    c["nmask"] = m2
    rows = TL // 64
    row = np.repeat(np.arange(rows, dtype=np.float32), 64)
    col = np.tile(np.arange(64, dtype=np.float32), rows)
    nf = 8
    inv = np.power(np.float32(10000.0), -np.arange(nf, dtype=np.float32) / nf).astype(np.float32)
    ang = np.concatenate([row[:, None] * inv, col[:, None] * inv], axis=-1).astype(np.float32)
    cosf = np.ones((96, T), np.float32)
    sinf = np.zeros((96, T), np.float32)
    cosf[64:80, :TL] = np.cos(ang).T
    cosf[80:96, :TL] = np.cos(ang).T
    sinf[64:80, :TL] = np.sin(ang).T
    sinf[80:96, :TL] = np.sin(ang).T
    c["cosf"] = cosf
    c["sinf"] = sinf
    rm = np.zeros((96, 96), np.float32)
    for t in range(16):
        rm[80 + t, 64 + t] = -1.0
        rm[64 + t, 80 + t] = 1.0
    c["rotm"] = rm
    jj = np.arange(128)[:, None]
    ii = np.arange(512)[None, :]
    mm = np.zeros((128, 2, 4, 512), np.float32)
    for r in range(4):
        mm[:, 0, r, :] = (jj + 128 * r) <= ii
        mm[:, 1, r, :] = (jj + 128 * r) >= ii
    c["lmask"] = mm
    sel = np.zeros((8, 8, 128), np.float32)
    for k in range(8):
        sel[k, k, :] = 1.0
    c["sel8"] = sel
    dcol = np.zeros((8, 2), np.float32)
    dcol[0:4, 0] = 1.0
    dcol[4:8, 0] = -1.0
    dcol[4:8, 1] = 1.0
    c["dircols"] = dcol
    return c


PARAMS = ["mod_w", "mod_b", "norm1_g", "norm2_g", "w_in", "w_out", "ffn_w_in", "ffn_w_out",
          "rwkv_mu", "rwkv_w0", "rwkv_w2", "rwkv_a0", "rwkv_a2", "rwkv_g2", "rwkv_k_k", "rwkv_k_a", "rwkv_r_k",
          "rwkv_ln_g", "rwkv_ln_b", "mla_q_norm_g", "mla_w_uq", "mla_kv_norm_g", "mla_w_ukv",
          "mla_q_qknorm_g", "mla_k_qknorm_g", "mlstm_conv_w", "mlstm_conv_b", "mlstm_i_b", "mlstm_f_b",
          "mlstm_norm_g"]


def build(shapes, dbg=(), stop_after=None, layers=DEPTH):
    nc = bass.Bass("TRN2", target_bir_lowering=False)
    I = {}
    for name, shp in shapes.items():
        I[name] = nc.dram_tensor(name, list(shp), F32, kind="ExternalInput").ap()

    _scr = {}

    def scratch(name, shape, dt=F32):
        if name not in _scr:
            kind = "ExternalOutput" if name in dbg else "Internal"
            _scr[name] = nc.dram_tensor(name, list(shape), dt, kind=kind).ap()
        return _scr[name]

    out = nc.dram_tensor("out", [TL, D], F32, kind="ExternalOutput").ap()
    xTa = scratch("xTa", [D, T])
    xTb = scratch("xTb", [D, T])
    zT = scratch("zT", [IN_COLS, T])
    mixT = scratch("mixT", [D, T], BF16)
    ydir = scratch("ydir", [2, 256, T])
    rkvg = scratch("rkvg", [4, 256, T])
    k = KB(nc)
    S = k.S

    ident = k.sb([128, 128], F32, "ident")
    identb = k.sb([128, 128], BF16, "identb")
    bones = k.sb([128, 128], F32, "bones")
    onesb = k.sb([128, 128], BF16, "onesb")
    onesf = k.sb([128, 128], F32, "onesf")
    MOD = k.sb([128, 48, 2], F32, "MOD")
    A1 = k.sb([128, 8, 2], F32, "A1")
    A2 = k.sb([128, 8, 2], F32, "A2")
    epsc = k.sb([128, 1], F32, "epsc")
    k.dma(ident[:, :], I["ident"][:, :])
    k.dma(bones[:, :], I["blockones"][:, :])
    k.copy("dve", identb[:, :], ident[:, :])
    k.memset("dve", onesb[:, :], 1.0)
    k.memset("dve", onesf[:, :], 1.0)
    k.memset("dve", epsc[:, :], EPS)
    base_mark = k.mark()

    def xview(ap):
        return ap.rearrange("(f p) t -> p f t", p=128)

    def phase_A():
        xin = [k.sb([128, D], F32, "xin") for _ in range(2)]
        stage = [k.sb([128, 8, 512], F32, "stg") for _ in range(2)]
        n_t = 0
        for ci, (t0, n) in enumerate(CH):
            st = stage[ci % 2]
            for j in range(n // 128):
                xi = xin[n_t % 2]
                n_t += 1
                if t0 < TL:
                    k.dma(xi[:, :], I["x"][t0 + j * 128:t0 + (j + 1) * 128, :])
                else:
                    k.dma(xi[:, :], I["ctx"][j * 128:(j + 1) * 128, :])
                for half in range(2):
                    pb = k.bank()
                    for f in range(4):
                        k.tr(pb[:, f * 128:(f + 1) * 128], xi[:, (half * 4 + f) * 128:(half * 4 + f + 1) * 128], ident[:, :])
                    k.copy("act" if half == 0 else "dve", st[:, half * 4:half * 4 + 4, j * 128:(j + 1) * 128],
                           pb[:, :].rearrange("p (a b) -> p a b", a=4))
            k.dma(xview(xTa)[:, :, t0:t0 + n], st[:, :, 0:n], q="pool")

    def phase_M(li):
        cc = k.sb([128, 8, 2], F32, "cc")
        modb = k.sb([128, 48], F32, "modb")
        g1 = k.sb([128, 8], F32, "g1")
        g2 = k.sb([128, 8], F32, "g2")
        wst = [k.sb([128, 6144], F32, "wst") for _ in range(2)]
        tmp = k.sb([128, 8, 2], F32, "tmpm")
        with nc.allow_non_contiguous_dma(reason="tiny column-layout parameter loads"):
            k.dma(cc[:, :, 0], I["c"].rearrange("(f p) -> p f", p=128))
            k.dma(cc[:, :, 1], I["c_ctx"].rearrange("(f p) -> p f", p=128))
            k.dma(modb[:, :], I["mod_b"][li].rearrange("(o p) -> p o", p=128))
            k.dma(g1[:, :], I["norm1_g"][li].rearrange("(f p) -> p f", p=128))
            k.dma(g2[:, :], I["norm2_g"][li].rearrange("(f p) -> p f", p=128))
        k.act(cc[:, :, :], cc[:, :, :], AF.Silu)
        pb = k.bank()
        for kt in range(8):
            w = wst[kt % 2]
            k.dma(w[:, :], I["mod_w"][li, kt * 128:(kt + 1) * 128, :])
            for o in range(48):
                k.mm(pb[:, 2 * o:2 * o + 2], w[:, o * 128:(o + 1) * 128], cc[:, kt, :], start=(kt == 0), stop=(kt == 7))
        k.tt("dve", MOD[:, :, :], pb[:, 0:96].rearrange("p (o s) -> p o s", s=2),
             modb[:, :].unsqueeze(2).to_broadcast([128, 48, 2]), ALU.add)
        k.ts("dve", tmp[:, :, :], MOD[:, 8:16, :], 1.0, ALU.add)
        k.tt("dve", A1[:, :, :], tmp[:, :, :], g1[:, :].unsqueeze(2).to_broadcast([128, 8, 2]), ALU.mult)
        k.ts("dve", tmp[:, :, :], MOD[:, 32:40, :], 1.0, ALU.add)
        k.tt("dve", A2[:, :, :], tmp[:, :, :], g2[:, :].unsqueeze(2).to_broadcast([128, 8, 2]), ALU.mult)

    def norm_chunk(xc, n, s, A, sh_off, xm_out, sq, rstd, tmp):
        k.act(sq[:, 0:8, 0:n], xc[:, :, 0:n], AF.Square)
        pb = k.bank()
        for f in range(8):
            k.mm(pb[:, 0:n], onesb[:, :], sq[:, f, 0:n], start=(f == 0), stop=(f == 7))
        k.act(rstd[:, 0:n], pb[:, 0:n], AF.Sqrt, bias=epsc[:, 0:1], scale=1.0 / D)
        k.recip(rstd[:, 0:n], rstd[:, 0:n])
        for f in range(8):
            k.tt("dve" if f % 2 == 0 else "pool", tmp[:, f % 2, 0:n], xc[:, f, 0:n], rstd[:, 0:n], ALU.mult)
            k.act(xm_out[:, f, :], tmp[:, f % 2, 0:n], AF.Identity, bias=MOD[:, sh_off + f, s:s + 1], scale=A[:, f, s:s + 1])

    def phase_NP(li, xsrc):
        xm = k.sb([128, 8, T], BF16, "xm")
        wbf = k.sb([128, 8, IN_COLS], BF16, "wbf")
        k.dma(wbf[:, :, :], I["w_in"][li].rearrange("(f p) c -> p f c", p=128), cast=True)
        m1 = k.mark()
        xc = [k.sb([128, 8, 512], F32, "xc") for _ in range(2)]
        sq = k.sb([128, 8, 512], BF16, "sq")
        rstd = k.sb([128, 512], F32, "rstd")
        tmp = k.sb([128, 2, 512], F32, "tmpn")
        for ci, (t0, n) in enumerate(CH):
            x_ = xc[ci % 2]
            k.dma(x_[:, :, 0:n], xview(xsrc)[:, :, t0:t0 + n])
            norm_chunk(x_, n, 0 if t0 < TL else 1, A1, 0, xm[:, :, t0:t0 + n], sq, rstd, tmp)
        k.release(m1)
        zrow = [k.sb([128, T], F32, "zrow") for _ in range(2)]
        nct = (IN_COLS + 127) // 128
        ev = 0
        for ct in range(nct):
            c0 = ct * 128
            cw = min(128, IN_COLS - c0)
            zr = zrow[ct % 2]
            for (t0, n) in CH:
                pb = k.bank()
                for f in range(8):
                    k.mm(pb[0:cw, 0:n], wbf[:, f, c0:c0 + cw], xm[:, f, t0:t0 + n], start=(f == 0), stop=(f == 7))
                k.copy("act" if ev % 2 == 0 else "dve", zr[0:cw, t0:t0 + n], pb[0:cw, 0:n])
                ev += 1
            k.dma(zT[c0:c0 + cw, :], zr[0:cw, :], q="pool")

    def phase_O(li, xsrc, xdst):
        wo = k.sb([128, 8, D], BF16, "wo")
        k.dma(wo[:, :, :], I["w_out"][li].rearrange("(f p) c -> p f c", p=128), cast=True)
        mx = [k.sb([128, 8, 512], BF16, "mx") for _ in range(2)]
        xc = [k.sb([128, 8, 512], F32, "xco") for _ in range(2)]
        xo = [k.sb([128, 8, 512], F32, "xoo") for _ in range(2)]
        for ci, (t0, n) in enumerate(CH):
            s = 0 if t0 < TL else 1
            m_ = mx[ci % 2]
            x_ = xc[ci % 2]
            o_ = xo[ci % 2]
            k.dma(m_[:, :, 0:n], xview(mixT)[:, :, t0:t0 + n])
            k.dma(x_[:, :, 0:n], xview(xsrc)[:, :, t0:t0 + n])
            for of in range(8):
                pb = k.bank()
                for f in range(8):
                    k.mm(pb[:, 0:n], wo[:, f, of * 128:(of + 1) * 128], m_[:, f, 0:n], start=(f == 0), stop=(f == 7))
                k.stt("dve", o_[:, of, 0:n], pb[:, 0:n], MOD[:, 16 + of, s:s + 1], x_[:, of, 0:n], ALU.mult, ALU.add)
            k.dma(xview(xdst)[:, :, t0:t0 + n], o_[:, :, 0:n], q="pool")

    def phase_F(li, xsrc, xdst, final):
        w1 = k.sb([128, 8, 2 * FFN_H], BF16, "w1")
        w2 = k.sb([128, 22, D], BF16, "w2")
        k.dma(w1[:, :, :], I["ffn_w_in"][li].rearrange("(f p) c -> p f c", p=128), cast=True)
        k.dma(w2[:, :, :], I["ffn_w_out"][li].rearrange("(f p) c -> p f c", p=128), cast=True)
        xc = [k.sb([128, 8, 512], F32, "xcf")] * 2
        rstd = k.sb([128, 512], F32, "rstdf")
        tmp = k.sb([128, 2, 512], F32, "tmpf")
        xm = k.sb([128, 8, 512], BF16, "xmf")
        actT = k.sb([128, 22, 512], BF16, "actT")
        sq = actT
        sg = [tmp[:, 0, :], tmp[:, 1, :]]
        chunks = CH if not final else CH[:8]
        otb = [k.sb([128, D], F32, "otb") for _ in range(2)] if final else None
        for ci, (t0, n) in enumerate(chunks):
            s = 0 if t0 < TL else 1
            x_ = xc[ci % 2]
            k.dma(x_[:, :, 0:n], xview(xsrc)[:, :, t0:t0 + n])
            norm_chunk(x_, n, s, A2, 24, xm[:, :, 0:n], sq, rstd, tmp)
            for h in range(22):
                pg = k.bank()
                pu = k.bank()
                for f in range(8):
                    k.mm(pg[:, 0:n], w1[:, f, h * 128:(h + 1) * 128], xm[:, f, 0:n], start=(f == 0), stop=(f == 7))
                for f in range(8):
                    k.mm(pu[:, 0:n], w1[:, f, FFN_H + h * 128:FFN_H + (h + 1) * 128], xm[:, f, 0:n], start=(f == 0), stop=(f == 7))
                s_ = sg[h % 2]
                k.act(s_[:, 0:n], pg[:, 0:n], AF.Silu)
                k.tt("dve", actT[:, h, 0:n], s_[:, 0:n], pu[:, 0:n], ALU.mult)
            for of in range(8):
                pb = k.bank()
                for h in range(22):
                    k.mm(pb[:, 0:n], w2[:, h, of * 128:(of + 1) * 128], actT[:, h, 0:n], start=(h == 0), stop=(h == 21))
                k.stt("dve", x_[:, of, 0:n], pb[:, 0:n], MOD[:, 40 + of, s:s + 1], x_[:, of, 0:n], ALU.mult, ALU.add)
            if not final:
                k.dma(xview(xdst)[:, :, t0:t0 + n], x_[:, :, 0:n], q="pool")
            else:
                for j in range(n // 128):
                    ot = otb[j % 2]
                    for half in range(2):
                        pb = k.bank()
                        for f in range(4):
                            k.tr(pb[:, f * 128:(f + 1) * 128], x_[:, half * 4 + f, j * 128:(j + 1) * 128], ident[:, :])
                        k.copy("act" if half == 0 else "dve", ot[:, half * 512:(half + 1) * 512], pb[:, :])
                    k.dma(out[t0 + j * 128:t0 + (j + 1) * 128, :], ot[:, :], q="pool")


    def phase_MLA(li, ctx_out):
        wuq = k.sb([128, 4, 768], BF16, "wuq")
        wukv = k.sb([128, 2, 1024], BF16, "wukv")
        k.dma(wuq[:, :, :], I["mla_w_uq"][li].rearrange("(f p) c -> p f c", p=128), cast=True)
        k.dma(wukv[:, :, :], I["mla_w_ukv"][li].rearrange("(f p) c -> p f c", p=128), cast=True)
        gq = k.sb([128, 4], F32, "gq")
        gkv = k.sb([128, 2], F32, "gkv")
        gqk = k.sb([96, 2], F32, "gqk")
        rotm = k.sb([96, 96], F32, "rotm")
        with nc.allow_non_contiguous_dma(reason="tiny column-layout parameter loads"):
            k.dma(gq[:, :], I["mla_q_norm_g"][li].rearrange("(f p) -> p f", p=128))
            k.dma(gkv[:, :], I["mla_kv_norm_g"][li].rearrange("(f p) -> p f", p=128))
            k.dma(gqk[:, 0:1], I["mla_q_qknorm_g"][li].rearrange("(p o) -> p o", o=1))
            k.dma(gqk[:, 1:2], I["mla_k_qknorm_g"][li].rearrange("(p o) -> p o", o=1))
        k.dma(rotm[:, :], I["rotm"][:, :])
        k.ts("dve", gqk[:, 0:1], gqk[:, 0:1], float(96 ** -0.5), ALU.mult)
        QTd = scratch("QTd", [96, 8, T], BF16)
        qst = [k.sb([96, 512], BF16, "qst") for _ in range(2)]
        KT = k.sb([96, 8, T], BF16, "KT")
        V = k.sb([128, 34, 8, 65], BF16, "V")
        k.memset("pool", V[:, :, :, 64:65], 1.0)
        m1 = k.mark()
        cq = k.sb([128, 4, 512], F32, "cq")
        ckv = k.sb([128, 2, 512], F32, "ckv")
        krb = k.sb([96, 512], F32, "krb")
        cs = k.sb([96, 512], F32, "cs")
        sn = k.sb([96, 512], F32, "sn")
        sqb = k.sb([128, 4, 512], BF16, "sqb")
        rs = k.sb([128, 512], F32, "rs")
        cqn = k.sb([128, 4, 512], BF16, "cqn")
        ckvn = k.sb([128, 2, 512], BF16, "ckvn")
        NW = 8
        CHN = [dict(src=k.sb([96, 512], F32, "csrc"), qn=k.sb([96, 512], F32, "qn"), sq96=k.sb([96, 512], BF16, "sq96"),
                    rs96=k.sb([96, 512], F32, "rs96")) for _ in range(NW)]

        def norm_feat(src, nt, nfeat, gcol, dst, n):
            k.act(sqb[:, 0:nt, 0:n], src[:, 0:nt, 0:n], AF.Square)
            pb = k.bank()
            for f in range(nt):
                k.mm(pb[:, 0:n], onesb[:, :], sqb[:, f, 0:n], start=(f == 0), stop=(f == nt - 1))
            k.act(rs[:, 0:n], pb[:, 0:n], AF.Sqrt, bias=epsc[:, 0:1], scale=1.0 / nfeat)
            k.recip(rs[:, 0:n], rs[:, 0:n])
            for f in range(nt):
                k.stt("dve", dst[:, f, 0:n], src[:, f, 0:n], gcol[:, f:f + 1], rs[:, 0:n], ALU.mult, ALU.mult)

        zcq = zT[1152:1664, :].rearrange("(f p) t -> p f t", p=128)
        zckv = zT[1664:1920, :].rearrange("(f p) t -> p f t", p=128)
        for (t0, n) in CH:
            k.dma(cq[:, :, 0:n], zcq[:, :, t0:t0 + n])
            k.dma(ckv[:, :, 0:n], zckv[:, :, t0:t0 + n])
            k.dma(krb[64:96, 0:n], zT[1920:1952, t0:t0 + n])
            k.dma(cs[:, 0:n], I["cosf"][:, t0:t0 + n])
            k.dma(sn[:, 0:n], I["sinf"][:, t0:t0 + n])
            norm_feat(cq, 4, 512, gq, cqn, n)
            norm_feat(ckv, 2, 256, gkv, ckvn, n)
            def front(h0, base):
                wave = []
                for h in (h0, h0 + 1):
                    cq_ = CHN[base + len(wave)]
                    pq = k.bank()
                    for f in range(4):
                        k.mm(pq[0:96, 0:n], wuq[:, f, 96 * h:96 * h + 96], cqn[:, f, 0:n], start=(f == 0), stop=(f == 3))
                    k.copy("act", cq_["src"][:, 0:n], pq[0:96, 0:n])
                    wave.append((cq_, gqk[:, 0:1], qst[h % 2][:, 0:n], ("q", h)))
                    ck_ = CHN[base + len(wave)]
                    pk = k.bank()
                    for f in range(2):
                        k.mm(pk[0:64, 0:n], wukv[:, f, 128 * h:128 * h + 64], ckvn[:, f, 0:n], start=(f == 0), stop=(f == 1))
                    k.copy("act", ck_["src"][0:64, 0:n], pk[0:64, 0:n])
                    k.copy("pool", ck_["src"][64:96, 0:n], krb[64:96, 0:n])
                    wave.append((ck_, gqk[:, 1:2], KT[:, h, t0:t0 + n], ("k", h)))
                for (c_, g_, dst, tag) in wave:
                    k.act(c_["sq96"][:, 0:n], c_["src"][:, 0:n], AF.Square)
                for (c_, g_, dst, tag) in wave:
                    pb = k.bank()
                    k.mm(pb[0:96, 0:n], onesb[0:96, 0:96], c_["sq96"][0:96, 0:n])
                    k.act(c_["rs96"][:, 0:n], pb[0:96, 0:n], AF.Sqrt, bias=epsc[0:96, 0:1], scale=1.0 / 96)
                for (c_, g_, dst, tag) in wave:
                    k.recip(c_["rs96"][:, 0:n], c_["rs96"][:, 0:n])
                for (c_, g_, dst, tag) in wave:
                    k.stt("dve", c_["qn"][:, 0:n], c_["src"][:, 0:n], g_, c_["rs96"][:, 0:n], ALU.mult, ALU.mult)
                return wave

            def back(wave):
                for (c_, g_, dst, tag) in wave:
                    pr = k.bank()
                    k.mm(pr[0:96, 0:n], rotm[0:96, 0:96], c_["qn"][0:96, 0:n])
                    k.tt("dve", c_["rs96"][:, 0:n], pr[0:96, 0:n], sn[:, 0:n], ALU.mult)
                for (c_, g_, dst, tag) in wave:
                    k.tt("pool", c_["qn"][:, 0:n], c_["qn"][:, 0:n], cs[:, 0:n], ALU.mult)
                for (c_, g_, dst, tag) in wave:
                    k.tt("dve", dst, c_["qn"][:, 0:n], c_["rs96"][:, 0:n], ALU.add)
                    if tag[0] == "q":
                        k.dma(QTd[:, tag[1], t0:t0 + n], dst, q="pool")

            prev = None
            for wi in range(5):
                cur = front(2 * wi, 4 * (wi % 2)) if wi < 4 else None
                if prev is not None:
                    back(prev)
                prev = cur
            for j in range(n // 128):
                pv = k.bank()
                for f in range(2):
                    k.mm(pv[:, 0:512], ckvn[:, f, j * 128:(j + 1) * 128],
                         wukv[:, f, :].rearrange("p (h c) -> p h c", c=128)[:, :, 64:128], start=(f == 0), stop=(f == 1))
                k.copy("act", V[:, t0 // 128 + j, :, 0:64], pv[:, 0:512].rearrange("p (h c) -> p h c", c=64))
        k.release(m1)
        pT = [k.sb([128, 512], BF16, "pT") for _ in range(4)]
        Otok = [k.sb([128, 4, 512], F32, "Otok") for _ in range(2)]
        OT = [k.sb([128, 4, 512], BF16, "OT") for _ in range(2)]
        rden = k.sb([128, 8, 4], F32, "rden")
        QTc = [k.sb([96, 8, 512], BF16, "QTc") for _ in range(2)]
        mv = mixT.rearrange("(f p) t -> p f t", p=128)
        chunks = CH if ctx_out else CH[:8]
        import os as _os2
        if _os2.environ.get('SKIP_ATT'):
            chunks = []
        for ci, (t0, n) in enumerate(chunks):
            nq = n // 128
            keys = list(range(34)) if t0 < TL else [32, 33]
            ot = Otok[ci % 2]
            QT = QTc[ci % 2]
            k.dma(QT[:, :, 0:n], QTd[:, :, t0:t0 + n])
            for h in range(8):
                po = k.acc()
                AHEAD = 4
                psl = {}

                def issue_qk(ix, h=h, n=n, QT=QT, keys=keys, psl=psl):
                    b_ = k.bank()
                    kt_ = keys[ix]
                    k.mm(b_[:, 0:n], KT[0:96, h, kt_ * 128:(kt_ + 1) * 128], QT[0:96, h, 0:n])
                    psl[ix] = b_
                for ix in range(min(AHEAD, len(keys))):
                    issue_qk(ix)
                for idx, kt in enumerate(keys):
                    ps_ = psl.pop(idx)
                    p_ = pT[idx % 4]
                    k.act(p_[:, 0:n], ps_[:, 0:n], AF.Exp)
                    if idx + AHEAD < len(keys):
                        issue_qk(idx + AHEAD)
                    for qi in range(nq):
                        k.mm(po[:, qi * 65:(qi + 1) * 65], p_[:, qi * 128:(qi + 1) * 128], V[:, kt, h, :],
                             start=(idx == 0), stop=(idx == len(keys) - 1))
                pov = po[:, 0:nq * 65].rearrange("p (q c) -> p q c", c=65)
                k.recip(rden[:, h, 0:nq], pov[:, :, 64])
                for qi in range(nq):
                    k.ts("dve", ot[:, qi, h * 64:(h + 1) * 64], pov[:, qi, 0:64], rden[:, h, qi:qi + 1], ALU.mult)
            o2 = OT[ci % 2]
            for qi in range(nq):
                pb = k.bank()
                for f in range(4):
                    k.tr(pb[:, f * 128:(f + 1) * 128], ot[:, qi, f * 128:(f + 1) * 128], ident[:, :])
                k.copy("act", o2[:, :, qi * 128:(qi + 1) * 128], pb[:, :].rearrange("p (a b) -> p a b", a=4))
            k.dma(mv[:, 2:6, t0:t0 + n], o2[:, :, 0:n], q="pool")


    def phase_RW(li):
        zsT = scratch("zsT", [1152, T])
        Tp = T + 3
        C = 64
        mu = k.sb([128, 9], F32, "mu")
        muh = k.sb([128, 9], F32, "muh")
        omu = k.sb([128, 9], F32, "omu")
        k.dma(mu[:, :], I["rwkv_mu"][li].rearrange("(f p) -> p f", p=128))
        k.ts("dve", muh[:, :], mu[:, :], 0.5, ALU.mult)
        k.ts("dve", omu[:, :], mu[:, :], -1.0, ALU.mult, 1.0, ALU.add)
        m0 = k.mark()
        zpb = [k.sb([128, Tp], F32, "zp") for _ in range(2)]
        tb = [k.sb([128, Tp], F32, "tsh") for _ in range(2)]
        for rt in range(9):
            zp = zpb[rt % 2]
            tm = tb[rt % 2]
            k.memset("pool", zp[:, 0:1], 0.0)
            k.memset("pool", zp[:, TL + 1:TL + 2], 0.0)
            k.memset("pool", zp[:, Tp - 1:Tp], 0.0)
            k.dma(zp[:, 1:TL + 1], zT[rt * 128:(rt + 1) * 128, 0:TL])
            k.dma(zp[:, TL + 2:TL + 2 + TC], zT[rt * 128:(rt + 1) * 128, TL:T])
            k.tt("dve", tm[:, 1:Tp - 1], zp[:, 0:Tp - 2], zp[:, 2:Tp], ALU.add)
            k.ts("dve", tm[:, 1:Tp - 1], tm[:, 1:Tp - 1], muh[:, rt:rt + 1], ALU.mult)
            k.stt("dve", tm[:, 1:Tp - 1], zp[:, 1:Tp - 1], omu[:, rt:rt + 1], tm[:, 1:Tp - 1], ALU.mult, ALU.add)
            k.dma(zsT[rt * 128:(rt + 1) * 128, 0:TL], tm[:, 1:TL + 1], q="pool")
            k.dma(zsT[rt * 128:(rt + 1) * 128, TL:T], tm[:, TL + 2:TL + 2 + TC], q="pool")
        k.release(m0)
        kkc = k.sb([64, 4], F32, "kkc")
        kac = k.sb([64, 4], F32, "kac")
        omka = k.sb([64, 4], F32, "omka")
        w0c = k.sb([64, 2, 4], F32, "w0c")
        a0c = k.sb([64, 2, 4], F32, "a0c")
        w2d = [k.sb([64, 256], F32, "w2d") for _ in range(2)]
        a2d = [k.sb([64, 256], F32, "a2d") for _ in range(2)]
        g2sb = k.sb([128, 256], F32, "g2sb")
        cmask = k.sb([64, 512], F32, "cmask")
        rmask = k.sb([64, 2, 256], F32, "rmask")
        nmask = k.sb([64, 2, 64], F32, "nmask")
        ones64 = onesf[0:64, 0:64]
        id64 = ident[0:64, 0:64]
        k.dma(kkc[:, :], I["rwkv_k_k"][li].rearrange("(h p) -> p h", p=64))
        k.dma(kac[:, :], I["rwkv_k_a"][li].rearrange("(h p) -> p h", p=64))
        for d_ in range(2):
            k.dma(w0c[:, d_, :], I["rwkv_w0"][li, d_].rearrange("(h p) -> p h", p=64))
            k.dma(a0c[:, d_, :], I["rwkv_a0"][li, d_].rearrange("(h p) -> p h", p=64))
            k.dma(w2d[d_][:, :], I["rwkv_w2"][li, d_])
            k.dma(a2d[d_][:, :], I["rwkv_a2"][li, d_])
        k.dma(g2sb[:, :], I["rwkv_g2"][li])
        k.dma(rmask[:, :, :], I["rmask"])
        k.dma(nmask[:, :, :], I["nmask"])
        k.ts("dve", omka[:, :], kac[:, :], -1.0, ALU.mult, 1.0, ALU.add)
        k.memset("dve", cmask[:, :], 1.0)
        k.memset("dve", cmask[:, :].rearrange("p (c i) -> p c i", i=C)[:, :, 0:1], 0.0)
        m1 = k.mark()

        def v3(t, n):
            return t[:, 0:n].rearrange("p (c i) -> p c i", i=C)

        GN = 256
        names = ["r_", "k_", "v_", "nrm", "kk", "logw", "a_", "kt", "b_", "Gf", "G", "Gex", "E1", "E2", "E4"]
        Wt = [{nm: k.sb([64, GN], F32, nm) for nm in names if nm != "v_"} for _ in range(2)]
        Vh = [[k.sb([64, GN], F32, "v_") for _ in range(4)] for _ in range(2)]
        AR = [[k.sb([64, GN // C, 2, C], F32, "AR") for _ in range(4)] for _ in range(2)]
        BK = [[k.sb([64, GN // C, 2, C], F32, "BK") for _ in range(4)] for _ in range(2)]
        BbT = [[k.sb([64, GN], F32, "BbT") for _ in range(4)] for _ in range(2)]
        KbT = [[k.sb([64, GN], F32, "KbT") for _ in range(4)] for _ in range(2)]
        gam = [[k.sb([64, GN // C], F32, "gam") for _ in range(4)] for _ in range(2)]
        yst = [[k.sb([64, 2, GN], F32, "yst") for _ in range(2)] for _ in range(2)]
        Sst = [k.sb([64, 2, 64], F32, "Sst") for _ in range(2)]
        wdd = k.sb([64, GN], F32, "wdd")
        add_ = k.sb([64, GN], F32, "add")
        gd = k.sb([128, GN], F32, "gd")
        gbuf = [k.sb([64, GN], F32, "gbuf") for _ in range(2)]
        NSL = 4
        SL = [dict(bkv=k.sb([64, 2, 3, C], F32, "BKV"), npmq=k.sb([64, 2, 256], F32, "NPMQ"), n0=k.sb([64, 2, 64], F32, "N0"),
                   nn=[k.sb([64, 2, 2, 64], F32, "NN") for _ in range(2)], tt=[k.sb([64, 2, 64], F32, "TT") for _ in range(2)],
                   xs=k.sb([64, 2, 64], F32, "Xs"), us=k.sb([64, 2, 64], F32, "Us")) for _ in range(NSL)]
        NEG = -float(np.exp(-0.5))

        def emit_prep(d, t0, n, par):
            ARg, BKg, BbTg, KbTg, gamg = AR[par], BK[par], BbT[par], KbT[par], gam[par]
            Wg = {h: dict(Wt[h % 2], v_=Vh[par][h]) for h in range(4)}
            ncnk = n // C
            k.dma(wdd[:, 0:n], zsT[768 + 64 * d:768 + 64 * d + 64, t0:t0 + n])
            k.dma(add_[:, 0:n], zsT[896 + 64 * d:896 + 64 * d + 64, t0:t0 + n])
            k.act(wdd[:, 0:n], wdd[:, 0:n], AF.Tanh)
            if d == 0:
                k.dma(gd[:, 0:n], zsT[1024:1152, t0:t0 + n])
                k.act(gd[:, 0:n], gd[:, 0:n], AF.Sigmoid)
            for h in range(4):
                w = Wg[h]
                hs = slice(h * 64, (h + 1) * 64)
                k.dma(w["r_"][:, 0:n], zsT[h * 64:(h + 1) * 64, t0:t0 + n])
                k.dma(w["k_"][:, 0:n], zsT[256 + h * 64:256 + (h + 1) * 64, t0:t0 + n])
                k.dma(w["v_"][:, 0:n], zsT[512 + h * 64:512 + (h + 1) * 64, t0:t0 + n])
                if d == 0:
                    pg = k.bank()
                    k.mm(pg[0:64, 0:n], g2sb[:, hs], gd[:, 0:n])
                    k.copy("act", gbuf[h % 2][:, 0:n], pg[0:64, 0:n])
                    k.dma(rkvg[3, hs, t0:t0 + n], gbuf[h % 2][:, 0:n], q="pool")
                k.ts("dve", w["kk"][:, 0:n], w["k_"][:, 0:n], kkc[:, h:h + 1], ALU.mult)
                k.act(w["nrm"][:, 0:n], w["kk"][:, 0:n], AF.Square)
                pss = k.bank()
                k.mm(pss[0:64, 0:n], ones64, w["nrm"][:, 0:n])
                k.act(w["nrm"][:, 0:n], pss[0:64, 0:n], AF.Sqrt)
                k.ts("dve", w["nrm"][:, 0:n], w["nrm"][:, 0:n], 1e-12, ALU.max)
                k.recip(w["nrm"][:, 0:n], w["nrm"][:, 0:n])
                k.tt("dve", w["kk"][:, 0:n], w["kk"][:, 0:n], w["nrm"][:, 0:n], ALU.mult)
                pw = k.bank()
                k.mm(pw[0:64, 0:n], w2d[d][:, hs], wdd[:, 0:n])
                k.act(w["logw"][:, 0:n], pw[0:64, 0:n], AF.Sigmoid, bias=w0c[:, d, h:h + 1])
                k.ts("dve", w["logw"][:, 0:n], w["logw"][:, 0:n], NEG, ALU.mult)
                pa = k.bank()
                k.mm(pa[0:64, 0:n], a2d[d][:, hs], add_[:, 0:n])
                k.act(w["a_"][:, 0:n], pa[0:64, 0:n], AF.Sigmoid, bias=a0c[:, d, h:h + 1])
                k.act(w["kt"][:, 0:n], w["a_"][:, 0:n], AF.Identity, bias=omka[:, h:h + 1], scale=kac[:, h:h + 1])
                k.tt("pool", w["kt"][:, 0:n], w["k_"][:, 0:n], w["kt"][:, 0:n], ALU.mult)
                k.tt("pool", w["b_"][:, 0:n], w["kk"][:, 0:n], w["a_"][:, 0:n], ALU.mult)
                k.scan(w["Gf"][:, 0:n], cmask[:, 0:n], w["logw"][:, 0:n], 0.0, ALU.mult, ALU.add)
                totb = v3(w["Gf"], n)[:, :, C - 1:C].to_broadcast([64, ncnk, C])
                if d == 0:
                    G = w["Gf"]
                    k.tt("dve", w["Gex"][:, 0:n], w["Gf"][:, 0:n], w["logw"][:, 0:n], ALU.subtract)
                else:
                    G = w["G"]
                    k.tt("dve", v3(w["Gex"], n), totb, v3(w["Gf"], n), ALU.subtract)
                    k.tt("dve", w["G"][:, 0:n], w["Gex"][:, 0:n], w["logw"][:, 0:n], ALU.add)
                k.act(w["E1"][:, 0:n], G[:, 0:n], AF.Exp)
                k.act(w["E2"][:, 0:n], G[:, 0:n], AF.Exp, scale=-1.0)
                k.act(w["Gex"][:, 0:n], w["Gex"][:, 0:n], AF.Exp)
                k.tt("dve", v3(w["E4"], n), totb, v3(G, n), ALU.subtract)
                k.act(w["E4"][:, 0:n], w["E4"][:, 0:n], AF.Exp)
                k.act(gamg[h][:, 0:ncnk], v3(w["Gf"], n)[:, :, C - 1], AF.Exp)
                k.stt("dve", ARg[h][:, 0:ncnk, 0, :], v3(w["kk"], n), -1.0, v3(w["Gex"], n), ALU.mult, ALU.mult)
                k.tt("pool", ARg[h][:, 0:ncnk, 1, :], v3(w["r_"], n), v3(w["E1"], n), ALU.mult)
                k.tt("dve", BKg[h][:, 0:ncnk, 0, :], v3(w["b_"], n), v3(w["E2"], n), ALU.mult)
                k.tt("pool", BKg[h][:, 0:ncnk, 1, :], v3(w["kt"], n), v3(w["E2"], n), ALU.mult)
                k.tt("dve", BbTg[h][:, 0:n], w["b_"][:, 0:n], w["E4"][:, 0:n], ALU.mult)
                k.tt("pool", KbTg[h][:, 0:n], w["kt"][:, 0:n], w["E4"][:, 0:n], ALU.mult)

        def emit_chain(d, t0, n, par, drip):
            ARg, BKg, BbTg, KbTg, gamg, ystg = AR[par], BK[par], BbT[par], KbT[par], gam[par], yst[par]
            Wg = {h: dict(Wt[h % 2], v_=Vh[par][h]) for h in range(4)}
            ncnk = n // C
            order = list(range(ncnk)) if d == 0 else list(range(ncnk - 1, -1, -1))
            chains = [(c, hp) for c in order for hp in range(2)]
            for w0 in range(0, len(chains), NSL):
                wave = chains[w0:w0 + NSL]
                for si, (c, hp) in enumerate(wave):
                    sl = SL[si]
                    cs_ = slice(c * C, (c + 1) * C)
                    pT_ = k.bank()
                    for hh in range(2):
                        h = 2 * hp + hh
                        k.tr(pT_[0:64, hh * 192:hh * 192 + 64], BbTg[h][:, cs_], id64)
                        k.tr(pT_[0:64, hh * 192 + 64:hh * 192 + 128], KbTg[h][:, cs_], id64)
                        k.tr(pT_[0:64, hh * 192 + 128:hh * 192 + 192], Wg[h]["v_"][:, cs_], id64)
                    k.copy("act", sl["bkv"][:, :, :, :], pT_[0:64, 0:384].rearrange("p (h a b) -> p h a b", h=2, a=3))
                    drip()
                for si, (c, hp) in enumerate(wave):
                    sl = SL[si]
                    pA = k.bank()
                    pN = k.bank()
                    for hh in range(2):
                        h = 2 * hp + hh
                        arf = ARg[h][:, c, :, :].rearrange("p a b -> p (a b)")
                        k.mm(pA[0:64, hh * 256:hh * 256 + 128], BKg[h][:, c, 0, :], arf)
                        k.mm(pA[0:64, hh * 256 + 128:hh * 256 + 256], BKg[h][:, c, 1, :], arf)
                        k.mm(pN[0:64, hh * 64:(hh + 1) * 64], ARg[h][:, c, 0, :], BKg[h][:, c, 0, :])
                    npmq = sl["npmq"]
                    n0 = sl["n0"]
                    k.tt("dve", npmq[:, :, :], pA[0:64, 0:512].rearrange("p (h c) -> p h c", h=2),
                         rmask[:, d:d + 1, :].to_broadcast([64, 2, 256]), ALU.mult)
                    k.tt("dve", n0[:, :, :], pN[0:64, 0:128].rearrange("p (h c) -> p h c", h=2),
                         nmask[:, d:d + 1, :].to_broadcast([64, 2, 64]), ALU.mult)
                    k.tt("pool", sl["tt"][0][:, :, :], npmq[:, :, 0:64], id64.unsqueeze(1).to_broadcast([64, 2, 64]), ALU.add)
                    sl["ttc"] = sl["tt"][0]
                    sl["ncur"] = [n0[:, 0, :], n0[:, 1, :]]
                    sl["ntcur"] = [npmq[:, 0, 0:64], npmq[:, 1, 0:64]]
                    drip()
                for p in range(5):
                    for si, (c, hp) in enumerate(wave):
                        sl = SL[si]
                        pC = k.bank()
                        for hh in range(2):
                            k.mm(pC[0:64, hh * 128:hh * 128 + 64], sl["ntcur"][hh], sl["ncur"][hh])
                            k.mm(pC[0:64, hh * 128 + 64:hh * 128 + 128], sl["ncur"][hh], sl["ntcur"][hh])
                        nn = sl["nn"][p % 2]
                        k.copy("act", nn[:, :, :, :], pC[0:64, 0:256].rearrange("p (h a c) -> p h a c", h=2, a=2))
                        sl["ncur"] = [nn[:, 0, 0, :], nn[:, 1, 0, :]]
                        sl["ntcur"] = [nn[:, 0, 1, :], nn[:, 1, 1, :]]
                        drip()
                    for si, (c, hp) in enumerate(wave):
                        sl = SL[si]
                        pD = k.bank()
                        for hh in range(2):
                            k.mm(pD[0:64, hh * 64:(hh + 1) * 64], sl["ncur"][hh], sl["ttc"][:, hh, :])
                        tn = sl["tt"][(p + 1) % 2]
                        k.tt("dve", tn[:, :, :], sl["ttc"][:, :, :], pD[0:64, 0:128].rearrange("p (h c) -> p h c", h=2), ALU.add)
                        sl["ttc"] = tn
                        drip()
                for si, (c, hp) in enumerate(wave):
                    sl = SL[si]
                    cs_ = slice(c * C, (c + 1) * C)
                    npmq = sl["npmq"]
                    bkv = sl["bkv"]
                    tt_ = sl["ttc"]
                    S_ = Sst[hp]
                    pX = k.bank()
                    for hh in range(2):
                        h = 2 * hp + hh
                        k.mm(pX[0:64, hh * 64:(hh + 1) * 64], ARg[h][:, c, 0, :], S_[:, hh, :], start=True, stop=False)
                        k.mm(pX[0:64, hh * 64:(hh + 1) * 64], npmq[:, hh, 128:192], bkv[:, hh, 2, :], start=False, stop=True)
                    k.copy("act", sl["xs"][:, :, :], pX[0:64, 0:128].rearrange("p (h c) -> p h c", h=2))
                    pU = k.bank()
                    for hh in range(2):
                        k.mm(pU[0:64, hh * 64:(hh + 1) * 64], tt_[:, hh, :], sl["xs"][:, hh, :])
                    k.copy("act", sl["us"][:, :, :], pU[0:64, 0:128].rearrange("p (h c) -> p h c", h=2))
                    drip()
                    pY = k.bank()
                    pS = k.bank()
                    for hh in range(2):
                        h = 2 * hp + hh
                        ysl = pY[0:64, hh * 64:(hh + 1) * 64]
                        k.mm(ysl, S_[:, hh, :], ARg[h][:, c, 1, :], start=True, stop=False)
                        k.mm(ysl, sl["us"][:, hh, :], npmq[:, hh, 64:128], start=False, stop=False)
                        k.mm(ysl, bkv[:, hh, 2, :], npmq[:, hh, 192:256], start=False, stop=True)
                        ssl = pS[0:64, hh * 64:(hh + 1) * 64]
                        k.mm(ssl, bkv[:, hh, 0, :], sl["us"][:, hh, :], start=True, stop=False)
                        k.mm(ssl, bkv[:, hh, 1, :], bkv[:, hh, 2, :], start=False, stop=True)
                    k.copy("act", ystg[hp][:, :, cs_], pY[0:64, 0:128].rearrange("p (h c) -> p h c", h=2))
                    for hh in range(2):
                        h = 2 * hp + hh
                        k.stt("dve", S_[:, hh, :], S_[:, hh, :], gamg[h][:, c:c + 1], pS[0:64, hh * 64:(hh + 1) * 64], ALU.mult, ALU.add)
            for hp in range(2):
                k.dma(ydir[d, hp * 128:(hp + 1) * 128, t0:t0 + n].rearrange("(h v) t -> v h t", v=64), ystg[hp][:, :, 0:n], q="pool")

        for d in range(2):
            for hp in range(2):
                k.memset("dve", Sst[hp][:, :, :], 0.0)
            lat = [(i * GN, GN) for i in range(TL // GN)]
            groups = [(TL, TC)] + (lat if d == 0 else lat[::-1])
            k.pool = [4, 5]
            emit_prep(d, groups[0][0], groups[0][1], 0)
            for gi, (t0, n) in enumerate(groups):
                pend = []
                if gi + 1 < len(groups):
                    S.capture = pend
                    k.pool = [4, 5]
                    emit_prep(d, groups[gi + 1][0], groups[gi + 1][1], (gi + 1) % 2)
                    S.capture = None
                k.pool = [0, 1, 2, 3, 6, 7]
                emit_chain(d, t0, n, gi % 2, lambda: S.replay(pend, 3))
                S.replay(pend)
        k.pool = [0, 1, 2, 3, 4, 5]
        k.release(m1)
        rkc = k.sb([128, 2], F32, "rkc")
        lng = k.sb([128, 2], F32, "lng")
        lnb = k.sb([128, 2], F32, "lnb")
        gne = k.sb([128, 1], F32, "gne")
        k.dma(rkc[:, :], I["rwkv_r_k"][li].rearrange("(a b) c -> (b c) a", b=2))
        k.dma(lng[:, :], I["rwkv_ln_g"][li].rearrange("(h p) -> p h", p=128))
        k.dma(lnb[:, :], I["rwkv_ln_b"][li].rearrange("(h p) -> p h", p=128))
        k.memset("dve", gne[:, :], 64e-5)
        nm2 = ["yf", "yr", "r_", "k_", "v_", "g_", "cen", "sq", "rs", "rk"]
        P2 = [{nm: k.sb([128, 512], F32, nm) for nm in nm2} for _ in range(2)]
        ob = [k.sb([128, 512], BF16, "ob") for _ in range(2)]
        it = 0
        for (t0, n) in CH:
            for hp in range(2):
                w = P2[it % 2]
                o_ = ob[it % 2]
                it += 1
                hs = slice(hp * 128, (hp + 1) * 128)
                k.dma(w["yf"][:, 0:n], ydir[0, hs, t0:t0 + n])
                k.dma(w["yr"][:, 0:n], ydir[1, hs, t0:t0 + n])
                k.dma(w["r_"][:, 0:n], zsT[hp * 128:(hp + 1) * 128, t0:t0 + n])
                k.dma(w["k_"][:, 0:n], zsT[256 + hp * 128:256 + (hp + 1) * 128, t0:t0 + n])
                k.dma(w["v_"][:, 0:n], zsT[512 + hp * 128:512 + (hp + 1) * 128, t0:t0 + n])
                k.dma(w["g_"][:, 0:n], rkvg[3, hs, t0:t0 + n])
                k.tt("dve", w["yf"][:, 0:n], w["yf"][:, 0:n], w["yr"][:, 0:n], ALU.add)
                pm = k.bank()
                k.mm(pm[:, 0:n], bones[:, :], w["yf"][:, 0:n])
                k.stt("dve", w["cen"][:, 0:n], pm[:, 0:n], -1.0 / 64, w["yf"][:, 0:n], ALU.mult, ALU.add)
                k.act(w["sq"][:, 0:n], w["cen"][:, 0:n], AF.Square)
                pv = k.bank()
                k.mm(pv[:, 0:n], bones[:, :], w["sq"][:, 0:n])
                k.act(w["rs"][:, 0:n], pv[:, 0:n], AF.Sqrt, bias=gne[:, 0:1], scale=1.0 / 64)
                k.recip(w["rs"][:, 0:n], w["rs"][:, 0:n])
                k.tt("dve", w["cen"][:, 0:n], w["cen"][:, 0:n], w["rs"][:, 0:n], ALU.mult)
                k.act(w["cen"][:, 0:n], w["cen"][:, 0:n], AF.Identity, bias=lnb[:, hp:hp + 1], scale=lng[:, hp:hp + 1])
                k.stt("dve", w["rk"][:, 0:n], w["r_"][:, 0:n], rkc[:, hp:hp + 1], w["k_"][:, 0:n], ALU.mult, ALU.mult)
                pb2 = k.bank()
                k.mm(pb2[:, 0:n], bones[:, :], w["rk"][:, 0:n])
                k.tt("dve", w["rk"][:, 0:n], pb2[:, 0:n], w["v_"][:, 0:n], ALU.mult)
                k.tt("dve", w["cen"][:, 0:n], w["cen"][:, 0:n], w["rk"][:, 0:n], ALU.add)
                k.tt("dve", o_[:, 0:n], w["cen"][:, 0:n], w["g_"][:, 0:n], ALU.mult)
                k.dma(mixT[hs, t0:t0 + n], o_[:, 0:n], q="pool")


    def phase_ML(li):
        B0 = 1952
        Tp = T + 3
        cw = k.sb([32, 8, 3], F32, "cw")
        cb = k.sb([32, 8], F32, "cb")
        for j in range(3):
            k.dma(cw[:, :, j], I["mlstm_conv_w"][li, j].rearrange("(g p) -> p g", p=32))
        k.dma(cb[:, :], I["mlstm_conv_b"][li].rearrange("(g p) -> p g", p=32))
        qkb = k.sb([32, 8, T], BF16, "qkb")
        VL = k.sb([128, 34, 4, 65], BF16, "VL")
        k.memset("pool", VL[:, :, :, 64:65], 1.0)
        Hs = k.sb([128, 34, 256], F32, "Hs")
        acol = k.sb([128, 34, 8], F32, "acol")
        Fcol = k.sb([128, 34, 8], F32, "Fcol")
        Rb = k.sb([128, 8, 9], F32, "Rb")
        nRb = k.sb([128, 8, 9], F32, "nRb")
        lmask = k.sb([128, 2, 4, 512], BF16, "lmask")
        k.dma(lmask[:, :, :, :], I["lmask"], cast=True)
        m1 = k.mark()
        zpb = [k.sb([32, Tp], F32, "zpm") for _ in range(2)]
        accb = [k.sb([32, Tp], F32, "accm") for _ in range(2)]
        for g in range(8):
            zp = zpb[g % 2]
            acc = accb[g % 2]
            k.memset("pool", zp[:, 0:1], 0.0)
            k.memset("pool", zp[:, TL + 1:TL + 2], 0.0)
            k.memset("pool", zp[:, Tp - 1:Tp], 0.0)
            k.dma(zp[:, 1:TL + 1], zT[B0 + 32 * g:B0 + 32 * g + 32, 0:TL])
            k.dma(zp[:, TL + 2:TL + 2 + TC], zT[B0 + 32 * g:B0 + 32 * g + 32, TL:T])
            k.ts("dve", acc[:, 1:Tp - 1], zp[:, 1:Tp - 1], cw[:, g, 1:2], ALU.mult)
            k.stt("dve", acc[:, 1:Tp - 1], zp[:, 0:Tp - 2], cw[:, g, 0:1], acc[:, 1:Tp - 1], ALU.mult, ALU.add)
            k.stt("dve", acc[:, 1:Tp - 1], zp[:, 2:Tp], cw[:, g, 2:3], acc[:, 1:Tp - 1], ALU.mult, ALU.add)
            if g < 4:
                k.act(acc[:, 1:Tp - 1], acc[:, 1:Tp - 1], AF.Silu, bias=cb[:, g:g + 1])
                k.ts("dve", qkb[:, g, 0:TL], acc[:, 1:TL + 1], float(32 ** -0.5), ALU.mult)
                k.ts("dve", qkb[:, g, TL:T], acc[:, TL + 2:TL + 2 + TC], float(32 ** -0.5), ALU.mult)
            else:
                k.act(qkb[:, g, 0:TL], acc[:, 1:TL + 1], AF.Silu, bias=cb[:, g:g + 1])
                k.act(qkb[:, g, TL:T], acc[:, TL + 2:TL + 2 + TC], AF.Silu, bias=cb[:, g:g + 1])
        k.release(m1)
        vT = [k.sb([128, 2, 512], F32, "vT") for _ in range(2)]
        for ci, (t0, n) in enumerate(CH):
            v_ = vT[ci % 2]
            k.dma(v_[:, :, 0:n], zT[B0 + 256:B0 + 512, t0:t0 + n].rearrange("(f p) t -> p f t", p=128))
            for j in range(n // 128):
                pb = k.bank()
                for f in range(2):
                    k.tr(pb[:, f * 128:(f + 1) * 128], v_[:, f, j * 128:(j + 1) * 128], ident[:, :])
                k.copy("act", VL[:, t0 // 128 + j, :, 0:64], pb[:, 0:256].rearrange("p (h c) -> p h c", c=64))
        k.release(m1)
        GI = k.sb([8, T], F32, "GI")
        GF = k.sb([8, T], F32, "GF")
        Fp = k.sb([8, T], F32, "Fp")
        ones8 = k.sb([8, TL], F32, "ones8")
        bi = k.sb([8, 2], F32, "bi")
        dc = k.sb([8, 2], F32, "dc")
        sm = k.sb([8, 16], F32, "sm")
        Rt = k.sb([8, 9], F32, "Rt")
        cm = k.sb([8, 9], F32, "cm")
        pmx = k.sb([8, 8], F32, "pmx")
        smx = k.sb([8, 8], F32, "smx")
        sel8 = k.sb([8, 8, 128], F32, "sel8")
        GB = B0 + 768
        for d in range(2):
            k.dma(GI[4 * d:4 * d + 4, :], zT[GB + 8 * d:GB + 8 * d + 4, :])
            k.dma(GF[4 * d:4 * d + 4, :], zT[GB + 8 * d + 4:GB + 8 * d + 8, :])
        k.dma(bi[:, 0:1], I["mlstm_i_b"][li].rearrange("d (h o) -> (d h) o", o=1))
        k.dma(bi[:, 1:2], I["mlstm_f_b"][li].rearrange("d (h o) -> (d h) o", o=1))
        k.dma(dc[:, :], I["dircols"])
        k.dma(sel8[:, :, :], I["sel8"])
        k.memset("dve", ones8[:, :], 1.0)
        k.ts("dve", bi[:, :], bi[:, :], 1.0 / 15.0, ALU.mult)
        k.act(GI[:, :], GI[:, :], AF.Tanh, bias=bi[:, 0:1], scale=1.0 / 15.0)
        k.ts("dve", GI[:, :], GI[:, :], 15.0, ALU.mult)
        k.act(GF[:, :], GF[:, :], AF.Tanh, bias=bi[:, 1:2], scale=1.0 / 15.0)
        k.act(GF[:, :], GF[:, :], AF.Exp, scale=-15.0)
        k.ts("dve", GF[:, :], GF[:, :], 1.0, ALU.add)
        k.act(GF[:, :], GF[:, :], AF.Ln)
        k.ts("dve", GF[:, :], GF[:, :], -1.0, ALU.mult)
        k.scan(Fp[:, 0:TL], ones8[:, 0:TL], GF[:, 0:TL], 0.0, ALU.mult, ALU.add)
        k.scan(Fp[:, TL:T], ones8[:, 0:TC], GF[:, TL:T], 0.0, ALU.mult, ALU.add)
        k.copy("dve", sm[:, 0:1], Fp[:, TL - 1:TL])
        k.copy("dve", sm[:, 1:2], Fp[:, T - 1:T])
        k.stt("dve", sm[:, 2:3], sm[:, 0:1], dc[:, 1:2], sm[:, 1:2], ALU.mult, ALU.add)
        k.tt("dve", sm[:, 3:4], sm[:, 1:2], dc[:, 1:2], ALU.mult)
        k.ts("dve", Fp[:, :], Fp[:, :], dc[:, 0:1], ALU.mult)
        k.stt("dve", Fp[:, :], GF[:, :], dc[:, 1:2], Fp[:, :], ALU.mult, ALU.add)
        k.ts("dve", Fp[:, 0:TL], Fp[:, 0:TL], sm[:, 2:3], ALU.add)
        k.ts("dve", Fp[:, TL:T], Fp[:, TL:T], sm[:, 3:4], ALU.add)
        k.tt("dve", GI[:, :], GI[:, :], Fp[:, :], ALU.subtract)
        k.reduce(cm[:, 0:8], GI[:, 0:TL].rearrange("p (c i) -> p c i", i=512), ALU.max)
        k.reduce(cm[:, 8:9], GI[:, TL:T].rearrange("p (c i) -> p c i", i=TC), ALU.max)
        k.tt("dve", pmx[:, 0:1], cm[:, 0:1], cm[:, 8:9], ALU.max)
        for c in range(1, 8):
            k.tt("dve", pmx[:, c:c + 1], pmx[:, c - 1:c], cm[:, c:c + 1], ALU.max)
        k.tt("dve", smx[:, 7:8], cm[:, 7:8], cm[:, 8:9], ALU.max)
        for c in range(6, -1, -1):
            k.tt("dve", smx[:, c:c + 1], smx[:, c + 1:c + 2], cm[:, c:c + 1], ALU.max)
        k.tt("dve", smx[:, :], smx[:, :], pmx[:, :], ALU.subtract)
        k.stt("dve", Rt[:, 0:8], smx[:, :], dc[:, 1:2], pmx[:, :], ALU.mult, ALU.add)
        k.copy("dve", Rt[:, 8:9], cm[:, 8:9])
        pa_ = k.bank()
        pf_ = k.bank()
        for j in range(34):
            k.tr(pa_[:, 8 * j:8 * j + 8], GI[0:8, j * 128:(j + 1) * 128], ident[0:8, 0:8])
            k.tr(pf_[:, 8 * j:8 * j + 8], Fp[0:8, j * 128:(j + 1) * 128], ident[0:8, 0:8])
        k.copy("act", acol[:, :, :], pa_[:, 0:272].rearrange("p (j c) -> p j c", c=8))
        k.copy("act", Fcol[:, :, :], pf_[:, 0:272].rearrange("p (j c) -> p j c", c=8))
        pr_ = k.bank()
        for c in range(8):
            k.mm(pr_[:, 9 * c:9 * c + 9], sel8[:, c, :], Rt[:, :])
        k.copy("act", Rb[:, :, :], pr_[:, 0:72].rearrange("p (c i) -> p c i", i=9))
        k.ts("dve", nRb[:, :, :], Rb[:, :, :], -1.0, ALU.mult)
        k.release(m1)
        Et = k.sb([128, 34, 9], F32, "Et")
        Wb = [k.sb([128, 512], BF16, "Wb") for _ in range(4)]
        thr = k.sb([128, 4], F32, "thr")
        den = k.sb([128, 4], F32, "den")
        hd = k.sb([128, 64], F32, "hd")
        import os as _os3
        for d in range(2 if not _os3.environ.get('SKIP_MLMAIN') else 0):
            for h in range(4):
                c = d * 4 + h
                for J in range(34):
                    k.ts("dve", Et[:, J, :], nRb[:, c, :], acol[:, J, c:c + 1], ALU.add)
                k.ts("dve", Et[:, :, :], Et[:, :, :], 0.0, ALU.min)
                k.act(Et[:, :, :], Et[:, :, :], AF.Exp)
                for I_, (t0, n) in enumerate(CH):
                    nq = n // 128
                    tb0 = t0 // 128
                    if t0 >= TL:
                        keys = [(32, 0), (33, 1)]
                    elif d == 0:
                        keys = [(32, None), (33, None)] + [(J, None) for J in range(0, 4 * I_)] + [(4 * I_ + r, r) for r in range(4)]
                    else:
                        keys = [(32, None), (33, None)] + [(J, None) for J in range(4 * I_ + 4, 32)] + [(4 * I_ + r, r) for r in range(4)]
                    def contributes(r, qi):
                        if r is None:
                            return True
                        return (r <= qi) if d == 0 else (r >= qi)
                    lists = {qi: [ix for ix, (J, r) in enumerate(keys) if contributes(r, qi)] for qi in range(nq)}
                    po = k.acc()
                    AHEAD = 4
                    psl = {}

                    def issue_qk(ixx, h=h, n=n, t0=t0, keys=keys, psl=psl):
                        b_ = k.bank()
                        J_ = keys[ixx][0]
                        k.mm(b_[:, 0:n], qkb[0:32, 4 + h, J_ * 128:(J_ + 1) * 128], qkb[0:32, h, t0:t0 + n])
                        psl[ixx] = b_
                    for ixx in range(min(AHEAD, len(keys))):
                        issue_qk(ixx)
                    for ix, (J, r) in enumerate(keys):
                        ps_ = psl.pop(ix)
                        if ix + AHEAD < len(keys):
                            issue_qk(ix + AHEAD)
                        w_ = Wb[ix % 4]
                        if r is None:
                            k.ts("dve", w_[:, 0:n], ps_[:, 0:n], Et[:, J, I_:I_ + 1], ALU.mult)
                        else:
                            k.stt("dve", w_[:, 0:n], ps_[:, 0:n], Et[:, J, I_:I_ + 1], lmask[:, d, r, 0:n], ALU.mult, ALU.mult)
                        for qi in range(nq):
                            if ix in lists[qi]:
                                k.mm(po[:, qi * 65:(qi + 1) * 65], w_[:, qi * 128:(qi + 1) * 128], VL[:, J, h, :],
                                     start=(ix == lists[qi][0]), stop=(ix == lists[qi][-1]))
                    pov = po[:, 0:nq * 65].rearrange("p (q c) -> p q c", c=65)
                    k.act(thr[:, 0:nq], Fcol[:, tb0:tb0 + nq, c], AF.Exp, bias=nRb[:, c, I_:I_ + 1], scale=-1.0)
                    k.act(den[:, 0:nq], pov[:, :, 64], AF.Abs)
                    k.tt("dve", den[:, 0:nq], den[:, 0:nq], thr[:, 0:nq], ALU.max)
                    k.recip(den[:, 0:nq], den[:, 0:nq])
                    for qi in range(nq):
                        if d == 0:
                            k.ts("dve", Hs[:, tb0 + qi, h * 64:(h + 1) * 64], pov[:, qi, 0:64], den[:, qi:qi + 1], ALU.mult)
                        else:
                            k.stt("dve", Hs[:, tb0 + qi, h * 64:(h + 1) * 64], pov[:, qi, 0:64], den[:, qi:qi + 1],
                                  Hs[:, tb0 + qi, h * 64:(h + 1) * 64], ALU.mult, ALU.add)
        ng = k.sb([128, 2], F32, "ng")
        k.dma(ng[:, :], I["mlstm_norm_g"][li].rearrange("(f p) -> p f", p=128))
        ssq = k.sb([128, 4], F32, "ssq")
        junk = k.sb([128, 64], F32, "junk")
        hn = [k.sb([128, 256], F32, "hn") for _ in range(2)]
        oT = [k.sb([128, 2, 512], F32, "oT") for _ in range(2)]
        hT = [k.sb([128, 2, 512], F32, "hT") for _ in range(2)]
        hb = [k.sb([128, 2, 512], BF16, "hb") for _ in range(2)]
        for ci, (t0, n) in enumerate(CH):
            o_ = oT[ci % 2]
            k.dma(o_[:, :, 0:n], zT[B0 + 512:B0 + 768, t0:t0 + n].rearrange("(f p) t -> p f t", p=128))
            k.act(o_[:, :, 0:n], o_[:, :, 0:n], AF.Sigmoid)
            ht = hT[ci % 2]
            for j in range(n // 128):
                tix = t0 // 128 + j
                for h in range(4):
                    k.act(junk[:, :], Hs[:, tix, h * 64:(h + 1) * 64], AF.Square, accum=ssq[:, h:h + 1])
                k.act(ssq[:, :], ssq[:, :], AF.Sqrt, bias=epsc[:, 0:1], scale=1.0 / 64)
                k.recip(ssq[:, :], ssq[:, :])
                hn_ = hn[j % 2]
                k.tt("dve", hn_[:, :].rearrange("p (h c) -> p h c", c=64), Hs[:, tix, :].rearrange("p (h c) -> p h c", c=64),
                     ssq[:, :].unsqueeze(2).to_broadcast([128, 4, 64]), ALU.mult)
                pb = k.bank()
                for f in range(2):
                    k.tr(pb[:, f * 128:(f + 1) * 128], hn_[:, f * 128:(f + 1) * 128], ident[:, :])
                for f in range(2):
                    k.ts("dve", ht[:, f, j * 128:(j + 1) * 128], pb[:, f * 128:(f + 1) * 128], ng[:, f:f + 1], ALU.mult)
            k.tt("pool", hb[ci % 2][:, :, 0:n], ht[:, :, 0:n], o_[:, :, 0:n], ALU.mult)
            k.dma(mixT[768:1024, t0:t0 + n].rearrange("(f p) t -> p f t", p=128), hb[ci % 2][:, :, 0:n], q="pool")

    PH = dict(A=phase_A, M=phase_M, NP=phase_NP, O=phase_O, F=phase_F, MLA=phase_MLA, RW=phase_RW, ML=phase_ML)
    return nc, k, PH, base_mark, (xTa, xTb)


def finish_build(nc, k, PH, base_mark, xs, n_layers=DEPTH, enabled=("RW", "MLA", "ML"), tail=True):
    xTa, xTb = xs
    PH["A"]()
    k.release(base_mark)
    for li in range(n_layers):
        ctx_out = li < DEPTH - 1
        PH["M"](li)
        k.release(base_mark)
        PH["NP"](li, xTa)
        k.release(base_mark)
        if "RW" in enabled:
            PH["RW"](li)
            k.release(base_mark)
        if "MLA" in enabled:
            PH["MLA"](li, ctx_out)
            k.release(base_mark)
        if "ML" in enabled:
            PH["ML"](li)
            k.release(base_mark)
        if tail:
            PH["O"](li, xTa, xTb)
            k.release(base_mark)
            PH["F"](li, xTb, xTa, li == DEPTH - 1)
            k.release(base_mark)
    k.S.barrier()
    k.S.emit()
    return nc


def make_in_maps(inputs, consts, cores):
    maps = []
    for b in cores:
        m = {"x": np.ascontiguousarray(inputs["x"][b]), "ctx": np.ascontiguousarray(inputs["ctx"][b]),
             "c": np.ascontiguousarray(inputs["c"][b]), "c_ctx": np.ascontiguousarray(inputs["c_ctx"])}
        for p in PARAMS:
            m[p] = np.ascontiguousarray(inputs[p])
        m.update(consts)
        maps.append(m)
    return maps


def all_shapes(inputs, consts):
    shp = {"x": (TL, D), "ctx": (TC, D), "c": (D,), "c_ctx": (D,)}
    for p in PARAMS:
        shp[p] = tuple(inputs[p].shape)
    for kk, v in consts.items():
        shp[kk] = tuple(v.shape)
    return shp


def kernel(**inputs):
    inputs = {kk: np.asarray(v, dtype=np.float32) for kk, v in inputs.items()}
    consts = host_consts()
    nc, k, PH, bm, xs = build(all_shapes(inputs, consts))
    finish_build(nc, k, PH, bm, xs)
    cores = list(range(8))
    res = run_bass_kernel_spmd(nc, make_in_maps(inputs, consts, cores), core_ids=cores)
    return np.stack([np.asarray(res.results[b]["out"], dtype=np.float32) for b in cores], axis=0)
```

```python
import contextlib
import numpy as np
import ml_dtypes
import concourse.bass as bass
import concourse.mybir as mybir
from concourse.bass_utils import run_bass_kernel_spmd

F32 = mybir.dt.float32
BF16 = mybir.dt.bfloat16
ALU = mybir.AluOpType
AF = mybir.ActivationFunctionType

D = 1024
TL = 4096
TC = 256
T = TL + TC
DEPTH = 2
IN_COLS = 2736
FFN_H = 2816
CH = [(i * 512, 512) for i in range(8)] + [(TL, TC)]
NSLOT = 24
EPS = 1e-6
SB_BASE = 16640
SB_TOP = 229376 - 2048


class Sched:
    ENGS = ("pe", "act", "dve", "pool", "sp")

    def __init__(self, nc):
        self.nc = nc
        self.ops = {e: [] for e in self.ENGS}
        self.count = {e: 0 for e in self.ENGS}
        self.known = {e: {} for e in self.ENGS}
        self.sems = {}
        self.wr = {}
        self.rd = {}
        self.slot_n = 0
        self.slot_sp = 0
        self.slot_pl = 0
        self.slot_cnt = [0] * NSLOT
        self.n_ops = 0
        self.limit = None
        self.log = None
        self.capture = None

    def _deps(self, reads, writes):
        d = {}
        for b in reads:
            w = self.wr.get(b)
            if w and w[1] > d.get(w[0], 0):
                d[w[0]] = w[1]
        for b in writes:
            w = self.wr.get(b)
            if w and w[1] > d.get(w[0], 0):
                d[w[0]] = w[1]
            for s, v in self.rd.get(b, {}).items():
                if v > d.get(s, 0):
                    d[s] = v
        return d

    def _mark(self, reads, writes, src, val):
        for b in reads:
            self.rd.setdefault(b, {})[src] = val
        for b in writes:
            self.wr[b] = (src, val)
            self.rd[b] = {}

    def _waits(self, eng, deps):
        ws = []
        kn = self.known[eng]
        for src, val in deps.items():
            if kn.get(src, 0) >= val:
                continue
            if src == eng == "pe":
                continue
            kn[src] = val
            ws.append((src, val))
        return ws

    def op(self, eng, fn, reads=(), writes=()):
        if self.capture is not None:
            self.capture.append(("op", eng, fn, tuple(reads), tuple(writes)))
            return
        if self.limit is not None and self.n_ops >= self.limit:
            return
        if self.log is not None:
            import sys as _s
            self.log.append((self.n_ops, eng, _s._getframe(2).f_lineno))
        ws = self._waits(eng, self._deps(reads, writes))
        self.count[eng] += 1
        val = self.count[eng]
        self.ops[eng].append((ws, fn, (eng, 1)))
        self._mark(reads, writes, eng, val)
        self.n_ops += 1

    def dma(self, fn, reads=(), writes=(), q="sp"):
        if self.capture is not None:
            self.capture.append(("dma", q, fn, tuple(reads), tuple(writes)))
            return
        if self.limit is not None and self.n_ops >= self.limit:
            return
        if self.log is not None:
            import sys as _s
            self.log.append((self.n_ops, "dma-" + q, _s._getframe(2).f_lineno))
        if q == "sp":
            slot = self.slot_sp % 16
            self.slot_sp += 1
        else:
            slot = 16 + self.slot_pl % (NSLOT - 16)
            self.slot_pl += 1
        src = "q%d" % slot
        prev = self.slot_cnt[slot]
        deps = self._deps(reads, writes)
        if prev:
            deps[src] = max(deps.get(src, 0), prev)
        ws = self._waits(q, deps)
        self.slot_cnt[slot] = prev + 1
        self.ops[q].append((ws, fn, (src, 16)))
        self._mark(reads, writes, src, prev + 1)
        self.n_ops += 1

    def replay(self, items, n=None):
        cnt = 0
        while items and (n is None or cnt < n):
            kind, e, fn, rd, wr = items.pop(0)
            if kind == "op":
                self.op(e, fn, rd, wr)
            else:
                self.dma(fn, rd, wr, q=e)
            cnt += 1

    def barrier(self):
        tot = dict(self.count)
        for s in range(NSLOT):
            if self.slot_cnt[s]:
                tot["q%d" % s] = self.slot_cnt[s]
        for e in self.ENGS:
            ws = self._waits(e, {k: v for k, v in tot.items() if v > 0})
            if ws:
                self.ops[e].append((ws, None, None))
        self.wr = {}
        self.rd = {}

    def emit(self):
        nc = self.nc
        EP = 12000
        with contextlib.ExitStack() as st:
            for e in self.ENGS:
                ne = max(1, (self.count[e] + EP - 1) // EP)
                self.sems[e] = [st.enter_context(nc.semaphore("s_%s%d" % (e, i))) for i in range(ne)]
            for s in range(NSLOT):
                self.sems["q%d" % s] = st.enter_context(nc.semaphore("s_q%d" % s))
            block = st.enter_context(nc.Block())
            sems = self.sems

            def run(engname):
                def body(e):
                    n_done = 0
                    for ws, fn, inc in self.ops[engname]:
                        for src, val in ws:
                            if src[0] == "q":
                                e.wait_ge(sems[src], val * 16)
                            else:
                                e.wait_ge(sems[src][(val - 1) // EP], (val - 1) % EP + 1)
                        if fn is not None:
                            if inc[0][0] == "q":
                                fn(e).then_inc(sems[inc[0]], 16)
                            else:
                                fn(e).then_inc(sems[engname][n_done // EP], 1)
                                n_done += 1
                return body
            block.tensor(run("pe"))
            block.scalar(run("act"))
            block.vector(run("dve"))
            block.gpsimd(run("pool"))
            block.sync(run("sp"))


def _nm(aps):
    out = []
    for a in aps:
        if a is None or isinstance(a, (int, float)):
            continue
        out.append(a.name)
    return out


class KB:
    def __init__(self, nc):
        self.nc = nc
        self.S = Sched(nc)
        self.ptr = SB_BASE
        self.uid = 0
        self.ps = [nc.alloc_psum_tensor("psb%d" % i, [128, 512], F32) for i in range(8)]
        self.ps_i = 0
        self.pool = [0, 1, 2, 3, 4, 5]
        self.started = {}
        self.acc_i = 0

    def sb(self, shape, dt=F32, name="t"):
        per = 1
        for s in shape[1:]:
            per *= s
        nbytes = per * (4 if dt == F32 else 2)
        nbytes = (nbytes + 63) // 64 * 64
        self.uid += 1
        assert self.ptr + nbytes <= SB_TOP, ("SBUF overflow", name, self.ptr, nbytes)
        t = self.nc.alloc_sbuf_tensor_at("%s_%d" % (name, self.uid), list(shape), dt, offset=self.ptr)
        self.ptr += nbytes
        return t

    def mark(self):
        return self.ptr

    def release(self, mark):
        self.S.barrier()
        self.ptr = mark

    def bank(self):
        b = self.ps[self.pool[self.ps_i % len(self.pool)]]
        self.ps_i += 1
        self.started[b[:, :].name] = set()
        return b

    def acc(self):
        b = self.ps[6 + self.acc_i % 2]
        self.acc_i += 1
        self.started[b[:, :].name] = set()
        return b

    def dma(self, out, in_, q="sp", cast=False):
        if cast:
            q = "pool"
        self.S.dma(lambda e: e.dma_start(out=out, in_=in_, allow_slow_non_contiguous=True), reads=_nm([in_]), writes=_nm([out]), q=q)

    def mm(self, out, lhsT, rhs, start=True, stop=True):
        p0 = out.base_partition()
        p1 = p0 + out.shape[0]
        quads = set(range(p0 // 32, (p1 - 1) // 32 + 1))
        st = self.started[out.name]
        fresh = quads - st
        assert len(fresh) == 0 or len(fresh) == len(quads), ("mixed quadrant start", out.name, quads, st)
        hw_start = len(fresh) > 0
        st |= quads
        self.S.op("pe", lambda e: e.matmul(out, lhsT=lhsT, rhs=rhs, start=hw_start, stop=stop, skip_group_check=True),
                  reads=_nm([lhsT, rhs]) + ([] if start else _nm([out])), writes=_nm([out]))

    def tr(self, out, in_, ident):
        self.S.op("pe", lambda e: e.transpose(out, in_, ident), reads=_nm([in_, ident]), writes=_nm([out]))

    def act(self, out, in_, func, bias=0.0, scale=1.0, accum=None):
        kw = {}
        if accum is not None:
            kw["accum_out"] = accum
        self.S.op("act", lambda e: e.activation(out=out, in_=in_, func=func, bias=bias, scale=scale, **kw),
                  reads=_nm([in_, bias, scale]), writes=_nm([out, accum]))

    def copy(self, eng, out, in_):
        if eng == "act":
            self.S.op("act", lambda e: e.copy(out=out, in_=in_), reads=_nm([in_]), writes=_nm([out]))
        else:
            self.S.op(eng, lambda e: e.tensor_copy(out=out, in_=in_), reads=_nm([in_]), writes=_nm([out]))

    def tt(self, eng, out, in0, in1, op):
        self.S.op(eng, lambda e: e.tensor_tensor(out=out, in0=in0, in1=in1, op=op),
                  reads=_nm([in0, in1]), writes=_nm([out]))

    def ts(self, eng, out, in0, s1, op0, s2=None, op1=None):
        if op1 is None:
            self.S.op(eng, lambda e: e.tensor_scalar(out=out, in0=in0, scalar1=s1, scalar2=None, op0=op0),
                      reads=_nm([in0, s1]), writes=_nm([out]))
        else:
            self.S.op(eng, lambda e: e.tensor_scalar(out=out, in0=in0, scalar1=s1, scalar2=s2, op0=op0, op1=op1),
                      reads=_nm([in0, s1, s2]), writes=_nm([out]))

    def stt(self, eng, out, in0, scalar, in1, op0, op1):
        self.S.op(eng, lambda e: e.scalar_tensor_tensor(out=out, in0=in0, scalar=scalar, in1=in1, op0=op0, op1=op1),
                  reads=_nm([in0, scalar, in1]), writes=_nm([out]))

    def memset(self, eng, out, val):
        self.S.op(eng, lambda e: e.memset(out, val), writes=_nm([out]))

    def recip(self, out, in_):
        self.S.op("dve", lambda e: e.reciprocal(out=out, in_=in_), reads=_nm([in_]), writes=_nm([out]))

    def scan(self, out, d0, d1, init, op0, op1):
        self.S.op("dve", lambda e: e.tensor_tensor_scan(out=out, data0=d0, data1=d1, initial=init, op0=op0, op1=op1),
                  reads=_nm([d0, d1, init]), writes=_nm([out]))

    def reduce(self, out, in_, op):
        self.S.op("dve", lambda e: e.tensor_reduce(out=out, in_=in_, axis=mybir.AxisListType.X, op=op),
                  reads=_nm([in_]), writes=_nm([out]))


def host_consts():
    c = {}
    c["ident"] = np.eye(128, dtype=np.float32)
    bo = np.zeros((128, 128), np.float32)
    bo[:64, :64] = 1.0
    bo[64:, 64:] = 1.0
    c["blockones"] = bo
    j = np.arange(64)[:, None]
    i = np.arange(64)[None, :]
    m = np.zeros((64, 2, 256), np.float32)
    for d in range(2):
        strict = (j < i) if d == 0 else (j > i)
        incl = (j <= i) if d == 0 else (j >= i)
        m[:, d, 0:64] = strict
        m[:, d, 64:128] = incl
        m[:, d, 128:192] = strict
        m[:, d, 192:256] = incl
    c["rmask"] = m
    m2 = np.zeros((64, 2, 64), np.float32)
    m2[:, 0, :] = (i.T > j.T)
    ii = np.arange(64)[:, None]
    jj = np.arange(64)[None, :]
    m2[:, 0, :] = (jj < ii)
    m2[:, 1, :] = (jj > ii)
    c["nmask"] = m2
    rows = TL // 64
    row = np.repeat(np.arange(rows, dtype=np.float32), 64)
    col = np.tile(np.arange(64, dtype=np.float32), rows)
    nf = 8
    inv = np.power(np.float32(10000.0), -np.arange(nf, dtype=np.float32) / nf).astype(np.float32)
    ang = np.concatenate([row[:, None] * inv, col[:, None] * inv], axis=-1).astype(np.float32)
    cosf = np.ones((96, T), np.float32)
    sinf = np.zeros((96, T), np.float32)
    cosf[64:80, :TL] = np.cos(ang).T
    cosf[80:96, :TL] = np.cos(ang).T
    sinf[64:80, :TL] = np.sin(ang).T
    sinf[80:96, :TL] = np.sin(ang).T
    c["cosf"] = cosf
    c["sinf"] = sinf
    rm = np.zeros((96, 96), np.float32)
    for t in range(16):
        rm[80 + t, 64 + t] = -1.0
        rm[64 + t, 80 + t] = 1.0
    c["rotm"] = rm
    jj = np.arange(128)[:, None]
    ii = np.arange(512)[None, :]
    mm = np.zeros((128, 2, 4, 512), np.float32)
    for r in range(4):
        mm[:, 0, r, :] = (jj + 128 * r) <= ii
        mm[:, 1, r, :] = (jj + 128 * r) >= ii
    c["lmask"] = mm
    sel = np.zeros((8, 8, 128), np.float32)
    for k in range(8):
        sel[k, k, :] = 1.0
    c["sel8"] = sel
    dcol = np.zeros((8, 2), np.float32)
    dcol[0:4, 0] = 1.0
    dcol[4:8, 0] = -1.0
    dcol[4:8, 1] = 1.0
    c["dircols"] = dcol
    return c


PARAMS = ["mod_w", "mod_b", "norm1_g", "norm2_g", "w_in", "w_out", "ffn_w_in", "ffn_w_out",
          "rwkv_mu", "rwkv_w0", "rwkv_w2", "rwkv_a0", "rwkv_a2", "rwkv_g2", "rwkv_k_k", "rwkv_k_a", "rwkv_r_k",
          "rwkv_ln_g", "rwkv_ln_b", "mla_q_norm_g", "mla_w_uq", "mla_kv_norm_g", "mla_w_ukv",
          "mla_q_qknorm_g", "mla_k_qknorm_g", "mlstm_conv_w", "mlstm_conv_b", "mlstm_i_b", "mlstm_f_b",
          "mlstm_norm_g"]


def build(shapes, dbg=(), stop_after=None, layers=DEPTH):
    nc = bass.Bass("TRN2", target_bir_lowering=False)
    I = {}
    for name, shp in shapes.items():
        I[name] = nc.dram_tensor(name, list(shp), F32, kind="ExternalInput").ap()

    _scr = {}

    def scratch(name, shape, dt=F32):
        if name not in _scr:
            kind = "ExternalOutput" if name in dbg else "Internal"
            _scr[name] = nc.dram_tensor(name, list(shape), dt, kind=kind).ap()
        return _scr[name]

    out = nc.dram_tensor("out", [TL, D], F32, kind="ExternalOutput").ap()
    xTa = scratch("xTa", [D, T])
    xTb = scratch("xTb", [D, T])
    zT = scratch("zT", [IN_COLS, T])
    mixT = scratch("mixT", [D, T], BF16)
    ydir = scratch("ydir", [2, 256, T])
    rkvg = scratch("rkvg", [4, 256, T])
    k = KB(nc)
    S = k.S

    ident = k.sb([128, 128], F32, "ident")
    identb = k.sb([128, 128], BF16, "identb")
    bones = k.sb([128, 128], F32, "bones")
    onesb = k.sb([128, 128], BF16, "onesb")
    onesf = k.sb([128, 128], F32, "onesf")
    MOD = k.sb([128, 48, 2], F32, "MOD")
    A1 = k.sb([128, 8, 2], F32, "A1")
    A2 = k.sb([128, 8, 2], F32, "A2")
    epsc = k.sb([128, 1], F32, "epsc")
    k.dma(ident[:, :], I["ident"][:, :])
    k.dma(bones[:, :], I["blockones"][:, :])
    k.copy("dve", identb[:, :], ident[:, :])
    k.memset("dve", onesb[:, :], 1.0)
    k.memset("dve", onesf[:, :], 1.0)
    k.memset("dve", epsc[:, :], EPS)
    base_mark = k.mark()

    def xview(ap):
        return ap.rearrange("(f p) t -> p f t", p=128)

    def phase_A():
        xin = [k.sb([128, D], F32, "xin") for _ in range(2)]
        stage = [k.sb([128, 8, 512], F32, "stg") for _ in range(2)]
        n_t = 0
        for ci, (t0, n) in enumerate(CH):
            st = stage[ci % 2]
            for j in range(n // 128):
                xi = xin[n_t % 2]
                n_t += 1
                if t0 < TL:
                    k.dma(xi[:, :], I["x"][t0 + j * 128:t0 + (j + 1) * 128, :])
                else:
                    k.dma(xi[:, :], I["ctx"][j * 128:(j + 1) * 128, :])
                for half in range(2):
                    pb = k.bank()
                    for f in range(4):
                        k.tr(pb[:, f * 128:(f + 1) * 128], xi[:, (half * 4 + f) * 128:(half * 4 + f + 1) * 128], ident[:, :])
                    k.copy("act" if half == 0 else "dve", st[:, half * 4:half * 4 + 4, j * 128:(j + 1) * 128],
                           pb[:, :].rearrange("p (a b) -> p a b", a=4))
            k.dma(xview(xTa)[:, :, t0:t0 + n], st[:, :, 0:n], q="pool")

    def phase_M(li):
        cc = k.sb([128, 8, 2], F32, "cc")
        modb = k.sb([128, 48], F32, "modb")
        g1 = k.sb([128, 8], F32, "g1")
        g2 = k.sb([128, 8], F32, "g2")
        wst = [k.sb([128, 6144], F32, "wst") for _ in range(2)]
        tmp = k.sb([128, 8, 2], F32, "tmpm")
        with nc.allow_non_contiguous_dma(reason="tiny column-layout parameter loads"):
            k.dma(cc[:, :, 0], I["c"].rearrange("(f p) -> p f", p=128))
            k.dma(cc[:, :, 1], I["c_ctx"].rearrange("(f p) -> p f", p=128))
            k.dma(modb[:, :], I["mod_b"][li].rearrange("(o p) -> p o", p=128))
            k.dma(g1[:, :], I["norm1_g"][li].rearrange("(f p) -> p f", p=128))
            k.dma(g2[:, :], I["norm2_g"][li].rearrange("(f p) -> p f", p=128))
        k.act(cc[:, :, :], cc[:, :, :], AF.Silu)
        pb = k.bank()
        for kt in range(8):
            w = wst[kt % 2]
            k.dma(w[:, :], I["mod_w"][li, kt * 128:(kt + 1) * 128, :])
            for o in range(48):
                k.mm(pb[:, 2 * o:2 * o + 2], w[:, o * 128:(o + 1) * 128], cc[:, kt, :], start=(kt == 0), stop=(kt == 7))
        k.tt("dve", MOD[:, :, :], pb[:, 0:96].rearrange("p (o s) -> p o s", s=2),
             modb[:, :].unsqueeze(2).to_broadcast([128, 48, 2]), ALU.add)
        k.ts("dve", tmp[:, :, :], MOD[:, 8:16, :], 1.0, ALU.add)
        k.tt("dve", A1[:, :, :], tmp[:, :, :], g1[:, :].unsqueeze(2).to_broadcast([128, 8, 2]), ALU.mult)
        k.ts("dve", tmp[:, :, :], MOD[:, 32:40, :], 1.0, ALU.add)
        k.tt("dve", A2[:, :, :], tmp[:, :, :], g2[:, :].unsqueeze(2).to_broadcast([128, 8, 2]), ALU.mult)

    def norm_chunk(xc, n, s, A, sh_off, xm_out, sq, rstd, tmp):
        k.act(sq[:, 0:8, 0:n], xc[:, :, 0:n], AF.Square)
        pb = k.bank()
        for f in range(8):
            k.mm(pb[:, 0:n], onesb[:, :], sq[:, f, 0:n], start=(f == 0), stop=(f == 7))
        k.act(rstd[:, 0:n], pb[:, 0:n], AF.Sqrt, bias=epsc[:, 0:1], scale=1.0 / D)
        k.recip(rstd[:, 0:n], rstd[:, 0:n])
        for f in range(8):
            k.tt("dve" if f % 2 == 0 else "pool", tmp[:, f % 2, 0:n], xc[:, f, 0:n], rstd[:, 0:n], ALU.mult)
            k.act(xm_out[:, f, :], tmp[:, f % 2, 0:n], AF.Identity, bias=MOD[:, sh_off + f, s:s + 1], scale=A[:, f, s:s + 1])

    def phase_NP(li, xsrc):
        xm = k.sb([128, 8, T], BF16, "xm")
        wbf = k.sb([128, 8, IN_COLS], BF16, "wbf")
        k.dma(wbf[:, :, :], I["w_in"][li].rearrange("(f p) c -> p f c", p=128), cast=True)
        m1 = k.mark()
        xc = [k.sb([128, 8, 512], F32, "xc") for _ in range(2)]
        sq = k.sb([128, 8, 512], BF16, "sq")
        rstd = k.sb([128, 512], F32, "rstd")
        tmp = k.sb([128, 2, 512], F32, "tmpn")
        for ci, (t0, n) in enumerate(CH):
            x_ = xc[ci % 2]
            k.dma(x_[:, :, 0:n], xview(xsrc)[:, :, t0:t0 + n])
            norm_chunk(x_, n, 0 if t0 < TL else 1, A1, 0, xm[:, :, t0:t0 + n], sq, rstd, tmp)
        k.release(m1)
        zrow = [k.sb([128, T], F32, "zrow") for _ in range(2)]
        nct = (IN_COLS + 127) // 128
        ev = 0
        for ct in range(nct):
            c0 = ct * 128
            cw = min(128, IN_COLS - c0)
            zr = zrow[ct % 2]
            for (t0, n) in CH:
                pb = k.bank()
                for f in range(8):
                    k.mm(pb[0:cw, 0:n], wbf[:, f, c0:c0 + cw], xm[:, f, t0:t0 + n], start=(f == 0), stop=(f == 7))
                k.copy("act" if ev % 2 == 0 else "dve", zr[0:cw, t0:t0 + n], pb[0:cw, 0:n])
                ev += 1
            k.dma(zT[c0:c0 + cw, :], zr[0:cw, :], q="pool")

    def phase_O(li, xsrc, xdst):
        wo = k.sb([128, 8, D], BF16, "wo")
        k.dma(wo[:, :, :], I["w_out"][li].rearrange("(f p) c -> p f c", p=128), cast=True)
        mx = [k.sb([128, 8, 512], BF16, "mx") for _ in range(2)]
        xc = [k.sb([128, 8, 512], F32, "xco") for _ in range(2)]
        xo = [k.sb([128, 8, 512], F32, "xoo") for _ in range(2)]
        for ci, (t0, n) in enumerate(CH):
            s = 0 if t0 < TL else 1
            m_ = mx[ci % 2]
            x_ = xc[ci % 2]
            o_ = xo[ci % 2]
            k.dma(m_[:, :, 0:n], xview(mixT)[:, :, t0:t0 + n])
            k.dma(x_[:, :, 0:n], xview(xsrc)[:, :, t0:t0 + n])
            for of in range(8):
                pb = k.bank()
                for f in range(8):
                    k.mm(pb[:, 0:n], wo[:, f, of * 128:(of + 1) * 128], m_[:, f, 0:n], start=(f == 0), stop=(f == 7))
                k.stt("dve", o_[:, of, 0:n], pb[:, 0:n], MOD[:, 16 + of, s:s + 1], x_[:, of, 0:n], ALU.mult, ALU.add)
            k.dma(xview(xdst)[:, :, t0:t0 + n], o_[:, :, 0:n], q="pool")

    def phase_F(li, xsrc, xdst, final):
        w1 = k.sb([128, 8, 2 * FFN_H], BF16, "w1")
        w2 = k.sb([128, 22, D], BF16, "w2")
        k.dma(w1[:, :, :], I["ffn_w_in"][li].rearrange("(f p) c -> p f c", p=128), cast=True)
        k.dma(w2[:, :, :], I["ffn_w_out"][li].rearrange("(f p) c -> p f c", p=128), cast=True)
        xc = [k.sb([128, 8, 512], F32, "xcf")] * 2
        rstd = k.sb([128, 512], F32, "rstdf")
        tmp = k.sb([128, 2, 512], F32, "tmpf")
        xm = k.sb([128, 8, 512], BF16, "xmf")
        actT = k.sb([128, 22, 512], BF16, "actT")
        sq = actT
        sg = [tmp[:, 0, :], tmp[:, 1, :]]
        chunks = CH if not final else CH[:8]
        otb = [k.sb([128, D], F32, "otb") for _ in range(2)] if final else None
        for ci, (t0, n) in enumerate(chunks):
            s = 0 if t0 < TL else 1
            x_ = xc[ci % 2]
            k.dma(x_[:, :, 0:n], xview(xsrc)[:, :, t0:t0 + n])
            norm_chunk(x_, n, s, A2, 24, xm[:, :, 0:n], sq, rstd, tmp)
            for h in range(22):
                pg = k.bank()
                pu = k.bank()
                for f in range(8):
                    k.mm(pg[:, 0:n], w1[:, f, h * 128:(h + 1) * 128], xm[:, f, 0:n], start=(f == 0), stop=(f == 7))
                for f in range(8):
                    k.mm(pu[:, 0:n], w1[:, f, FFN_H + h * 128:FFN_H + (h + 1) * 128], xm[:, f, 0:n], start=(f == 0), stop=(f == 7))
                s_ = sg[h % 2]
                k.act(s_[:, 0:n], pg[:, 0:n], AF.Silu)
                k.tt("dve", actT[:, h, 0:n], s_[:, 0:n], pu[:, 0:n], ALU.mult)
            for of in range(8):
                pb = k.bank()
                for h in range(22):
                    k.mm(pb[:, 0:n], w2[:, h, of * 128:(of + 1) * 128], actT[:, h, 0:n], start=(h == 0), stop=(h == 21))
                k.stt("dve", x_[:, of, 0:n], pb[:, 0:n], MOD[:, 40 + of, s:s + 1], x_[:, of, 0:n], ALU.mult, ALU.add)
            if not final:
                k.dma(xview(xdst)[:, :, t0:t0 + n], x_[:, :, 0:n], q="pool")
            else:
                for j in range(n // 128):
                    ot = otb[j % 2]
                    for half in range(2):
                        pb = k.bank()
                        for f in range(4):
                            k.tr(pb[:, f * 128:(f + 1) * 128], x_[:, half * 4 + f, j * 128:(j + 1) * 128], ident[:, :])
                        k.copy("act" if half == 0 else "dve", ot[:, half * 512:(half + 1) * 512], pb[:, :])
                    k.dma(out[t0 + j * 128:t0 + (j + 1) * 128, :], ot[:, :], q="pool")


    def phase_MLA(li, ctx_out):
        wuq = k.sb([128, 4, 768], BF16, "wuq")
        wukv = k.sb([128, 2, 1024], BF16, "wukv")
        k.dma(wuq[:, :, :], I["mla_w_uq"][li].rearrange("(f p) c -> p f c", p=128), cast=True)
        k.dma(wukv[:, :, :], I["mla_w_ukv"][li].rearrange("(f p) c -> p f c", p=128), cast=True)
        gq = k.sb([128, 4], F32, "gq")
        gkv = k.sb([128, 2], F32, "gkv")
        gqk = k.sb([96, 2], F32, "gqk")
        rotm = k.sb([96, 96], F32, "rotm")
        with nc.allow_non_contiguous_dma(reason="tiny column-layout parameter loads"):
            k.dma(gq[:, :], I["mla_q_norm_g"][li].rearrange("(f p) -> p f", p=128))
            k.dma(gkv[:, :], I["mla_kv_norm_g"][li].rearrange("(f p) -> p f", p=128))
            k.dma(gqk[:, 0:1], I["mla_q_qknorm_g"][li].rearrange("(p o) -> p o", o=1))
            k.dma(gqk[:, 1:2], I["mla_k_qknorm_g"][li].rearrange("(p o) -> p o", o=1))
        k.dma(rotm[:, :], I["rotm"][:, :])
        k.ts("dve", gqk[:, 0:1], gqk[:, 0:1], float(96 ** -0.5), ALU.mult)
        QTd = scratch("QTd", [96, 8, T], BF16)
        qst = [k.sb([96, 512], BF16, "qst") for _ in range(2)]
        KT = k.sb([96, 8, T], BF16, "KT")
        V = k.sb([128, 34, 8, 65], BF16, "V")
        k.memset("pool", V[:, :, :, 64:65], 1.0)
        m1 = k.mark()
        cq = k.sb([128, 4, 512], F32, "cq")
        ckv = k.sb([128, 2, 512], F32, "ckv")
        krb = k.sb([96, 512], F32, "krb")
        cs = k.sb([96, 512], F32, "cs")
        sn = k.sb([96, 512], F32, "sn")
        sqb = k.sb([128, 4, 512], BF16, "sqb")
        rs = k.sb([128, 512], F32, "rs")
        cqn = k.sb([128, 4, 512], BF16, "cqn")
        ckvn = k.sb([128, 2, 512], BF16, "ckvn")
        NW = 8
        CHN = [dict(src=k.sb([96, 512], F32, "csrc"), qn=k.sb([96, 512], F32, "qn"), sq96=k.sb([96, 512], BF16, "sq96"),
                    rs96=k.sb([96, 512], F32, "rs96")) for _ in range(NW)]

        def norm_feat(src, nt, nfeat, gcol, dst, n):
            k.act(sqb[:, 0:nt, 0:n], src[:, 0:nt, 0:n], AF.Square)
            pb = k.bank()
            for f in range(nt):
                k.mm(pb[:, 0:n], onesb[:, :], sqb[:, f, 0:n], start=(f == 0), stop=(f == nt - 1))
            k.act(rs[:, 0:n], pb[:, 0:n], AF.Sqrt, bias=epsc[:, 0:1], scale=1.0 / nfeat)
            k.recip(rs[:, 0:n], rs[:, 0:n])
            for f in range(nt):
                k.stt("dve", dst[:, f, 0:n], src[:, f, 0:n], gcol[:, f:f + 1], rs[:, 0:n], ALU.mult, ALU.mult)

        zcq = zT[1152:1664, :].rearrange("(f p) t -> p f t", p=128)
        zckv = zT[1664:1920, :].rearrange("(f p) t -> p f t", p=128)
        for (t0, n) in CH:
            k.dma(cq[:, :, 0:n], zcq[:, :, t0:t0 + n])
            k.dma(ckv[:, :, 0:n], zckv[:, :, t0:t0 + n])
            k.dma(krb[64:96, 0:n], zT[1920:1952, t0:t0 + n])
            k.dma(cs[:, 0:n], I["cosf"][:, t0:t0 + n])
            k.dma(sn[:, 0:n], I["sinf"][:, t0:t0 + n])
            norm_feat(cq, 4, 512, gq, cqn, n)
            norm_feat(ckv, 2, 256, gkv, ckvn, n)
            def front(h0, base):
                wave = []
                for h in (h0, h0 + 1):
                    cq_ = CHN[base + len(wave)]
                    pq = k.bank()
                    for f in range(4):
                        k.mm(pq[0:96, 0:n], wuq[:, f, 96 * h:96 * h + 96], cqn[:, f, 0:n], start=(f == 0), stop=(f == 3))
                    k.copy("act", cq_["src"][:, 0:n], pq[0:96, 0:n])
                    wave.append((cq_, gqk[:, 0:1], qst[h % 2][:, 0:n], ("q", h)))
                    ck_ = CHN[base + len(wave)]
                    pk = k.bank()
                    for f in range(2):
                        k.mm(pk[0:64, 0:n], wukv[:, f, 128 * h:128 * h + 64], ckvn[:, f, 0:n], start=(f == 0), stop=(f == 1))
                    k.copy("act", ck_["src"][0:64, 0:n], pk[0:64, 0:n])
                    k.copy("pool", ck_["src"][64:96, 0:n], krb[64:96, 0:n])
                    wave.append((ck_, gqk[:, 1:2], KT[:, h, t0:t0 + n], ("k", h)))
                for (c_, g_, dst, tag) in wave:
                    k.act(c_["sq96"][:, 0:n], c_["src"][:, 0:n], AF.Square)
                for (c_, g_, dst, tag) in wave:
                    pb = k.bank()
                    k.mm(pb[0:96, 0:n], onesb[0:96, 0:96], c_["sq96"][0:96, 0:n])
                    k.act(c_["rs96"][:, 0:n], pb[0:96, 0:n], AF.Sqrt, bias=epsc[0:96, 0:1], scale=1.0 / 96)
                for (c_, g_, dst, tag) in wave:
                    k.recip(c_["rs96"][:, 0:n], c_["rs96"][:, 0:n])
                for (c_, g_, dst, tag) in wave:
                    k.stt("dve", c_["qn"][:, 0:n], c_["src"][:, 0:n], g_, c_["rs96"][:, 0:n], ALU.mult, ALU.mult)
                return wave

            def back(wave):
                for (c_, g_, dst, tag) in wave:
                    pr = k.bank()
                    k.mm(pr[0:96, 0:n], rotm[0:96, 0:96], c_["qn"][0:96, 0:n])
                    k.tt("dve", c_["rs96"][:, 0:n], pr[0:96, 0:n], sn[:, 0:n], ALU.mult)
                for (c_, g_, dst, tag) in wave:
                    k.tt("pool", c_["qn"][:, 0:n], c_["qn"][:, 0:n], cs[:, 0:n], ALU.mult)
                for (c_, g_, dst, tag) in wave:
                    k.tt("dve", dst, c_["qn"][:, 0:n], c_["rs96"][:, 0:n], ALU.add)
                    if tag[0] == "q":
                        k.dma(QTd[:, tag[1], t0:t0 + n], dst, q="pool")

            prev = None
            for wi in range(5):
                cur = front(2 * wi, 4 * (wi % 2)) if wi < 4 else None
                if prev is not None:
                    back(prev)
                prev = cur
            for j in range(n // 128):
                pv = k.bank()
                for f in range(2):
                    k.mm(pv[:, 0:512], ckvn[:, f, j * 128:(j + 1) * 128],
                         wukv[:, f, :].rearrange("p (h c) -> p h c", c=128)[:, :, 64:128], start=(f == 0), stop=(f == 1))
                k.copy("act", V[:, t0 // 128 + j, :, 0:64], pv[:, 0:512].rearrange("p (h c) -> p h c", c=64))
        k.release(m1)
        pT = [k.sb([128, 512], BF16, "pT") for _ in range(4)]
        Otok = [k.sb([128, 4, 512], F32, "Otok") for _ in range(2)]
        OT = [k.sb([128, 4, 512], BF16, "OT") for _ in range(2)]
        rden = k.sb([128, 8, 4], F32, "rden")
        QTc = [k.sb([96, 8, 512], BF16, "QTc") for _ in range(2)]
        mv = mixT.rearrange("(f p) t -> p f t", p=128)
        chunks = CH if ctx_out else CH[:8]
        import os as _os2
        if _os2.environ.get('SKIP_ATT'):
            chunks = []
        for ci, (t0, n) in enumerate(chunks):
            nq = n // 128
            keys = list(range(34)) if t0 < TL else [32, 33]
            ot = Otok[ci % 2]
            QT = QTc[ci % 2]
            k.dma(QT[:, :, 0:n], QTd[:, :, t0:t0 + n])
            for h in range(8):
                po = k.acc()
                AHEAD = 4
                psl = {}

                def issue_qk(ix, h=h, n=n, QT=QT, keys=keys, psl=psl):
                    b_ = k.bank()
                    kt_ = keys[ix]
                    k.mm(b_[:, 0:n], KT[0:96, h, kt_ * 128:(kt_ + 1) * 128], QT[0:96, h, 0:n])
                    psl[ix] = b_
                for ix in range(min(AHEAD, len(keys))):
                    issue_qk(ix)
                for idx, kt in enumerate(keys):
                    ps_ = psl.pop(idx)
                    p_ = pT[idx % 4]
                    k.act(p_[:, 0:n], ps_[:, 0:n], AF.Exp)
                    if idx + AHEAD < len(keys):
                        issue_qk(idx + AHEAD)
                    for qi in range(nq):
                        k.mm(po[:, qi * 65:(qi + 1) * 65], p_[:, qi * 128:(qi + 1) * 128], V[:, kt, h, :],
                             start=(idx == 0), stop=(idx == len(keys) - 1))
                pov = po[:, 0:nq * 65].rearrange("p (q c) -> p q c", c=65)
                k.recip(rden[:, h, 0:nq], pov[:, :, 64])
                for qi in range(nq):
                    k.ts("dve", ot[:, qi, h * 64:(h + 1) * 64], pov[:, qi, 0:64], rden[:, h, qi:qi + 1], ALU.mult)
            o2 = OT[ci % 2]
            for qi in range(nq):
                pb = k.bank()
                for f in range(4):
                    k.tr(pb[:, f * 128:(f + 1) * 128], ot[:, qi, f * 128:(f + 1) * 128], ident[:, :])
                k.copy("act", o2[:, :, qi * 128:(qi + 1) * 128], pb[:, :].rearrange("p (a b) -> p a b", a=4))
            k.dma(mv[:, 2:6, t0:t0 + n], o2[:, :, 0:n], q="pool")


    def phase_RW(li):
        zsT = scratch("zsT", [1152, T])
        Tp = T + 3
        C = 64
        mu = k.sb([128, 9], F32, "mu")
        muh = k.sb([128, 9], F32, "muh")
        omu = k.sb([128, 9], F32, "omu")
        k.dma(mu[:, :], I["rwkv_mu"][li].rearrange("(f p) -> p f", p=128))
        k.ts("dve", muh[:, :], mu[:, :], 0.5, ALU.mult)
        k.ts("dve", omu[:, :], mu[:, :], -1.0, ALU.mult, 1.0, ALU.add)
        m0 = k.mark()
        zpb = [k.sb([128, Tp], F32, "zp") for _ in range(2)]
        tb = [k.sb([128, Tp], F32, "tsh") for _ in range(2)]
        for rt in range(9):
            zp = zpb[rt % 2]
            tm = tb[rt % 2]
            k.memset("pool", zp[:, 0:1], 0.0)
            k.memset("pool", zp[:, TL + 1:TL + 2], 0.0)
            k.memset("pool", zp[:, Tp - 1:Tp], 0.0)
            k.dma(zp[:, 1:TL + 1], zT[rt * 128:(rt + 1) * 128, 0:TL])
            k.dma(zp[:, TL + 2:TL + 2 + TC], zT[rt * 128:(rt + 1) * 128, TL:T])
            k.tt("dve", tm[:, 1:Tp - 1], zp[:, 0:Tp - 2], zp[:, 2:Tp], ALU.add)
            k.ts("dve", tm[:, 1:Tp - 1], tm[:, 1:Tp - 1], muh[:, rt:rt + 1], ALU.mult)
            k.stt("dve", tm[:, 1:Tp - 1], zp[:, 1:Tp - 1], omu[:, rt:rt + 1], tm[:, 1:Tp - 1], ALU.mult, ALU.add)
            k.dma(zsT[rt * 128:(rt + 1) * 128, 0:TL], tm[:, 1:TL + 1], q="pool")
            k.dma(zsT[rt * 128:(rt + 1) * 128, TL:T], tm[:, TL + 2:TL + 2 + TC], q="pool")
        k.release(m0)
        kkc = k.sb([64, 4], F32, "kkc")
        kac = k.sb([64, 4], F32, "kac")
        omka = k.sb([64, 4], F32, "omka")
        w0c = k.sb([64, 2, 4], F32, "w0c")
        a0c = k.sb([64, 2, 4], F32, "a0c")
        w2d = [k.sb([64, 256], F32, "w2d") for _ in range(2)]
        a2d = [k.sb([64, 256], F32, "a2d") for _ in range(2)]
        g2sb = k.sb([128, 256], F32, "g2sb")
        cmask = k.sb([64, 512], F32, "cmask")
        rmask = k.sb([64, 2, 256], F32, "rmask")
        nmask = k.sb([64, 2, 64], F32, "nmask")
        ones64 = onesf[0:64, 0:64]
        id64 = ident[0:64, 0:64]
        k.dma(kkc[:, :], I["rwkv_k_k"][li].rearrange("(h p) -> p h", p=64))
        k.dma(kac[:, :], I["rwkv_k_a"][li].rearrange("(h p) -> p h", p=64))
        for d_ in range(2):
            k.dma(w0c[:, d_, :], I["rwkv_w0"][li, d_].rearrange("(h p) -> p h", p=64))
            k.dma(a0c[:, d_, :], I["rwkv_a0"][li, d_].rearrange("(h p) -> p h", p=64))
            k.dma(w2d[d_][:, :], I["rwkv_w2"][li, d_])
            k.dma(a2d[d_][:, :], I["rwkv_a2"][li, d_])
        k.dma(g2sb[:, :], I["rwkv_g2"][li])
        k.dma(rmask[:, :, :], I["rmask"])
        k.dma(nmask[:, :, :], I["nmask"])
        k.ts("dve", omka[:, :], kac[:, :], -1.0, ALU.mult, 1.0, ALU.add)
        k.memset("dve", cmask[:, :], 1.0)
        k.memset("dve", cmask[:, :].rearrange("p (c i) -> p c i", i=C)[:, :, 0:1], 0.0)
        m1 = k.mark()

        def v3(t, n):
            return t[:, 0:n].rearrange("p (c i) -> p c i", i=C)

        GN = 256
        names = ["r_", "k_", "v_", "nrm", "kk", "logw", "a_", "kt", "b_", "Gf", "G", "Gex", "E1", "E2", "E4"]
        Wt = [{nm: k.sb([64, GN], F32, nm) for nm in names if nm != "v_"} for _ in range(2)]
        Vh = [[k.sb([64, GN], F32, "v_") for _ in range(4)] for _ in range(2)]
        AR = [[k.sb([64, GN // C, 2, C], F32, "AR") for _ in range(4)] for _ in range(2)]
        BK = [[k.sb([64, GN // C, 2, C], F32, "BK") for _ in range(4)] for _ in range(2)]
        BbT = [[k.sb([64, GN], F32, "BbT") for _ in range(4)] for _ in range(2)]
        KbT = [[k.sb([64, GN], F32, "KbT") for _ in range(4)] for _ in range(2)]
        gam = [[k.sb([64, GN // C], F32, "gam") for _ in range(4)] for _ in range(2)]
        yst = [[k.sb([64, 2, GN], F32, "yst") for _ in range(2)] for _ in range(2)]
        Sst = [k.sb([64, 2, 64], F32, "Sst") for _ in range(2)]
        wdd = k.sb([64, GN], F32, "wdd")
        add_ = k.sb([64, GN], F32, "add")
        gd = k.sb([128, GN], F32, "gd")
        gbuf = [k.sb([64, GN], F32, "gbuf") for _ in range(2)]
        NSL = 4
        SL = [dict(bkv=k.sb([64, 2, 3, C], F32, "BKV"), npmq=k.sb([64, 2, 256], F32, "NPMQ"), n0=k.sb([64, 2, 64], F32, "N0"),
                   nn=[k.sb([64, 2, 2, 64], F32, "NN") for _ in range(2)], tt=[k.sb([64, 2, 64], F32, "TT") for _ in range(2)],
                   xs=k.sb([64, 2, 64], F32, "Xs"), us=k.sb([64, 2, 64], F32, "Us")) for _ in range(NSL)]
        NEG = -float(np.exp(-0.5))

        def emit_prep(d, t0, n, par):
            ARg, BKg, BbTg, KbTg, gamg = AR[par], BK[par], BbT[par], KbT[par], gam[par]
            Wg = {h: dict(Wt[h % 2], v_=Vh[par][h]) for h in range(4)}
            ncnk = n // C
            k.dma(wdd[:, 0:n], zsT[768 + 64 * d:768 + 64 * d + 64, t0:t0 + n])
            k.dma(add_[:, 0:n], zsT[896 + 64 * d:896 + 64 * d + 64, t0:t0 + n])
            k.act(wdd[:, 0:n], wdd[:, 0:n], AF.Tanh)
            if d == 0:
                k.dma(gd[:, 0:n], zsT[1024:1152, t0:t0 + n])
                k.act(gd[:, 0:n], gd[:, 0:n], AF.Sigmoid)
            for h in range(4):
                w = Wg[h]
                hs = slice(h * 64, (h + 1) * 64)
                k.dma(w["r_"][:, 0:n], zsT[h * 64:(h + 1) * 64, t0:t0 + n])
                k.dma(w["k_"][:, 0:n], zsT[256 + h * 64:256 + (h + 1) * 64, t0:t0 + n])
                k.dma(w["v_"][:, 0:n], zsT[512 + h * 64:512 + (h + 1) * 64, t0:t0 + n])
                if d == 0:
                    pg = k.bank()
                    k.mm(pg[0:64, 0:n], g2sb[:, hs], gd[:, 0:n])
                    k.copy("act", gbuf[h % 2][:, 0:n], pg[0:64, 0:n])
                    k.dma(rkvg[3, hs, t0:t0 + n], gbuf[h % 2][:, 0:n], q="pool")
                k.ts("dve", w["kk"][:, 0:n], w["k_"][:, 0:n], kkc[:, h:h + 1], ALU.mult)
                k.act(w["nrm"][:, 0:n], w["kk"][:, 0:n], AF.Square)
                pss = k.bank()
                k.mm(pss[0:64, 0:n], ones64, w["nrm"][:, 0:n])
                k.act(w["nrm"][:, 0:n], pss[0:64, 0:n], AF.Sqrt)
                k.ts("dve", w["nrm"][:, 0:n], w["nrm"][:, 0:n], 1e-12, ALU.max)
                k.recip(w["nrm"][:, 0:n], w["nrm"][:, 0:n])
                k.tt("dve", w["kk"][:, 0:n], w["kk"][:, 0:n], w["nrm"][:, 0:n], ALU.mult)
                pw = k.bank()
                k.mm(pw[0:64, 0:n], w2d[d][:, hs], wdd[:, 0:n])
                k.act(w["logw"][:, 0:n], pw[0:64, 0:n], AF.Sigmoid, bias=w0c[:, d, h:h + 1])
                k.ts("dve", w["logw"][:, 0:n], w["logw"][:, 0:n], NEG, ALU.mult)
                pa = k.bank()
                k.mm(pa[0:64, 0:n], a2d[d][:, hs], add_[:, 0:n])
                k.act(w["a_"][:, 0:n], pa[0:64, 0:n], AF.Sigmoid, bias=a0c[:, d, h:h + 1])
                k.act(w["kt"][:, 0:n], w["a_"][:, 0:n], AF.Identity, bias=omka[:, h:h + 1], scale=kac[:, h:h + 1])
                k.tt("pool", w["kt"][:, 0:n], w["k_"][:, 0:n], w["kt"][:, 0:n], ALU.mult)
                k.tt("pool", w["b_"][:, 0:n], w["kk"][:, 0:n], w["a_"][:, 0:n], ALU.mult)
                k.scan(w["Gf"][:, 0:n], cmask[:, 0:n], w["logw"][:, 0:n], 0.0, ALU.mult, ALU.add)
                totb = v3(w["Gf"], n)[:, :, C - 1:C].to_broadcast([64, ncnk, C])
                if d == 0:
                    G = w["Gf"]
                    k.tt("dve", w["Gex"][:, 0:n], w["Gf"][:, 0:n], w["logw"][:, 0:n], ALU.subtract)
                else:
                    G = w["G"]
                    k.tt("dve", v3(w["Gex"], n), totb, v3(w["Gf"], n), ALU.subtract)
                    k.tt("dve", w["G"][:, 0:n], w["Gex"][:, 0:n], w["logw"][:, 0:n], ALU.add)
                k.act(w["E1"][:, 0:n], G[:, 0:n], AF.Exp)
                k.act(w["E2"][:, 0:n], G[:, 0:n], AF.Exp, scale=-1.0)
                k.act(w["Gex"][:, 0:n], w["Gex"][:, 0:n], AF.Exp)
                k.tt("dve", v3(w["E4"], n), totb, v3(G, n), ALU.subtract)
                k.act(w["E4"][:, 0:n], w["E4"][:, 0:n], AF.Exp)
                k.act(gamg[h][:, 0:ncnk], v3(w["Gf"], n)[:, :, C - 1], AF.Exp)
                k.stt("dve", ARg[h][:, 0:ncnk, 0, :], v3(w["kk"], n), -1.0, v3(w["Gex"], n), ALU.mult, ALU.mult)
                k.tt("pool", ARg[h][:, 0:ncnk, 1, :], v3(w["r_"], n), v3(w["E1"], n), ALU.mult)
                k.tt("dve", BKg[h][:, 0:ncnk, 0, :], v3(w["b_"], n), v3(w["E2"], n), ALU.mult)
                k.tt("pool", BKg[h][:, 0:ncnk, 1, :], v3(w["kt"], n), v3(w["E2"], n), ALU.mult)
                k.tt("dve", BbTg[h][:, 0:n], w["b_"][:, 0:n], w["E4"][:, 0:n], ALU.mult)
                k.tt("pool", KbTg[h][:, 0:n], w["kt"][:, 0:n], w["E4"][:, 0:n], ALU.mult)

        def emit_chain(d, t0, n, par, drip):
            ARg, BKg, BbTg, KbTg, gamg, ystg = AR[par], BK[par], BbT[par], KbT[par], gam[par], yst[par]
            Wg = {h: dict(Wt[h % 2], v_=Vh[par][h]) for h in range(4)}
            ncnk = n // C
            order = list(range(ncnk)) if d == 0 else list(range(ncnk - 1, -1, -1))
            chains = [(c, hp) for c in order for hp in range(2)]
            for w0 in range(0, len(chains), NSL):
                wave = chains[w0:w0 + NSL]
                for si, (c, hp) in enumerate(wave):
                    sl = SL[si]
                    cs_ = slice(c * C, (c + 1) * C)
                    pT_ = k.bank()
                    for hh in range(2):
                        h = 2 * hp + hh
                        k.tr(pT_[0:64, hh * 192:hh * 192 + 64], BbTg[h][:, cs_], id64)
                        k.tr(pT_[0:64, hh * 192 + 64:hh * 192 + 128], KbTg[h][:, cs_], id64)
                        k.tr(pT_[0:64, hh * 192 + 128:hh * 192 + 192], Wg[h]["v_"][:, cs_], id64)
                    k.copy("act", sl["bkv"][:, :, :, :], pT_[0:64, 0:384].rearrange("p (h a b) -> p h a b", h=2, a=3))
                    drip()
                for si, (c, hp) in enumerate(wave):
                    sl = SL[si]
                    pA = k.bank()
                    pN = k.bank()
                    for hh in range(2):
                        h = 2 * hp + hh
                        arf = ARg[h][:, c, :, :].rearrange("p a b -> p (a b)")
                        k.mm(pA[0:64, hh * 256:hh * 256 + 128], BKg[h][:, c, 0, :], arf)
                        k.mm(pA[0:64, hh * 256 + 128:hh * 256 + 256], BKg[h][:, c, 1, :], arf)
                        k.mm(pN[0:64, hh * 64:(hh + 1) * 64], ARg[h][:, c, 0, :], BKg[h][:, c, 0, :])
                    npmq = sl["npmq"]
                    n0 = sl["n0"]
                    k.tt("dve", npmq[:, :, :], pA[0:64, 0:512].rearrange("p (h c) -> p h c", h=2),
                         rmask[:, d:d + 1, :].to_broadcast([64, 2, 256]), ALU.mult)
                    k.tt("dve", n0[:, :, :], pN[0:64, 0:128].rearrange("p (h c) -> p h c", h=2),
                         nmask[:, d:d + 1, :].to_broadcast([64, 2, 64]), ALU.mult)
                    k.tt("pool", sl["tt"][0][:, :, :], npmq[:, :, 0:64], id64.unsqueeze(1).to_broadcast([64, 2, 64]), ALU.add)
                    sl["ttc"] = sl["tt"][0]
                    sl["ncur"] = [n0[:, 0, :], n0[:, 1, :]]
                    sl["ntcur"] = [npmq[:, 0, 0:64], npmq[:, 1, 0:64]]
                    drip()
                for p in range(5):
                    for si, (c, hp) in enumerate(wave):
                        sl = SL[si]
                        pC = k.bank()
                        for hh in range(2):
                            k.mm(pC[0:64, hh * 128:hh * 128 + 64], sl["ntcur"][hh], sl["ncur"][hh])
                            k.mm(pC[0:64, hh * 128 + 64:hh * 128 + 128], sl["ncur"][hh], sl["ntcur"][hh])
                        nn = sl["nn"][p % 2]
                        k.copy("act", nn[:, :, :, :], pC[0:64, 0:256].rearrange("p (h a c) -> p h a c", h=2, a=2))
                        sl["ncur"] = [nn[:, 0, 0, :], nn[:, 1, 0, :]]
                        sl["ntcur"] = [nn[:, 0, 1, :], nn[:, 1, 1, :]]
                        drip()
                    for si, (c, hp) in enumerate(wave):
                        sl = SL[si]
                        pD = k.bank()
                        for hh in range(2):
                            k.mm(pD[0:64, hh * 64:(hh + 1) * 64], sl["ncur"][hh], sl["ttc"][:, hh, :])
                        tn = sl["tt"][(p + 1) % 2]
                        k.tt("dve", tn[:, :, :], sl["ttc"][:, :, :], pD[0:64, 0:128].rearrange("p (h c) -> p h c", h=2), ALU.add)
                        sl["ttc"] = tn
                        drip()
                for si, (c, hp) in enumerate(wave):
                    sl = SL[si]
                    cs_ = slice(c * C, (c + 1) * C)
                    npmq = sl["npmq"]
                    bkv = sl["bkv"]
                    tt_ = sl["ttc"]
                    S_ = Sst[hp]
                    pX = k.bank()
                    for hh in range(2):
                        h = 2 * hp + hh
                        k.mm(pX[0:64, hh * 64:(hh + 1) * 64], ARg[h][:, c, 0, :], S_[:, hh, :], start=True, stop=False)
                        k.mm(pX[0:64, hh * 64:(hh + 1) * 64], npmq[:, hh, 128:192], bkv[:, hh, 2, :], start=False, stop=True)
                    k.copy("act", sl["xs"][:, :, :], pX[0:64, 0:128].rearrange("p (h c) -> p h c", h=2))
                    pU = k.bank()
                    for hh in range(2):
                        k.mm(pU[0:64, hh * 64:(hh + 1) * 64], tt_[:, hh, :], sl["xs"][:, hh, :])
                    k.copy("act", sl["us"][:, :, :], pU[0:64, 0:128].rearrange("p (h c) -> p h c", h=2))
                    drip()
                    pY = k.bank()
                    pS = k.bank()
                    for hh in range(2):
                        h = 2 * hp + hh
                        ysl = pY[0:64, hh * 64:(hh + 1) * 64]
                        k.mm(ysl, S_[:, hh, :], ARg[h][:, c, 1, :], start=True, stop=False)
                        k.mm(ysl, sl["us"][:, hh, :], npmq[:, hh, 64:128], start=False, stop=False)
                        k.mm(ysl, bkv[:, hh, 2, :], npmq[:, hh, 192:256], start=False, stop=True)
                        ssl = pS[0:64, hh * 64:(hh + 1) * 64]
                        k.mm(ssl, bkv[:, hh, 0, :], sl["us"][:, hh, :], start=True, stop=False)
                        k.mm(ssl, bkv[:, hh, 1, :], bkv[:, hh, 2, :], start=False, stop=True)
                    k.copy("act", ystg[hp][:, :, cs_], pY[0:64, 0:128].rearrange("p (h c) -> p h c", h=2))
                    for hh in range(2):
                        h = 2 * hp + hh
                        k.stt("dve", S_[:, hh, :], S_[:, hh, :], gamg[h][:, c:c + 1], pS[0:64, hh * 64:(hh + 1) * 64], ALU.mult, ALU.add)
            for hp in range(2):
                k.dma(ydir[d, hp * 128:(hp + 1) * 128, t0:t0 + n].rearrange("(h v) t -> v h t", v=64), ystg[hp][:, :, 0:n], q="pool")

        for d in range(2):
            for hp in range(2):
                k.memset("dve", Sst[hp][:, :, :], 0.0)
            lat = [(i * GN, GN) for i in range(TL // GN)]
            groups = [(TL, TC)] + (lat if d == 0 else lat[::-1])
            k.pool = [4, 5]
            emit_prep(d, groups[0][0], groups[0][1], 0)
            for gi, (t0, n) in enumerate(groups):
                pend = []
                if gi + 1 < len(groups):
                    S.capture = pend
                    k.pool = [4, 5]
                    emit_prep(d, groups[gi + 1][0], groups[gi + 1][1], (gi + 1) % 2)
                    S.capture = None
                k.pool = [0, 1, 2, 3, 6, 7]
                emit_chain(d, t0, n, gi % 2, lambda: S.replay(pend, 3))
                S.replay(pend)
        k.pool = [0, 1, 2, 3, 4, 5]
        k.release(m1)
        rkc = k.sb([128, 2], F32, "rkc")
        lng = k.sb([128, 2], F32, "lng")
        lnb = k.sb([128, 2], F32, "lnb")
        gne = k.sb([128, 1], F32, "gne")
        k.dma(rkc[:, :], I["rwkv_r_k"][li].rearrange("(a b) c -> (b c) a", b=2))
        k.dma(lng[:, :], I["rwkv_ln_g"][li].rearrange("(h p) -> p h", p=128))
        k.dma(lnb[:, :], I["rwkv_ln_b"][li].rearrange("(h p) -> p h", p=128))
        k.memset("dve", gne[:, :], 64e-5)
        nm2 = ["yf", "yr", "r_", "k_", "v_", "g_", "cen", "sq", "rs", "rk"]
        P2 = [{nm: k.sb([128, 512], F32, nm) for nm in nm2} for _ in range(2)]
        ob = [k.sb([128, 512], BF16, "ob") for _ in range(2)]
        it = 0
        for (t0, n) in CH:
            for hp in range(2):
                w = P2[it % 2]
                o_ = ob[it % 2]
                it += 1
                hs = slice(hp * 128, (hp + 1) * 128)
                k.dma(w["yf"][:, 0:n], ydir[0, hs, t0:t0 + n])
                k.dma(w["yr"][:, 0:n], ydir[1, hs, t0:t0 + n])
                k.dma(w["r_"][:, 0:n], zsT[hp * 128:(hp + 1) * 128, t0:t0 + n])
                k.dma(w["k_"][:, 0:n], zsT[256 + hp * 128:256 + (hp + 1) * 128, t0:t0 + n])
                k.dma(w["v_"][:, 0:n], zsT[512 + hp * 128:512 + (hp + 1) * 128, t0:t0 + n])
                k.dma(w["g_"][:, 0:n], rkvg[3, hs, t0:t0 + n])
                k.tt("dve", w["yf"][:, 0:n], w["yf"][:, 0:n], w["yr"][:, 0:n], ALU.add)
                pm = k.bank()
                k.mm(pm[:, 0:n], bones[:, :], w["yf"][:, 0:n])
                k.stt("dve", w["cen"][:, 0:n], pm[:, 0:n], -1.0 / 64, w["yf"][:, 0:n], ALU.mult, ALU.add)
                k.act(w["sq"][:, 0:n], w["cen"][:, 0:n], AF.Square)
                pv = k.bank()
                k.mm(pv[:, 0:n], bones[:, :], w["sq"][:, 0:n])
                k.act(w["rs"][:, 0:n], pv[:, 0:n], AF.Sqrt, bias=gne[:, 0:1], scale=1.0 / 64)
                k.recip(w["rs"][:, 0:n], w["rs"][:, 0:n])
                k.tt("dve", w["cen"][:, 0:n], w["cen"][:, 0:n], w["rs"][:, 0:n], ALU.mult)
                k.act(w["cen"][:, 0:n], w["cen"][:, 0:n], AF.Identity, bias=lnb[:, hp:hp + 1], scale=lng[:, hp:hp + 1])
                k.stt("dve", w["rk"][:, 0:n], w["r_"][:, 0:n], rkc[:, hp:hp + 1], w["k_"][:, 0:n], ALU.mult, ALU.mult)
                pb2 = k.bank()
                k.mm(pb2[:, 0:n], bones[:, :], w["rk"][:, 0:n])
                k.tt("dve", w["rk"][:, 0:n], pb2[:, 0:n], w["v_"][:, 0:n], ALU.mult)
                k.tt("dve", w["cen"][:, 0:n], w["cen"][:, 0:n], w["rk"][:, 0:n], ALU.add)
                k.tt("dve", o_[:, 0:n], w["cen"][:, 0:n], w["g_"][:, 0:n], ALU.mult)
                k.dma(mixT[hs, t0:t0 + n], o_[:, 0:n], q="pool")


    def phase_ML(li):
        B0 = 1952
        Tp = T + 3
        cw = k.sb([32, 8, 3], F32, "cw")
        cb = k.sb([32, 8], F32, "cb")
        for j in range(3):
            k.dma(cw[:, :, j], I["mlstm_conv_w"][li, j].rearrange("(g p) -> p g", p=32))
        k.dma(cb[:, :], I["mlstm_conv_b"][li].rearrange("(g p) -> p g", p=32))
        qkb = k.sb([32, 8, T], BF16, "qkb")
        VL = k.sb([128, 34, 4, 65], BF16, "VL")
        k.memset("pool", VL[:, :, :, 64:65], 1.0)
        Hs = k.sb([128, 34, 256], F32, "Hs")
        acol = k.sb([128, 34, 8], F32, "acol")
        Fcol = k.sb([128, 34, 8], F32, "Fcol")
        Rb = k.sb([128, 8, 9], F32, "Rb")
        nRb = k.sb([128, 8, 9], F32, "nRb")
        lmask = k.sb([128, 2, 4, 512], BF16, "lmask")
        k.dma(lmask[:, :, :, :], I["lmask"], cast=True)
        m1 = k.mark()
        zpb = [k.sb([32, Tp], F32, "zpm") for _ in range(2)]
        accb = [k.sb([32, Tp], F32, "accm") for _ in range(2)]
        for g in range(8):
            zp = zpb[g % 2]
            acc = accb[g % 2]
            k.memset("pool", zp[:, 0:1], 0.0)
            k.memset("pool", zp[:, TL + 1:TL + 2], 0.0)
            k.memset("pool", zp[:, Tp - 1:Tp], 0.0)
            k.dma(zp[:, 1:TL + 1], zT[B0 + 32 * g:B0 + 32 * g + 32, 0:TL])
            k.dma(zp[:, TL + 2:TL + 2 + TC], zT[B0 + 32 * g:B0 + 32 * g + 32, TL:T])
            k.ts("dve", acc[:, 1:Tp - 1], zp[:, 1:Tp - 1], cw[:, g, 1:2], ALU.mult)
            k.stt("dve", acc[:, 1:Tp - 1], zp[:, 0:Tp - 2], cw[:, g, 0:1], acc[:, 1:Tp - 1], ALU.mult, ALU.add)
            k.stt("dve", acc[:, 1:Tp - 1], zp[:, 2:Tp], cw[:, g, 2:3], acc[:, 1:Tp - 1], ALU.mult, ALU.add)
            if g < 4:
                k.act(acc[:, 1:Tp - 1], acc[:, 1:Tp - 1], AF.Silu, bias=cb[:, g:g + 1])
                k.ts("dve", qkb[:, g, 0:TL], acc[:, 1:TL + 1], float(32 ** -0.5), ALU.mult)
                k.ts("dve", qkb[:, g, TL:T], acc[:, TL + 2:TL + 2 + TC], float(32 ** -0.5), ALU.mult)
            else:
                k.act(qkb[:, g, 0:TL], acc[:, 1:TL + 1], AF.Silu, bias=cb[:, g:g + 1])
                k.act(qkb[:, g, TL:T], acc[:, TL + 2:TL + 2 + TC], AF.Silu, bias=cb[:, g:g + 1])
        k.release(m1)
        vT = [k.sb([128, 2, 512], F32, "vT") for _ in range(2)]
        for ci, (t0, n) in enumerate(CH):
            v_ = vT[ci % 2]
            k.dma(v_[:, :, 0:n], zT[B0 + 256:B0 + 512, t0:t0 + n].rearrange("(f p) t -> p f t", p=128))
            for j in range(n // 128):
                pb = k.bank()
                for f in range(2):
                    k.tr(pb[:, f * 128:(f + 1) * 128], v_[:, f, j * 128:(j + 1) * 128], ident[:, :])
                k.copy("act", VL[:, t0 // 128 + j, :, 0:64], pb[:, 0:256].rearrange("p (h c) -> p h c", c=64))
        k.release(m1)
        GI = k.sb([8, T], F32, "GI")
        GF = k.sb([8, T], F32, "GF")
        Fp = k.sb([8, T], F32, "Fp")
        ones8 = k.sb([8, TL], F32, "ones8")
        bi = k.sb([8, 2], F32, "bi")
        dc = k.sb([8, 2], F32, "dc")
        sm = k.sb([8, 16], F32, "sm")
        Rt = k.sb([8, 9], F32, "Rt")
        cm = k.sb([8, 9], F32, "cm")
        pmx = k.sb([8, 8], F32, "pmx")
        smx = k.sb([8, 8], F32, "smx")
        sel8 = k.sb([8, 8, 128], F32, "sel8")
        GB = B0 + 768
        for d in range(2):
            k.dma(GI[4 * d:4 * d + 4, :], zT[GB + 8 * d:GB + 8 * d + 4, :])
            k.dma(GF[4 * d:4 * d + 4, :], zT[GB + 8 * d + 4:GB + 8 * d + 8, :])
        k.dma(bi[:, 0:1], I["mlstm_i_b"][li].rearrange("d (h o) -> (d h) o", o=1))
        k.dma(bi[:, 1:2], I["mlstm_f_b"][li].rearrange("d (h o) -> (d h) o", o=1))
        k.dma(dc[:, :], I["dircols"])
        k.dma(sel8[:, :, :], I["sel8"])
        k.memset("dve", ones8[:, :], 1.0)
        k.ts("dve", bi[:, :], bi[:, :], 1.0 / 15.0, ALU.mult)
        k.act(GI[:, :], GI[:, :], AF.Tanh, bias=bi[:, 0:1], scale=1.0 / 15.0)
        k.ts("dve", GI[:, :], GI[:, :], 15.0, ALU.mult)
        k.act(GF[:, :], GF[:, :], AF.Tanh, bias=bi[:, 1:2], scale=1.0 / 15.0)
        k.act(GF[:, :], GF[:, :], AF.Exp, scale=-15.0)
        k.ts("dve", GF[:, :], GF[:, :], 1.0, ALU.add)
        k.act(GF[:, :], GF[:, :], AF.Ln)
        k.ts("dve", GF[:, :], GF[:, :], -1.0, ALU.mult)
        k.scan(Fp[:, 0:TL], ones8[:, 0:TL], GF[:, 0:TL], 0.0, ALU.mult, ALU.add)
        k.scan(Fp[:, TL:T], ones8[:, 0:TC], GF[:, TL:T], 0.0, ALU.mult, ALU.add)
        k.copy("dve", sm[:, 0:1], Fp[:, TL - 1:TL])
        k.copy("dve", sm[:, 1:2], Fp[:, T - 1:T])
        k.stt("dve", sm[:, 2:3], sm[:, 0:1], dc[:, 1:2], sm[:, 1:2], ALU.mult, ALU.add)
        k.tt("dve", sm[:, 3:4], sm[:, 1:2], dc[:, 1:2], ALU.mult)
        k.ts("dve", Fp[:, :], Fp[:, :], dc[:, 0:1], ALU.mult)
        k.stt("dve", Fp[:, :], GF[:, :], dc[:, 1:2], Fp[:, :], ALU.mult, ALU.add)
        k.ts("dve", Fp[:, 0:TL], Fp[:, 0:TL], sm[:, 2:3], ALU.add)
        k.ts("dve", Fp[:, TL:T], Fp[:, TL:T], sm[:, 3:4], ALU.add)
        k.tt("dve", GI[:, :], GI[:, :], Fp[:, :], ALU.subtract)
        k.reduce(cm[:, 0:8], GI[:, 0:TL].rearrange("p (c i) -> p c i", i=512), ALU.max)
        k.reduce(cm[:, 8:9], GI[:, TL:T].rearrange("p (c i) -> p c i", i=TC), ALU.max)
        k.tt("dve", pmx[:, 0:1], cm[:, 0:1], cm[:, 8:9], ALU.max)
        for c in range(1, 8):
            k.tt("dve", pmx[:, c:c + 1], pmx[:, c - 1:c], cm[:, c:c + 1], ALU.max)
        k.tt("dve", smx[:, 7:8], cm[:, 7:8], cm[:, 8:9], ALU.max)
        for c in range(6, -1, -1):
            k.tt("dve", smx[:, c:c + 1], smx[:, c + 1:c + 2], cm[:, c:c + 1], ALU.max)
        k.tt("dve", smx[:, :], smx[:, :], pmx[:, :], ALU.subtract)
        k.stt("dve", Rt[:, 0:8], smx[:, :], dc[:, 1:2], pmx[:, :], ALU.mult, ALU.add)
        k.copy("dve", Rt[:, 8:9], cm[:, 8:9])
        pa_ = k.bank()
        pf_ = k.bank()
        for j in range(34):
            k.tr(pa_[:, 8 * j:8 * j + 8], GI[0:8, j * 128:(j + 1) * 128], ident[0:8, 0:8])
            k.tr(pf_[:, 8 * j:8 * j + 8], Fp[0:8, j * 128:(j + 1) * 128], ident[0:8, 0:8])
        k.copy("act", acol[:, :, :], pa_[:, 0:272].rearrange("p (j c) -> p j c", c=8))
        k.copy("act", Fcol[:, :, :], pf_[:, 0:272].rearrange("p (j c) -> p j c", c=8))
        pr_ = k.bank()
        for c in range(8):
            k.mm(pr_[:, 9 * c:9 * c + 9], sel8[:, c, :], Rt[:, :])
        k.copy("act", Rb[:, :, :], pr_[:, 0:72].rearrange("p (c i) -> p c i", i=9))
        k.ts("dve", nRb[:, :, :], Rb[:, :, :], -1.0, ALU.mult)
        k.release(m1)
        Et = k.sb([128, 34, 9], F32, "Et")
        Wb = [k.sb([128, 512], BF16, "Wb") for _ in range(4)]
        thr = k.sb([128, 4], F32, "thr")
        den = k.sb([128, 4], F32, "den")
        hd = k.sb([128, 64], F32, "hd")
        import os as _os3
        for d in range(2 if not _os3.environ.get('SKIP_MLMAIN') else 0):
            for h in range(4):
                c = d * 4 + h
                for J in range(34):
                    k.ts("dve", Et[:, J, :], nRb[:, c, :], acol[:, J, c:c + 1], ALU.add)
                k.ts("dve", Et[:, :, :], Et[:, :, :], 0.0, ALU.min)
                k.act(Et[:, :, :], Et[:, :, :], AF.Exp)
                for I_, (t0, n) in enumerate(CH):
                    nq = n // 128
                    tb0 = t0 // 128
                    if t0 >= TL:
                        keys = [(32, 0), (33, 1)]
                    elif d == 0:
                        keys = [(32, None), (33, None)] + [(J, None) for J in range(0, 4 * I_)] + [(4 * I_ + r, r) for r in range(4)]
                    else:
                        keys = [(32, None), (33, None)] + [(J, None) for J in range(4 * I_ + 4, 32)] + [(4 * I_ + r, r) for r in range(4)]
                    def contributes(r, qi):
                        if r is None:
                            return True
                        return (r <= qi) if d == 0 else (r >= qi)
                    lists = {qi: [ix for ix, (J, r) in enumerate(keys) if contributes(r, qi)] for qi in range(nq)}
                    po = k.acc()
                    AHEAD = 4
                    psl = {}

                    def issue_qk(ixx, h=h, n=n, t0=t0, keys=keys, psl=psl):
                        b_ = k.bank()
                        J_ = keys[ixx][0]
                        k.mm(b_[:, 0:n], qkb[0:32, 4 + h, J_ * 128:(J_ + 1) * 128], qkb[0:32, h, t0:t0 + n])
                        psl[ixx] = b_
                    for ixx in range(min(AHEAD, len(keys))):
                        issue_qk(ixx)
                    for ix, (J, r) in enumerate(keys):
                        ps_ = psl.pop(ix)
                        if ix + AHEAD < len(keys):
                            issue_qk(ix + AHEAD)
                        w_ = Wb[ix % 4]
                        if r is None:
                            k.ts("dve", w_[:, 0:n], ps_[:, 0:n], Et[:, J, I_:I_ + 1], ALU.mult)
                        else:
                            k.stt("dve", w_[:, 0:n], ps_[:, 0:n], Et[:, J, I_:I_ + 1], lmask[:, d, r, 0:n], ALU.mult, ALU.mult)
                        for qi in range(nq):
                            if ix in lists[qi]:
                                k.mm(po[:, qi * 65:(qi + 1) * 65], w_[:, qi * 128:(qi + 1) * 128], VL[:, J, h, :],
                                     start=(ix == lists[qi][0]), stop=(ix == lists[qi][-1]))
                    pov = po[:, 0:nq * 65].rearrange("p (q c) -> p q c", c=65)
                    k.act(thr[:, 0:nq], Fcol[:, tb0:tb0 + nq, c], AF.Exp, bias=nRb[:, c, I_:I_ + 1], scale=-1.0)
                    k.act(den[:, 0:nq], pov[:, :, 64], AF.Abs)
                    k.tt("dve", den[:, 0:nq], den[:, 0:nq], thr[:, 0:nq], ALU.max)
                    k.recip(den[:, 0:nq], den[:, 0:nq])
                    for qi in range(nq):
                        if d == 0:
                            k.ts("dve", Hs[:, tb0 + qi, h * 64:(h + 1) * 64], pov[:, qi, 0:64], den[:, qi:qi + 1], ALU.mult)
                        else:
                            k.stt("dve", Hs[:, tb0 + qi, h * 64:(h + 1) * 64], pov[:, qi, 0:64], den[:, qi:qi + 1],
                                  Hs[:, tb0 + qi, h * 64:(h + 1) * 64], ALU.mult, ALU.add)
        ng = k.sb([128, 2], F32, "ng")
        k.dma(ng[:, :], I["mlstm_norm_g"][li].rearrange("(f p) -> p f", p=128))
        ssq = k.sb([128, 4], F32, "ssq")
        junk = k.sb([128, 64], F32, "junk")
        hn = [k.sb([128, 256], F32, "hn") for _ in range(2)]
        oT = [k.sb([128, 2, 512], F32, "oT") for _ in range(2)]
        hT = [k.sb([128, 2, 512], F32, "hT") for _ in range(2)]
        hb = [k.sb([128, 2, 512], BF16, "hb") for _ in range(2)]
        for ci, (t0, n) in enumerate(CH):
            o_ = oT[ci % 2]
            k.dma(o_[:, :, 0:n], zT[B0 + 512:B0 + 768, t0:t0 + n].rearrange("(f p) t -> p f t", p=128))
            k.act(o_[:, :, 0:n], o_[:, :, 0:n], AF.Sigmoid)
            ht = hT[ci % 2]
            for j in range(n // 128):
                tix = t0 // 128 + j
                for h in range(4):
                    k.act(junk[:, :], Hs[:, tix, h * 64:(h + 1) * 64], AF.Square, accum=ssq[:, h:h + 1])
                k.act(ssq[:, :], ssq[:, :], AF.Sqrt, bias=epsc[:, 0:1], scale=1.0 / 64)
                k.recip(ssq[:, :], ssq[:, :])
                hn_ = hn[j % 2]
                k.tt("dve", hn_[:, :].rearrange("p (h c) -> p h c", c=64), Hs[:, tix, :].rearrange("p (h c) -> p h c", c=64),
                     ssq[:, :].unsqueeze(2).to_broadcast([128, 4, 64]), ALU.mult)
                pb = k.bank()
                for f in range(2):
                    k.tr(pb[:, f * 128:(f + 1) * 128], hn_[:, f * 128:(f + 1) * 128], ident[:, :])
                for f in range(2):
                    k.ts("dve", ht[:, f, j * 128:(j + 1) * 128], pb[:, f * 128:(f + 1) * 128], ng[:, f:f + 1], ALU.mult)
            k.tt("pool", hb[ci % 2][:, :, 0:n], ht[:, :, 0:n], o_[:, :, 0:n], ALU.mult)
            k.dma(mixT[768:1024, t0:t0 + n].rearrange("(f p) t -> p f t", p=128), hb[ci % 2][:, :, 0:n], q="pool")

    PH = dict(A=phase_A, M=phase_M, NP=phase_NP, O=phase_O, F=phase_F, MLA=phase_MLA, RW=phase_RW, ML=phase_ML)
    return nc, k, PH, base_mark, (xTa, xTb)


def finish_build(nc, k, PH, base_mark, xs, n_layers=DEPTH, enabled=("RW", "MLA", "ML"), tail=True):
    xTa, xTb = xs
    PH["A"]()
    k.release(base_mark)
    for li in range(n_layers):
        ctx_out = li < DEPTH - 1
        PH["M"](li)
        k.release(base_mark)
        PH["NP"](li, xTa)
        k.release(base_mark)
        if "RW" in enabled:
            PH["RW"](li)
            k.release(base_mark)
        if "MLA" in enabled:
            PH["MLA"](li, ctx_out)
            k.release(base_mark)
        if "ML" in enabled:
            PH["ML"](li)
            k.release(base_mark)
        if tail:
            PH["O"](li, xTa, xTb)
            k.release(base_mark)
            PH["F"](li, xTb, xTa, li == DEPTH - 1)
            k.release(base_mark)
    k.S.barrier()
    k.S.emit()
    return nc


def make_in_maps(inputs, consts, cores):
    maps = []
    for b in cores:
        m = {"x": np.ascontiguousarray(inputs["x"][b]), "ctx": np.ascontiguousarray(inputs["ctx"][b]),
             "c": np.ascontiguousarray(inputs["c"][b]), "c_ctx": np.ascontiguousarray(inputs["c_ctx"])}
        for p in PARAMS:
            m[p] = np.ascontiguousarray(inputs[p])
        m.update(consts)
        maps.append(m)
    return maps


def all_shapes(inputs, consts):
    shp = {"x": (TL, D), "ctx": (TC, D), "c": (D,), "c_ctx": (D,)}
    for p in PARAMS:
        shp[p] = tuple(inputs[p].shape)
    for kk, v in consts.items():
        shp[kk] = tuple(v.shape)
    return shp


def kernel(**inputs):
    inputs = {kk: np.asarray(v, dtype=np.float32) for kk, v in inputs.items()}
    consts = host_consts()
    nc, k, PH, bm, xs = build(all_shapes(inputs, consts))
    finish_build(nc, k, PH, bm, xs)
    cores = list(range(8))
    res = run_bass_kernel_spmd(nc, make_in_maps(inputs, consts, cores), core_ids=cores)
    return np.stack([np.asarray(res.results[b]["out"], dtype=np.float32) for b in cores], axis=0)
```
